# Optimizing a Trainium2 kernel written in Bass

```python
import numpy as np
import jax
import jax.numpy as jnp
from jax import lax

D_MODEL = 1024
BATCH = 8
SEQ = 8192
DEPTH = 2
DEC_BATCH = 8
DEC_SEQ = 32
PAST_LEN = 2048

CHUNK = 64
QBLOCK = 128
HEAD_DIM = 64
FOX_HEADS = 4
SB_HEADS = 4
LRU_WIDTH = 256
LRU_BLOCKS = 4
LRU_CONV = 4
LRU_C = 8.0
MLA_HEADS = 4
MLA_Q_RANK = 256
MLA_KV_RANK = 128
MLA_NOPE = 64
MLA_ROPE = 32
MLA_V = 64
ROPE_BASE = 10000.0
N_BRANCH = 4
BRANCH_WIDTH = 256
MEM_LEN = 256
MEM_HEADS = 4
MEM_HEAD_DIM = 128
D_FF = -(-8 * D_MODEL // (3 * 256)) * 256
EPS = 1e-6
NEG_INF = -1e30

_MIX_SIZES = (FOX_HEADS * HEAD_DIM, FOX_HEADS * HEAD_DIM, FOX_HEADS * HEAD_DIM, FOX_HEADS,
              LRU_WIDTH, LRU_WIDTH,
              MLA_Q_RANK, MLA_KV_RANK, MLA_ROPE,
              SB_HEADS * HEAD_DIM, SB_HEADS * HEAD_DIM, SB_HEADS * HEAD_DIM)
D_MIX_IN = sum(_MIX_SIZES)
SPLIT_IDX = tuple(int(i) for i in np.cumsum(_MIX_SIZES)[:-1])
D_IN = D_MIX_IN + N_BRANCH * D_MODEL

kernel_name = 'hybrid_streaming_encoder_step'


def rms_norm(x, g):
    xf = x.astype(jnp.float32)
    y = xf * lax.rsqrt(jnp.mean(xf * xf, axis=-1, keepdims=True) + EPS)
    return (y * g.astype(jnp.float32)).astype(x.dtype)


def sweep_queries(fn, *q_args):
    t = q_args[0].shape[1]
    blk = min(QBLOCK, t)
    n = t // blk
    blocks = tuple(jnp.swapaxes(a.reshape(a.shape[0], n, blk, *a.shape[2:]), 0, 1) for a in q_args)
    out = lax.map(lambda args: fn(*args), blocks)
    out = jnp.swapaxes(out, 0, 1)
    return out.reshape(out.shape[0], t, *out.shape[3:])


def apply_rope(x, pos):
    half = x.shape[-1] // 2
    inv = jnp.power(ROPE_BASE, -jnp.arange(half, dtype=jnp.float32) / half)
    ang = pos.astype(jnp.float32)[:, None, None] * inv
    cos, sin = jnp.cos(ang), jnp.sin(ang)
    xf = x.astype(jnp.float32)
    x1, x2 = xf[..., :half], xf[..., half:]
    return jnp.concatenate([x1 * cos - x2 * sin, x1 * sin + x2 * cos], axis=-1).astype(x.dtype)


def forgetting_attention(q, k, v, logf, pos_q, pos_k):
    t = q.shape[1]
    c = jnp.cumsum(logf.astype(jnp.float32), axis=1)
    c_k = jnp.swapaxes(c, 1, 2)[:, :, None, :]
    c_q = c[:, -t:]
    vf = v.astype(jnp.float32)
    scale = q.shape[-1] ** -0.5

    def block(qb, cqb, qpb):
        s = jnp.einsum('bqhd,bkhd->bhqk', qb, k, preferred_element_type=jnp.float32) * scale
        s = s + jnp.swapaxes(cqb, 1, 2)[..., None] - c_k
        mask = pos_k[None, :] <= qpb[0][:, None]
        p = jax.nn.softmax(jnp.where(mask, s, NEG_INF), axis=-1)
        return jnp.einsum('bhqk,bkhd->bqhd', p, vf).astype(v.dtype)

    return sweep_queries(block, q, c_q, pos_q[None])


def stick_breaking_attention(q, k, v, pos_q, pos_k):
    vf = v.astype(jnp.float32)
    scale = q.shape[-1] ** -0.5

    def block(qb, qpb):
        z = jnp.einsum('bqhd,bkhd->bhqk', qb, k, preferred_element_type=jnp.float32) * scale
        mask = pos_k[None, :] < qpb[0][:, None]
        log_keep = jnp.where(mask, jax.nn.log_sigmoid(-z), 0.0)
        later = lax.cumsum(log_keep, axis=3, reverse=True) - log_keep
        a = jnp.where(mask, jnp.exp(jax.nn.log_sigmoid(z) + later), 0.0)
        return jnp.einsum('bhqk,bkhd->bqhd', a, vf).astype(v.dtype)

    return sweep_queries(block, q, pos_q[None])


def latent_attention(cq, ckv_new, kpe_new, ckv_past, kpe_past, pos_q, pos_k,
                     q_norm, w_uq, kv_norm, w_uk, w_uv):
    b, t, _ = cq.shape
    q = (rms_norm(cq, q_norm) @ w_uq).reshape(b, t, MLA_HEADS, MLA_NOPE + MLA_ROPE)
    q = jnp.concatenate([q[..., :MLA_NOPE], apply_rope(q[..., MLA_NOPE:], pos_q)], axis=-1)
    ckv_n = rms_norm(ckv_new, kv_norm)
    kpe_r = apply_rope(kpe_new[:, :, None, :], pos_q)[:, :, 0, :]
    ckv = jnp.concatenate([ckv_past.astype(ckv_n.dtype), ckv_n], axis=1)
    kpe = jnp.concatenate([kpe_past.astype(kpe_r.dtype), kpe_r], axis=1)
    s_len = ckv.shape[1]
    k_nope = (ckv @ w_uk).reshape(b, s_len, MLA_HEADS, MLA_NOPE)
    vf = (ckv @ w_uv).reshape(b, s_len, MLA_HEADS, MLA_V).astype(jnp.float32)
    k = jnp.concatenate([k_nope, jnp.broadcast_to(kpe[:, :, None, :], (b, s_len, MLA_HEADS, MLA_ROPE)).astype(k_nope.dtype)], axis=-1)
    scale = (MLA_NOPE + MLA_ROPE) ** -0.5
    k_chunk = pos_k // CHUNK

    def block(qb, qcb):
        s = jnp.einsum('bqhd,bkhd->bhqk', qb, k.astype(qb.dtype), preferred_element_type=jnp.float32) * scale
        mask = k_chunk[None, :] <= qcb[0][:, None]
        p = jax.nn.softmax(jnp.where(mask, s, NEG_INF), axis=-1)
        return jnp.einsum('bhqk,bkhd->bqhd', p, vf).astype(cq.dtype)

    out = sweep_queries(block, q, (pos_q // CHUNK)[None])
    return out, ckv_n, kpe_r


def lru_combine(left, right):
    a_l, b_l = left
    a_r, b_r = right
    return a_l * a_r, a_r * b_l + b_r


def rg_lru_branch(xb, gb, conv_buf, h0, pos_q, conv_w, conv_b, wr, br, wi, bi, lam):
    b, t, width = xb.shape
    f32 = jnp.float32
    xpad = jnp.concatenate([conv_buf.astype(xb.dtype), xb], axis=1)
    xc = conv_b.astype(f32)
    for tap in range(LRU_CONV):
        xc = xc + xpad[:, tap:tap + t].astype(f32) * conv_w[tap].astype(f32)
    new_buf = xpad[:, -(LRU_CONV - 1):]
    xh = xc.reshape(b, t, LRU_BLOCKS, width // LRU_BLOCKS)
    r = jax.nn.sigmoid(jnp.einsum('bthi,hij->bthj', xh, wr.astype(f32)) + br.astype(f32)).reshape(b, t, width)
    i = jax.nn.sigmoid(jnp.einsum('bthi,hij->bthj', xh, wi.astype(f32)) + bi.astype(f32)).reshape(b, t, width)
    log_a = -LRU_C * r * jax.nn.softplus(-lam.astype(f32))
    reset = (pos_q == 0)[None, :, None]
    a = jnp.where(reset, 0.0, jnp.exp(log_a))
    mult = jnp.where(reset, 1.0, jnp.sqrt(-jnp.expm1(2.0 * log_a)))
    u = mult * i * xc
    u = u.at[:, 0].add(a[:, 0] * h0.astype(f32))
    _, hs = lax.associative_scan(lru_combine, (a, u), axis=1)
    y = (hs * jax.nn.gelu(gb.astype(f32))).astype(xb.dtype)
    return y, new_buf, hs[:, -1]


def memory_kv(mem, g, wk, wv):
    b, m, _ = mem.shape
    mn = rms_norm(mem, g)
    return ((mn @ wk).reshape(b, m, MEM_HEADS, MEM_HEAD_DIM),
            (mn @ wv).reshape(b, m, MEM_HEADS, MEM_HEAD_DIM))


def memory_attention(h, mk, mv, wq, wo):
    b, t, _ = h.shape
    q = (h @ wq).reshape(b, t, MEM_HEADS, MEM_HEAD_DIM)
    s = jnp.einsum('bqhd,bkhd->bhqk', q, mk.astype(q.dtype), preferred_element_type=jnp.float32) * MEM_HEAD_DIM ** -0.5
    p = jax.nn.softmax(s, axis=-1)
    o = jnp.einsum('bhqk,bkhd->bqhd', p, mv.astype(jnp.float32)).astype(h.dtype)
    return o.reshape(b, t, MEM_HEADS * MEM_HEAD_DIM) @ wo


def swiglu(h, wg, wu, wd):
    return (jax.nn.silu(h @ wg) * (h @ wu)) @ wd


def empty_past(b, dtype):
    return dict(
        fox_k=jnp.zeros((b, 0, FOX_HEADS, HEAD_DIM), dtype),
        fox_v=jnp.zeros((b, 0, FOX_HEADS, HEAD_DIM), dtype),
        fox_logf=jnp.zeros((b, 0, FOX_HEADS), jnp.float32),
        lru_h=jnp.zeros((b, LRU_WIDTH), jnp.float32),
        lru_conv=jnp.zeros((b, LRU_CONV - 1, LRU_WIDTH), dtype),
        mla_ckv=jnp.zeros((b, 0, MLA_KV_RANK), dtype),
        mla_kpe=jnp.zeros((b, 0, MLA_ROPE), dtype),
        sb_k=jnp.zeros((b, 0, SB_HEADS, HEAD_DIM), dtype),
        sb_v=jnp.zeros((b, 0, SB_HEADS, HEAD_DIM), dtype))


def trunk_layer(x, past, mem_k, mem_v, w):
    bsz, t, _ = x.shape
    past_len = past['fox_k'].shape[1]
    pos_q = past_len + jnp.arange(t, dtype=jnp.int32)
    pos_k = jnp.arange(past_len + t, dtype=jnp.int32)

    def cat(a, bnew):
        return jnp.concatenate([a.astype(bnew.dtype), bnew], axis=1)

    def heads(a):
        return a.reshape(bsz, t, -1, HEAD_DIM)

    h = rms_norm(x, w['ln_mix_pre'])
    w_mix, w_gate = w['w_in'][:, :D_MIX_IN], w['w_in'][:, D_MIX_IN:]
    fq, fk, fv, ff, lx, lg, cq, ckv, kpe, sq, sk, sv = jnp.split(h @ w_mix, SPLIT_IDX, axis=-1)

    fk, fv = heads(fk), heads(fv)
    logf = jax.nn.log_sigmoid(ff.astype(jnp.float32) + w['fox_bf'].astype(jnp.float32))
    o_a = forgetting_attention(heads(fq), cat(past['fox_k'], fk), cat(past['fox_v'], fv),
                               cat(past['fox_logf'], logf), pos_q, pos_k)
    o_b, lru_conv, lru_h = rg_lru_branch(lx, lg, past['lru_conv'], past['lru_h'], pos_q,
                                         w['lru_conv_w'], w['lru_conv_b'], w['lru_wr'], w['lru_br'],
                                         w['lru_wi'], w['lru_bi'], w['lru_lam'])
    o_c, ckv_n, kpe_r = latent_attention(cq, ckv, kpe, past['mla_ckv'], past['mla_kpe'], pos_q, pos_k,
                                         w['mla_q_norm'], w['mla_w_uq'], w['mla_kv_norm'],
                                         w['mla_w_uk'], w['mla_w_uv'])
    sk, sv = heads(sk), heads(sv)
    o_d = stick_breaking_attention(heads(sq), cat(past['sb_k'], sk), cat(past['sb_v'], sv), pos_q, pos_k)

    terms = []
    for n, o in enumerate((o_a, o_b, o_c, o_d)):
        gate = jax.nn.sigmoid(h @ w_gate[:, n * D_MODEL:(n + 1) * D_MODEL])
        terms.append(gate * (o.reshape(bsz, t, BRANCH_WIDTH) @ w['w_branch'][n]))
    merged = terms[0] + terms[1] + terms[2] + terms[3]
    x = x + rms_norm(merged @ w['w_out'], w['ln_mix_post'])

    m = memory_attention(rms_norm(x, w['ln_mem_pre']), mem_k, mem_v, w['mem_wq'], w['mem_wo'])
    x = x + rms_norm(m, w['ln_mem_post'])

    f = swiglu(rms_norm(x, w['ln_ffn_pre']), w['ffn_wg'], w['ffn_wu'], w['ffn_wd'])
    x = x + rms_norm(f, w['ln_ffn_post'])

    new = dict(fox_k=fk, fox_v=fv, fox_logf=logf, lru_h=lru_h, lru_conv=lru_conv,
               mla_ckv=ckv_n, mla_kpe=kpe_r, sb_k=sk, sb_v=sv)
    return x, new


def setup_inputs(seed: int = 0) -> dict:
    key = jax.random.key(seed)
    keys = iter(jax.random.split(key, 64))

    def nrm(shape, scale=1.0):
        return scale * jax.random.normal(next(keys), shape, jnp.float32)

    def gain(shape):
        return 1.0 + 0.05 * nrm(shape)

    L, D = DEPTH, D_MODEL
    gb = LRU_WIDTH // LRU_BLOCKS
    a_lru = jax.random.uniform(next(keys), (L, LRU_WIDTH), jnp.float32, 0.9, 0.999) ** (1.0 / LRU_C)
    return {
        'x_prompt': nrm((BATCH, SEQ, D)),
        'x_sample': nrm((DEC_BATCH, DEC_SEQ, D)),
        'cache_fox_k': nrm((L, DEC_BATCH, PAST_LEN, FOX_HEADS, HEAD_DIM)),
        'cache_fox_v': nrm((L, DEC_BATCH, PAST_LEN, FOX_HEADS, HEAD_DIM)),
        'cache_fox_logf': jax.nn.log_sigmoid(2.0 + nrm((L, DEC_BATCH, PAST_LEN, FOX_HEADS), 0.5)),
        'state_lru_h': nrm((L, DEC_BATCH, LRU_WIDTH), 0.5),
        'state_lru_conv': nrm((L, DEC_BATCH, LRU_CONV - 1, LRU_WIDTH)),
        'cache_mla_ckv': nrm((L, DEC_BATCH, PAST_LEN, MLA_KV_RANK)),
        'cache_mla_kpe': nrm((L, DEC_BATCH, PAST_LEN, MLA_ROPE)),
        'cache_sb_k': nrm((L, DEC_BATCH, PAST_LEN, SB_HEADS, HEAD_DIM)),
        'cache_sb_v': nrm((L, DEC_BATCH, PAST_LEN, SB_HEADS, HEAD_DIM)),
        'cache_mem_k': nrm((L, DEC_BATCH, MEM_LEN, MEM_HEADS, MEM_HEAD_DIM)),
        'cache_mem_v': nrm((L, DEC_BATCH, MEM_LEN, MEM_HEADS, MEM_HEAD_DIM)),
        'mem_prompt': nrm((BATCH, MEM_LEN, D)),
        'ln_mix_pre': gain((L, D)),
        'ln_mix_post': gain((L, D)),
        'w_in': nrm((L, D, D_IN), D ** -0.5),
        'fox_bf': 2.0 + nrm((L, FOX_HEADS), 0.5),
        'lru_conv_w': nrm((L, LRU_CONV, LRU_WIDTH), LRU_CONV ** -0.5),
        'lru_conv_b': nrm((L, LRU_WIDTH), 0.02),
        'lru_wr': nrm((L, LRU_BLOCKS, gb, gb), gb ** -0.5),
        'lru_br': nrm((L, LRU_BLOCKS, gb), 0.02),
        'lru_wi': nrm((L, LRU_BLOCKS, gb, gb), gb ** -0.5),
        'lru_bi': nrm((L, LRU_BLOCKS, gb), 0.02),
        'lru_lam': jnp.log(a_lru) - jnp.log1p(-a_lru),
        'mla_q_norm': gain((L, MLA_Q_RANK)),
        'mla_w_uq': nrm((L, MLA_Q_RANK, MLA_HEADS * (MLA_NOPE + MLA_ROPE)), MLA_Q_RANK ** -0.5),
        'mla_kv_norm': gain((L, MLA_KV_RANK)),
        'mla_w_uk': nrm((L, MLA_KV_RANK, MLA_HEADS * MLA_NOPE), MLA_KV_RANK ** -0.5),
        'mla_w_uv': nrm((L, MLA_KV_RANK, MLA_HEADS * MLA_V), MLA_KV_RANK ** -0.5),
        'w_branch': nrm((L, N_BRANCH, BRANCH_WIDTH, D), BRANCH_WIDTH ** -0.5),
        'w_out': nrm((L, D, D), D ** -0.5),
        'ln_mem_pre': gain((L, D)),
        'ln_mem_post': gain((L, D)),
        'mem_norm': gain((L, D)),
        'mem_wq': nrm((L, D, MEM_HEADS * MEM_HEAD_DIM), D ** -0.5),
        'mem_wk': nrm((L, D, MEM_HEADS * MEM_HEAD_DIM), D ** -0.5),
        'mem_wv': nrm((L, D, MEM_HEADS * MEM_HEAD_DIM), D ** -0.5),
        'mem_wo': nrm((L, MEM_HEADS * MEM_HEAD_DIM, D), (MEM_HEADS * MEM_HEAD_DIM) ** -0.5),
        'ln_ffn_pre': gain((L, D)),
        'ln_ffn_post': gain((L, D)),
        'ffn_wg': nrm((L, D, D_FF), D ** -0.5),
        'ffn_wu': nrm((L, D, D_FF), D ** -0.5),
        'ffn_wd': nrm((L, D_FF, D), D_FF ** -0.5),
    }


def reference(x_prompt, x_sample, cache_fox_k, cache_fox_v, cache_fox_logf, state_lru_h, state_lru_conv,
              cache_mla_ckv, cache_mla_kpe, cache_sb_k, cache_sb_v, cache_mem_k, cache_mem_v, mem_prompt,
              ln_mix_pre, ln_mix_post, w_in, fox_bf, lru_conv_w, lru_conv_b, lru_wr, lru_br, lru_wi, lru_bi,
              lru_lam, mla_q_norm, mla_w_uq, mla_kv_norm, mla_w_uk, mla_w_uv, w_branch, w_out,
              ln_mem_pre, ln_mem_post, mem_norm, mem_wq, mem_wk, mem_wv, mem_wo,
              ln_ffn_pre, ln_ffn_post, ffn_wg, ffn_wu, ffn_wd):
    weights = dict(ln_mix_pre=ln_mix_pre, ln_mix_post=ln_mix_post, w_in=w_in, fox_bf=fox_bf,
                   lru_conv_w=lru_conv_w, lru_conv_b=lru_conv_b, lru_wr=lru_wr, lru_br=lru_br,
                   lru_wi=lru_wi, lru_bi=lru_bi, lru_lam=lru_lam, mla_q_norm=mla_q_norm,
                   mla_w_uq=mla_w_uq, mla_kv_norm=mla_kv_norm, mla_w_uk=mla_w_uk, mla_w_uv=mla_w_uv,
                   w_branch=w_branch, w_out=w_out, ln_mem_pre=ln_mem_pre, ln_mem_post=ln_mem_post,
                   mem_wq=mem_wq, mem_wo=mem_wo, ln_ffn_pre=ln_ffn_pre, ln_ffn_post=ln_ffn_post,
                   ffn_wg=ffn_wg, ffn_wu=ffn_wu, ffn_wd=ffn_wd)

    def layer_weights(l):
        return {name: arr[l] for name, arr in weights.items()}

    y_prompt = x_prompt
    p_states = []
    for l in range(DEPTH):
        mk, mv = memory_kv(mem_prompt, mem_norm[l], mem_wk[l], mem_wv[l])
        y_prompt, st = trunk_layer(y_prompt, empty_past(x_prompt.shape[0], x_prompt.dtype), mk, mv, layer_weights(l))
        st['mem_k'] = mk
        st['mem_v'] = mv
        p_states.append(st)

    y_sample = x_sample
    s_states = []
    for l in range(DEPTH):
        past = dict(fox_k=cache_fox_k[l], fox_v=cache_fox_v[l], fox_logf=cache_fox_logf[l],
                    lru_h=state_lru_h[l], lru_conv=state_lru_conv[l],
                    mla_ckv=cache_mla_ckv[l], mla_kpe=cache_mla_kpe[l],
                    sb_k=cache_sb_k[l], sb_v=cache_sb_v[l])
        y_sample, st = trunk_layer(y_sample, past, cache_mem_k[l], cache_mem_v[l], layer_weights(l))
        s_states.append(st)

    def stk(states, name):
        return jnp.stack([s[name] for s in states])

    return (y_prompt, y_sample,
            stk(p_states, 'fox_k'), stk(p_states, 'fox_v'), stk(p_states, 'fox_logf'),
            stk(p_states, 'lru_h'), stk(p_states, 'lru_conv'),
            stk(p_states, 'mla_ckv'), stk(p_states, 'mla_kpe'),
            stk(p_states, 'sb_k'), stk(p_states, 'sb_v'),
            stk(p_states, 'mem_k'), stk(p_states, 'mem_v'),
            stk(s_states, 'fox_k'), stk(s_states, 'fox_v'), stk(s_states, 'fox_logf'),
            stk(s_states, 'lru_h'), stk(s_states, 'lru_conv'),
            stk(s_states, 'mla_ckv'), stk(s_states, 'mla_kpe'),
            stk(s_states, 'sb_k'), stk(s_states, 'sb_v'))
```

```python
import math
import os
import traceback
from collections import deque
import numpy as np
import concourse.bass as bass
import concourse.mybir as mybir
from concourse.bass_utils import run_bass_kernel_spmd

F32 = mybir.dt.float32
BF16 = mybir.dt.bfloat16
I32 = mybir.dt.int32
AF = mybir.ActivationFunctionType
ALU = mybir.AluOpType

D = 1024
L = 2
DFF = 2816
DIN = 6564
PAST = 2048
TS_ = 32
MEM = 256
EPS = 1e-6
NEG = -30000.0
NW = 5


DEBUG = bool(os.environ.get('MK_DEBUG'))


def _site():
    return [f"{f.name}:{f.lineno}" for f in traceback.extract_stack(limit=6)[:-2]]


class Sched:
    ENG = ('pe', 'act', 'dve', 'pool', 'sp')

    def __init__(self, nc):
        self.nc = nc
        self.q = {e: [] for e in self.ENG}
        self.sem = {}
        self.cnt = {}
        for e in ('pe', 'act', 'dve', 'pool'):
            self.sem[e] = nc.alloc_semaphore(name=f"c_{e}")
            self.cnt[e] = 0
        self.seen = {e: {} for e in self.ENG}
        self.res = {}
        self.nops = 0

    def chan(self, name):
        if name not in self.sem:
            self.sem[name] = self.nc.alloc_semaphore(name=f"d_{name}")
            self.cnt[name] = 0
        return name

    def _need(self, eng, toks, waits):
        for t in toks:
            if t is None:
                continue
            s, v = t
            if eng == 'pe' and s == 'pe':
                continue
            if self.seen[eng].get(s, 0) >= v:
                continue
            if waits.get(s, 0) < v:
                waits[s] = v

    def _deps(self, eng, reads, writes):
        waits = {}
        for k in reads:
            r = self.res.get(k)
            if r:
                self._need(eng, [r[0]], waits)
        for k in writes:
            r = self.res.get(k)
            if r:
                self._need(eng, [r[0]] + r[1], waits)
        for s, v in waits.items():
            self.seen[eng][s] = v
        return [(self.sem[s], v) for s, v in waits.items()]

    def _mark(self, tok, reads, writes):
        for k in reads:
            r = self.res.setdefault(k, [None, []])
            r[1].append(tok)
            if len(r[1]) > 16:
                m = {}
                for s, v in r[1]:
                    if m.get(s, 0) < v:
                        m[s] = v
                r[1] = list(m.items())
        for k in writes:
            self.res[k] = [tok, []]

    def op(self, eng, fn, reads=(), writes=()):
        pr = [k for k in reads if k[:2] in ('ps', 'pT')]
        if pr and eng != 'pe':
            writes = list(writes) + pr
            reads = [k for k in reads if k not in pr]
        waits = self._deps(eng, reads, writes)
        self.cnt[eng] += 1
        tok = (eng, self.cnt[eng])
        sem = self.sem[eng]

        site = _site() if DEBUG else None

        def emit(h, waits=waits, fn=fn, sem=sem, site=site):
            for s, v in waits:
                h.wait_ge(s, v)
            try:
                fn(h).then_inc(sem, 1)
            except Exception:
                print('FAILED OP SITE:', site)
                raise
        self.q[eng].append(emit)
        self._mark(tok, reads, writes)
        self.nops += 1
        return tok

    def dma(self, queue, chan, pairs, reads=(), writes=(), **kw):
        self.chan(chan)
        waits = self._deps(queue, reads, writes)
        prev = self.cnt[chan]
        w2 = {}
        if prev > 0:
            self._need(queue, [(chan, prev)], w2)
        for s, v in w2.items():
            self.seen[queue][s] = v
        waits = waits + [(self.sem[s], v) for s, v in w2.items()]
        self.cnt[chan] += 16 * len(pairs)
        tok = (chan, self.cnt[chan])
        sem = self.sem[chan]

        site = _site() if DEBUG else None

        def emit(h, waits=waits, pairs=pairs, sem=sem, kw=kw, site=site):
            for s, v in waits:
                h.wait_ge(s, v)
            try:
                for o, i in pairs:
                    h.dma_start(out=o, in_=i, **kw).then_inc(sem, 16)
            except Exception:
                print('FAILED DMA SITE:', site)
                raise
        self.q[queue].append(emit)
        self._mark(tok, reads, writes)
        self.nops += len(pairs)
        return tok

    def barrier(self):
        snap = {s: v for s, v in self.cnt.items() if v > 0}
        for e in self.ENG:
            waits = []
            for s, v in snap.items():
                if self.seen[e].get(s, 0) < v:
                    waits.append((self.sem[s], v))
                    self.seen[e][s] = v

            def emit(h, waits=waits):
                for s, v in waits:
                    h.wait_ge(s, v)
            self.q[e].append(emit)
        self.res = {}

    def finish(self):
        self.barrier()
        nc = self.nc
        q = self.q
        with nc.Block() as block:
            @block.tensor
            def _(h):
                for f in q['pe']:
                    f(h)

            @block.scalar
            def _(h):
                for f in q['act']:
                    f(h)

            @block.vector
            def _(h):
                for f in q['dve']:
                    f(h)

            @block.gpsimd
            def _(h):
                for f in q['pool']:
                    f(h)

            @block.sync
            def _(h):
                for f in q['sp']:
                    f(h)


class Tl:
    __slots__ = ('ap', 'k')

    def __init__(self, ap, k):
        self.ap = ap
        self.k = k

    def __getitem__(self, idx):
        return Tl(self.ap[idx], self.k)

    def re(self, pat_, **kw):
        return Tl(self.ap.rearrange(pat_, **kw), self.k)

    def bc(self, shape):
        return Tl(self.ap.broadcast_to(shape), self.k)

    def us(self, ax):
        return Tl(self.ap.unsqueeze(ax), self.k)


def _ap(x):
    return x.ap if isinstance(x, Tl) else x


def _keys(*xs):
    return [x.k for x in xs if isinstance(x, Tl)]


class TilePool:
    def __init__(self, tiles):
        self.free = deque(tiles)

    def get(self):
        return self.free.popleft()

    def put(self, *ts):
        for t in ts:
            self.free.append(t)


class Seq:
    pass


class StopBuild(Exception):
    pass


class Builder:
    def __init__(self, T):
        self.T = T
        self.nc = nc = bass.Bass("TRN2", target_bir_lowering=False)
        self.S = Sched(nc)
        self._rr = 0
        self.declare_dram()
        self.alloc_sbuf()

    def din(self, name, shape):
        return Tl(self.nc.dram_tensor(name, list(shape), F32, kind="ExternalInput").ap(), 'in_' + name)

    def dout(self, name, shape):
        return Tl(self.nc.dram_tensor(name, list(shape), F32, kind="ExternalOutput").ap(), 'out_' + name)

    def dscr(self, name, shape, dt):
        kind = "ExternalOutput" if (os.environ.get('MK_DBGOUT') and not name.startswith('b_')) else "Internal"
        return Tl(self.nc.dram_tensor(name, list(shape), dt, kind=kind).ap(), 'scr_' + name)

    def declare_dram(self):
        T = self.T
        I = self.I = {}
        O = self.O = {}
        I['x_prompt'] = self.din('x_prompt', [T, D])
        I['x_sample'] = self.din('x_sample', [TS_, D])
        for n in ('cache_fox_k', 'cache_fox_v', 'cache_sb_k', 'cache_sb_v'):
            I[n] = self.din(n, [L, PAST, 256])
        I['cache_fox_logf'] = self.din('cache_fox_logf', [L, PAST, 4])
        I['state_lru_h'] = self.din('state_lru_h', [L, 256])
        I['state_lru_conv'] = self.din('state_lru_conv', [L, 3, 256])
        I['cache_mla_ckv'] = self.din('cache_mla_ckv', [L, PAST, 128])
        I['cache_mla_kpe'] = self.din('cache_mla_kpe', [L, PAST, 32])
        I['cache_mem_k'] = self.din('cache_mem_k', [L, MEM, 512])
        I['cache_mem_v'] = self.din('cache_mem_v', [L, MEM, 512])
        I['mem_prompt'] = self.din('mem_prompt', [MEM, D])
        wshapes = dict(
            ln_mix_pre=[L, D], ln_mix_post=[L, D], w_in=[L, D, DIN], fox_bf=[L, 4], lru_conv_w=[L, 4, 256],
            lru_conv_b=[L, 256], lru_wr=[L, 256, 64], lru_br=[L, 256], lru_wi=[L, 256, 64], lru_bi=[L, 256],
            lru_lam=[L, 256], mla_q_norm=[L, 256], mla_w_uq=[L, 256, 384], mla_kv_norm=[L, 128],
            mla_w_uk=[L, 128, 256], mla_w_uv=[L, 128, 256], w_branch=[L, 1024, 1024], w_out=[L, D, D],
            ln_mem_pre=[L, D], ln_mem_post=[L, D], mem_norm=[L, D], mem_wq=[L, D, 512], mem_wk=[L, D, 512],
            mem_wv=[L, D, 512], mem_wo=[L, 512, D], ln_ffn_pre=[L, D], ln_ffn_post=[L, D],
            ffn_wg=[L, D, DFF], ffn_wu=[L, D, DFF], ffn_wd=[L, DFF, D])
        self.wshapes = wshapes
        for n, s in wshapes.items():
            I[n] = self.din(n, s)
        for pre, t in (('p', T), ('s', TS_)):
            O[pre + '_y'] = self.dout(pre + '_y', [t, D])
            for n in ('fox_k', 'fox_v', 'sb_k', 'sb_v'):
                O[f'{pre}_{n}'] = self.dout(f'{pre}_{n}', [L, t, 256])
            O[pre + '_fox_logf'] = self.dout(pre + '_fox_logf', [L, t, 4])
            O[pre + '_lru_h'] = self.dout(pre + '_lru_h', [L, 256])
            O[pre + '_lru_conv'] = self.dout(pre + '_lru_conv', [L, 3, 256])
            O[pre + '_mla_ckv'] = self.dout(pre + '_mla_ckv', [L, t, 128])
            O[pre + '_mla_kpe'] = self.dout(pre + '_mla_kpe', [L, t, 32])
        O['p_mem_k'] = self.dout('p_mem_k', [L, MEM, 512])
        O['p_mem_v'] = self.dout('p_mem_v', [L, MEM, 512])
        self.big = ['w_in', 'w_branch', 'w_out', 'mem_wq', 'mem_wk', 'mem_wv', 'mem_wo', 'ffn_wg', 'ffn_wu', 'ffn_wd']
        self.Wb = {n: self.dscr('b_' + n, wshapes[n], BF16) for n in self.big}
        self.seqs = []
        for name, t, p in (('p', T, 0), ('s', TS_, PAST)):
            sq = Seq()
            sq.name = name
            sq.T = t
            sq.P = p
            sq.NT = min(512, t)
            sq.TB = min(128, t)
            sq.nb = sq.NT // sq.TB
            sq.ntiles = t // sq.NT
            sq.Ttot = p + t
            sq.nblk = (sq.Ttot + 127) // 128
            sq.x_in = I['x_prompt'] if name == 'p' else I['x_sample']
            sq.x_mid = self.dscr(name + '_xmid', [t, D], F32)
            sq.y = O[name + '_y']
            sq.HT = self.dscr(name + '_HT', [sq.ntiles, 128, 8, sq.NT], BF16)
            sq.Qs = [[self.dscr(f'{name}_Q{ty}{h}', [(65, 96, 64)[ty], t], BF16) for h in range(4)] for ty in range(3)]
            sq.KTs = [[self.dscr(f'{name}_K{ty}{h}', [(65, 96, 64)[ty], sq.Ttot], BF16) for h in range(4)] for ty in range(3)]
            sq.Vs = [[self.dscr(f'{name}_V{ty}{h}', [128, sq.nblk, 68], BF16) for h in range(4)] for ty in range(3)]
            sq.Os = [[self.dscr(f'{name}_O{ty}{h}', [64, t], BF16) for h in range(4)] for ty in range(3)]
            sq.OB = self.dscr(name + '_OB', [128, 2, t], BF16)
            self.seqs.append(sq)

    def sb(self, name, shape, dt):
        return Tl(self.nc.alloc_sbuf_tensor(name, list(shape), dt).ap(), name)

    def alloc_sbuf(self):
        nc = self.nc
        self.wslots = [self.sb(f'wslot{i}', [128, 4096], BF16) for i in range(NW)]
        NA, NB = 16, 24
        self.A = TilePool([self.sb(f'A{i}', [128, 512], F32) for i in range(NA)])
        self.Bp = TilePool([self.sb(f'B{i}', [128, 512], BF16) for i in range(NB)])
        self.gpost = [self.sb(f'gpost{i}', [128, D], F32) for i in range(3)]
        self.ident_bf = self.sb('ident_bf', [128, 128], BF16)
        self.ident_f = self.sb('ident_f', [128, 128], F32)
        self.E0 = self.sb('E0', [128, 128], F32)
        self.U = self.sb('U', [128, 128], BF16)
        self.ones_bf = self.sb('ones_bf', [128, 128], BF16)
        self.ones_f = self.sb('ones_f', [128, 512], F32)
        self.ones_row = self.sb('ones_row', [1, 512], BF16)
        self.ctok = self.sb('ctok', [128, 4, 68], F32)
        self.rb = self.sb('rb', [128, 16, 4], F32)
        self.fb = [self.sb(f'fb{i}', [128, 68], F32) for i in range(2)]
        self.gpre = self.sb('gpre', [128, 4, 8], F32)
        self.qng = self.sb('qng', [128, 2], F32)
        self.kvg = self.sb('kvg', [128, 128], F32)
        self.nbf = self.sb('nbf', [4, 1], F32)
        self.cw = self.sb('cw', [128, 2, 4], F32)
        self.cbias = self.sb('cbias', [128, 2], F32)
        self.br = self.sb('br', [128, 2], F32)
        self.bi = self.sb('bi', [128, 2], F32)
        self.sl = self.sb('sl', [128, 2], F32)
        self.wr_bd = self.sb('wr_bd', [128, 2, 128], BF16)
        self.wi_bd = self.sb('wi_bd', [128, 2, 128], BF16)
        self.wuq = self.sb('wuq', [128, 2, 384], BF16)
        self.wuq_rot = self.sb('wuq_rot', [128, 2, 4, 96], BF16)
        self.wuk = self.sb('wuk', [128, 256], BF16)
        self.wuv = self.sb('wuv', [128, 256], BF16)
        self.MKT = self.sb('MKT', [128, 4, 256], BF16)
        self.MV = self.sb('MV', [128, 2, 512], BF16)
        self.ss = self.sb('ss', [128, 8], F32)
        self.rstd = self.sb('rstd', [128, 8], F32)
        self.hcar = self.sb('hcar', [128, 2], F32)
        self.ccar = self.sb('ccar', [4, 1], F32)
        self.invp = self.sb('invp', [128, 1], F32)
        self.invr = self.sb('invr', [128, 16], F32)
        RSZ = 31 * 1024
        self.R = self.sb('R', [128, RSZ], BF16)
        self.RSZ = RSZ
        banks = [Tl(nc.alloc_psum_tensor(f'ps{i}', [128, 512], F32).ap(), f'ps{i}') for i in range(8)]
        self.PS = TilePool(banks[0:6])
        self.ps_extra = banks[6:8]
        self.pT = [Tl(banks[6].ap.bitcast(BF16)[:, 0:512], 'pT0'), Tl(banks[7].ap.bitcast(BF16)[:, 0:512], 'pT1')]
        self._pTi = 0

    def carve(self, phase):
        R = self.R.ap
        off = [0]

        def take(n_bf16, key, dt=BF16):
            a = R[:, off[0]:off[0] + n_bf16]
            off[0] += n_bf16
            if dt is not BF16:
                a = a.bitcast(dt)
            return Tl(a, key)
        base = [t for t in self.Bp.free if not t.k.startswith('BX')]
        assert len(base) == 24, len(base)
        assert len(self.A.free) == 16, len(self.A.free)
        psb = [t for t in self.PS.free if t.k not in ('ps6', 'ps7')]
        assert len(psb) == 6, len(psb)
        self.PS.free = deque(psb + (self.ps_extra if phase == 'B' else []))
        self.Bp.free = deque(base)
        if phase in ('A', 'C'):
            self.xres = take(4 * D * 2, 'xres', F32).re("p (b d) -> p b d", d=D)
            self.xn = take(4 * D, 'xn')
            self.junk = take(D, 'junk')
            self.junk2 = take(D, 'junk2')
            if phase == 'A':
                self.tm4 = take(4 * 416 * 2, 'tm4', F32).re("p (b d) -> p b d", d=416)
                self.xpad = take(2 * 516 * 2, 'xpad', F32).re("p (c t) -> p c t", t=516)
                self.vaug = [take(4 * 4 * 68, f'vaug{i}') for i in range(2)]
                self.kpad = take(4 * 96, 'kpad').re("p (b d) -> p b d", d=96)
                self.ropeT = take(2 * 512 * 2, 'ropeT', F32).re("p (s t) -> p s t", t=512)
                self.ropeK = take(2 * 64 * 2, 'ropeK', F32).re("p (s b j) -> p s b j", s=2, j=16)
                self.ropetmp = take(2 * 512 * 2, 'ropetmp', F32).re("p (s t) -> p s t", t=512)
                self.ropei = take(2 * 512 * 2, 'ropei', I32).re("p (s t) -> p s t", t=512)
            i = 0
            while off[0] + 512 <= self.RSZ:
                self.Bp.put(take(512, f'BX{i}'))
                i += 1
        elif phase == 'B':
            self.ktbuf = [take(8192, f'ktbuf{i}') for i in range(2)]
            self.vbuf = [take(65 * 68, f'vbuf{i}') for i in range(2)]
            self.masks = [[take(512, f'mask{ty}{m}') for m in range(4)] for ty in range(3)]
        self._bt = {t.k: t for t in self.Bp.free}

    def MM(self, out, lhsT, rhs, start=True, stop=True, sgc=False):
        o, a, b = _ap(out), _ap(lhsT), _ap(rhs)
        if sgc:
            self.S.op('pe', lambda h: h.matmul(o, a, b, start=start, stop=stop, skip_group_check=True), reads=_keys(lhsT, rhs), writes=_keys(out))
        else:
            self.S.op('pe', lambda h: h.matmul(o, a, b, start=start, stop=stop), reads=_keys(lhsT, rhs), writes=_keys(out))

    def TR(self, out, in_, ident):
        o, a, b = _ap(out), _ap(in_), _ap(ident)
        self.S.op('pe', lambda h: h.transpose(o, a, b), reads=_keys(in_, ident), writes=_keys(out))

    def ACT(self, out, in_, func, bias=None, scale=None, accum=None):
        kw = {}
        if bias is not None:
            kw['bias'] = _ap(bias)
        if scale is not None:
            kw['scale'] = _ap(scale)
        if accum is not None:
            kw['accum_out'] = _ap(accum)
        o, a = _ap(out), _ap(in_)
        self.S.op('act', lambda h: h.activation(out=o, in_=a, func=func, **kw),
                  reads=_keys(in_, bias, scale), writes=_keys(out, accum))

    def TSC(self, eng, out, in0, s1, s2, op0, op1=None):
        o, a, x1, x2 = _ap(out), _ap(in0), _ap(s1), _ap(s2)
        if op1 is None:
            self.S.op(eng, lambda h: h.tensor_scalar(o, a, x1, None, op0), reads=_keys(in0, s1), writes=_keys(out))
        else:
            self.S.op(eng, lambda h: h.tensor_scalar(o, a, x1, x2, op0, op1), reads=_keys(in0, s1, s2), writes=_keys(out))

    def TT(self, eng, out, in0, in1, op):
        o, a, b = _ap(out), _ap(in0), _ap(in1)
        self.S.op(eng, lambda h: h.tensor_tensor(out=o, in0=a, in1=b, op=op), reads=_keys(in0, in1), writes=_keys(out))

    def STT(self, out, in0, scalar, in1, op0, op1):
        o, a, s, b = _ap(out), _ap(in0), _ap(scalar), _ap(in1)
        self.S.op('dve', lambda h: h.scalar_tensor_tensor(out=o, in0=a, scalar=s, in1=b, op0=op0, op1=op1),
                  reads=_keys(in0, scalar, in1), writes=_keys(out))

    def CP(self, eng, out, in_):
        o, a = _ap(out), _ap(in_)
        if eng == 'act':
            self.S.op('act', lambda h: h.copy(o, a), reads=_keys(in_), writes=_keys(out))
        else:
            self.S.op(eng, lambda h: h.tensor_copy(o, a), reads=_keys(in_), writes=_keys(out))

    def SCAN(self, out, d0, d1, init, op0, op1):
        o, a, b, i = _ap(out), _ap(d0), _ap(d1), _ap(init)
        self.S.op('dve', lambda h: h.tensor_tensor_scan(out=o, data0=a, data1=b, initial=i, op0=op0, op1=op1),
                  reads=_keys(d0, d1, init), writes=_keys(out))

    def RECIP(self, out, in_):
        o, a = _ap(out), _ap(in_)
        self.S.op('dve', lambda h: h.reciprocal(o, a), reads=_keys(in_), writes=_keys(out))

    def MEMSET(self, eng, out, val):
        o = _ap(out)
        self.S.op(eng, lambda h: h.memset(o, val), writes=_keys(out))

    def ASEL(self, out, in_, pattern, cmp, fill, base, cm):
        o, a = _ap(out), _ap(in_)
        regs = self.__dict__.setdefault('_fillregs', {})

        def fn(h):
            if fill not in regs:
                regs[fill] = h.to_reg(fill)
            return h.affine_select(out=o, in_=a, pattern=pattern, compare_op=cmp, fill=regs[fill], base=base, channel_multiplier=cm)
        self.S.op('pool', fn, reads=_keys(in_), writes=_keys(out))

    def IOTA(self, out, pattern, base, cm):
        o = _ap(out)
        self.S.op('pool', lambda h: h.iota(o, pattern, base=base, channel_multiplier=cm), writes=_keys(out))

    def LD(self, chan, out, in_, **kw):
        self.S.dma('sp', chan, [(_ap(out), _ap(in_))], reads=_keys(in_), writes=_keys(out), **kw)

    def ST(self, chan, out, in_, **kw):
        self.S.dma('pool', chan, [(_ap(out), _ap(in_))], reads=_keys(in_), writes=_keys(out), **kw)

    def ck(self, n):
        if int(os.environ.get('MK_SUB', '999')) == n:
            raise StopBuild()

    def ev(self):
        self._rr ^= 1
        return 'act' if self._rr else 'dve'

    def next_pT(self):
        self._pTi ^= 1
        return self.pT[self._pTi]

    def wplan_reset(self, plan):
        self.wplan = plan
        self.wissued = 0
        self.wused = 0

    def _wissue(self, i):
        name, src, npart, nk, ncol = self.wplan[i]
        slot = self.wslots[i % NW]
        view = slot[:npart, 0:nk * ncol].re("p (k c) -> p k c", c=ncol)
        self.LD(f'w{i % NW}', view, src)

    def wnext(self, expect):
        i = self.wused
        assert self.wplan[i][0] == expect, (self.wplan[i][0], expect)
        while self.wissued < min(len(self.wplan), i + NW - 2):
            self._wissue(self.wissued)
            self.wissued += 1
        self.wused += 1
        name, src, npart, nk, ncol = self.wplan[i]
        return self.wslots[i % NW][:npart, 0:nk * ncol].re("p (k c) -> p k c", c=ncol)

    def wsrc(self, name, l, r0, nk, c0, ncol, p=128):
        w = self.Wb[name]
        return Tl(w.ap[l, r0:r0 + nk * p, c0:c0 + ncol].rearrange("(k q) c -> q k c", q=p), w.k)

    def plan_A(self, l):
        pl = []
        for nm, c0, ncol in (('in1', 0, 512), ('in2', 512, 260), ('in3', 772, 512), ('in4', 1284, 416),
                             ('in5', 1700, 512), ('in6', 2212, 256)):
            pl.append((nm, self.wsrc('w_in', l, 0, 8, c0, ncol), 128, 8, ncol))
        return pl

    def plan_C(self, l):
        pl = []
        for n in range(4):
            pl.append((f'wb{n}', self.wsrc('w_branch', l, n * 256, 2, 0, 1024), 128, 2, 1024))
            for hf in range(2):
                pl.append((f'gate{n}{hf}', self.wsrc('w_in', l, 0, 8, 2468 + n * 1024 + hf * 512, 512), 128, 8, 512))
        for hf in range(2):
            pl.append((f'wout{hf}', self.wsrc('w_out', l, 0, 8, hf * 512, 512), 128, 8, 512))
        pl.append(('wq', self.wsrc('mem_wq', l, 0, 8, 0, 512), 128, 8, 512))
        for hf in range(2):
            pl.append((f'wo{hf}', self.wsrc('mem_wo', l, 0, 4, hf * 512, 512), 128, 4, 512))
        for j in range(6):
            ncol = 512 if j < 5 else 256
            pl.append((f'wg{j}', self.wsrc('ffn_wg', l, 0, 8, j * 512, ncol), 128, 8, ncol))
            pl.append((f'wu{j}', self.wsrc('ffn_wu', l, 0, 8, j * 512, ncol), 128, 8, ncol))
        for hf in range(2):
            for kg, nk in enumerate((8, 8, 6)):
                pl.append((f'wd{hf}{kg}', self.wsrc('ffn_wd', l, kg * 1024, nk, hf * 512, 512), 128, nk, 512))
        return pl

    def setup_consts(self):
        z = self.A.get()
        self.MEMSET('pool', z, 0.0)
        self.MEMSET('pool', self.ones_f, 1.0)
        self.MEMSET('pool', self.ones_bf, 1.0)
        self.MEMSET('pool', self.ones_row, 1.0)
        self.MEMSET('pool', self.ctok, 0.0)
        self.MEMSET('pool', self.ident_f, 0.0)
        self.ASEL(self.ident_f, self.ident_f, [[-1, 128]], ALU.not_equal, 1.0, 0, 1)
        self.CP('dve', self.ident_bf, self.ident_f)
        self.MEMSET('pool', self.E0, 0.0)
        self.ASEL(self.E0, self.E0, [[0, 128]], ALU.not_equal, 1.0, 0, 1)
        self.ASEL(self.U, self.ones_bf, [[-1, 128]], ALU.is_ge, 0.0, 0, 1)
        ti = self.A.get()
        tiv = Tl(ti.ap.bitcast(I32), ti.k)
        self.IOTA(tiv[:, 0:1], [[0, 1]], 0, 1)
        self.S.op('dve', lambda h: h.tensor_single_scalar(out=tiv.ap[:, 0:1], in_=tiv.ap[:, 0:1], scalar=15, op=ALU.bitwise_and),
                  reads=[tiv.k], writes=[tiv.k])
        tf = self.A.get()
        self.CP('dve', tf[:, 0:1], tiv[:, 0:1])
        self.ACT(self.invp, tf[:, 0:1], AF.Exp, scale=-math.log(10000.0) / 16.0)
        self.IOTA(tiv[:, 16:32], [[1, 16]], 0, 0)
        self.CP('dve', tf[:, 16:32], tiv[:, 16:32])
        self.ACT(self.invr, tf[:, 16:32], AF.Exp, scale=-math.log(10000.0) / 16.0)
        self.A.put(z, ti, tf)

    def make_masks(self):
        zb = self.Bp.get()
        sm = self.Bp.get()
        self.MEMSET('pool', zb, 0.0)
        for m in range(4):
            self.ASEL(self.masks[0][m], zb, [[1, 512]], ALU.is_ge, NEG, -128 * m, -1)
            self.ASEL(sm[:, 8 * m:8 * m + 8], zb[:, 0:8], [[64, 8]], ALU.is_ge, NEG, 63 - 128 * m, -1)
            self.CP('dve', self.masks[1][m].re("p (a b) -> p a b", b=64), sm[:, 8 * m:8 * m + 8].us(2).bc([128, 8, 64]))
            self.ASEL(self.masks[2][m], zb, [[1, 512]], ALU.is_ge, NEG, -128 * m - 1, -1)
        self.Bp.put(zb, sm)

    def cast_weights(self):
        i = 0
        for n in self.big:
            src = self.I[n]
            dst = self.Wb[n]
            shp = self.wshapes[n]
            tot = int(np.prod(shp))
            per = tot // 128
            dims = " ".join("abc"[:len(shp)])
            sv = src.ap.rearrange(f"{dims} -> ({dims})").rearrange("(p x) -> p x", p=128)
            dv = dst.ap.rearrange(f"{dims} -> ({dims})").rearrange("(p x) -> p x", p=128)
            CH = 2048
            for x0 in range(0, per, CH):
                w = min(CH, per - x0)
                for y0 in range(x0, x0 + w, 512):
                    ww = min(512, x0 + w - y0)
                    a = self.A.get()
                    b = self.Bp.get()
                    self.LD(f'cv{i % 4}', a[:, :ww], Tl(sv[:, y0:y0 + ww], src.k))
                    self.CP(('act', 'dve', 'pool')[i % 3] if i % 3 != 2 else 'dve', b[:, :ww], a[:, :ww])
                    self.S.dma('pool', f'cs{i % 4}', [(dv[:, y0:y0 + ww], b.ap[:, :ww])], reads=[b.k], writes=[])
                    self.A.put(a)
                    self.Bp.put(b)
                    i += 1
        self.S.barrier()

    def load_layer_params(self, l):
        I = self.I
        ld = lambda out, in_, **kw: self.LD('par', out, in_, **kw)
        for j, n in enumerate(('ln_mix_pre', 'ln_mem_pre', 'ln_ffn_pre', 'mem_norm')):
            ld(self.gpre[:, j, :], Tl(I[n].ap[l].rearrange("(k p) -> p k", p=128), I[n].k), allow_slow_non_contiguous=True)
        for j, n in enumerate(('ln_mix_post', 'ln_mem_post', 'ln_ffn_post')):
            ld(self.gpost[j], Tl(I[n].ap[l].partition_broadcast(128), I[n].k))
        ld(self.qng, Tl(I['mla_q_norm'].ap[l].rearrange("(k p) -> p k", p=128), 'x'), allow_slow_non_contiguous=True)
        ld(self.kvg, Tl(I['mla_kv_norm'].ap[l].partition_broadcast(128), 'x'))
        t = self.A.get()
        ld(t[0:4, 0:1], Tl(I['fox_bf'].ap[l].rearrange("(p o) -> p o", o=1), 'x'))
        self.TSC('dve', self.nbf, t[0:4, 0:1], -1.0, None, ALU.mult)
        for c in range(2):
            ld(self.cw[:, c, :], Tl(I['lru_conv_w'].ap[l][:, c * 128:(c + 1) * 128].rearrange("t p -> p t"), 'x'), allow_slow_non_contiguous=True)
        for dst, n in ((self.cbias, 'lru_conv_b'), (self.br, 'lru_br'), (self.bi, 'lru_bi')):
            ld(dst, Tl(I[n].ap[l].rearrange("(c p) -> p c", p=128), 'x'), allow_slow_non_contiguous=True)
        ld(t[:, 8:10], Tl(I['lru_lam'].ap[l].rearrange("(c p) -> p c", p=128), 'x'), allow_slow_non_contiguous=True)
        self.ACT(t[:, 8:10], t[:, 8:10], AF.Exp, scale=-1.0)
        self.ACT(t[:, 8:10], t[:, 8:10], AF.Ln, bias=1.0)
        self.TSC('dve', self.sl, t[:, 8:10], -8.0, None, ALU.mult)
        for dst, n in ((self.wr_bd, 'lru_wr'), (self.wi_bd, 'lru_wi')):
            ld(t[:, 64:192].re("p (c j) -> p c j", j=64), Tl(I[n].ap[l].rearrange("(c p) j -> p c j", p=128), 'x'))
            self.MEMSET('pool', dst, 0.0)
            for c in range(2):
                self.CP('dve', dst[0:64, c, 0:64], t[0:64, 64 + c * 64:128 + c * 64])
                self.CP('dve', dst[64:128, c, 64:128], t[64:128, 64 + c * 64:128 + c * 64])
        t2 = self.A.get()
        t3 = self.A.get()
        ld(t2[:, 0:384], Tl(I['mla_w_uq'].ap[l, 0:128, :], 'x'))
        ld(t3[:, 0:384], Tl(I['mla_w_uq'].ap[l, 128:256, :], 'x'))
        self.MEMSET('pool', self.wuq_rot, 0.0)
        for kc, tt in enumerate((t2, t3)):
            self.CP('dve', self.wuq[:, kc, :], tt[:, 0:384])
            tv = tt[:, 0:384].re("p (h c) -> p h c", c=96)
            self.TSC('dve', self.wuq_rot[:, kc, :, 64:80], tv[:, :, 80:96], -1.0, None, ALU.mult)
            self.CP('dve', self.wuq_rot[:, kc, :, 80:96], tv[:, :, 64:80])
        t4 = self.A.get()
        ld(t4[:, 0:256], Tl(I['mla_w_uk'].ap[l], 'x'))
        ld(t4[:, 256:512], Tl(I['mla_w_uv'].ap[l], 'x'))
        self.CP('dve', self.wuk, t4[:, 0:256])
        self.CP('dve', self.wuv, t4[:, 256:512])
        self.A.put(t, t2, t3, t4)

    def rmsnorm_T(self, sq, src, W, gain_pp, NT=None, TB=None, nb=None):
        NT = NT or sq.NT
        TB = TB or sq.TB
        nb = nb or sq.nb
        nkc = W // 128
        ss, rstd = self.ss, self.rstd
        for b in range(nb):
            if b % 2 == 0 or W != D:
                self.ACT(self.junk[:TB, :W], src[:TB, b, :], AF.Square, accum=ss[:TB, b:b + 1])
            else:
                o_, a_, acc_ = _ap(self.junk2[:TB, :W]), _ap(src[:TB, b, :]), _ap(ss[:TB, b:b + 1])
                self.S.op('dve', lambda h, o_=o_, a_=a_, acc_=acc_: h.scalar_tensor_tensor(out=o_, in0=a_, scalar=1.0, in1=a_, op0=ALU.mult, op1=ALU.mult, accum_out=acc_),
                          reads=[src.k], writes=[self.junk2.k, ss.k])
        self.TSC('dve', rstd[:TB, :nb], ss[:TB, :nb], 1.0 / W, EPS, ALU.mult, ALU.add)
        self.ACT(rstd[:TB, :nb], rstd[:TB, :nb], AF.Sqrt)
        self.RECIP(rstd[:TB, :nb], rstd[:TB, :nb])
        xn = self.xn[:, 0:nb * W].re("p (b d) -> p b d", d=W)
        for b in range(nb):
            if b % 2 == 0:
                self.TSC('dve', xn[:TB, b, :], src[:TB, b, :], rstd[:TB, b:b + 1], None, ALU.mult)
            else:
                self.ACT(xn[:TB, b, :], src[:TB, b, :], AF.Copy, scale=rstd[:TB, b:b + 1])
        return self.transpose_T(xn, W, gain_pp, NT, TB, nb)

    def transpose_T(self, xn, W, gain_pp, NT, TB, nb):
        outs = []
        for kc in range(W // 128):
            pT = self.next_pT()
            for b in range(nb):
                self.TR(pT[:, b * TB:(b + 1) * TB], xn[:TB, b, kc * 128:(kc + 1) * 128], self.ident_bf[:TB, :TB])
            o = self.Bp.get()
            if gain_pp is None:
                self.CP(self.ev(), o[:, :NT], pT[:, :NT])
            elif kc % 2 == 0:
                self.TSC('dve', o[:, :NT], pT[:, :NT], gain_pp[:, kc:kc + 1], None, ALU.mult)
            else:
                self.ACT(o[:, :NT], pT[:, :NT], AF.Copy, scale=gain_pp[:, kc:kc + 1])
            outs.append(o)
        return outs

    def fm_proj(self, wv, c0, M, hT, NT, nk=None):
        ps = self.PS.get()
        nk = nk or len(hT)
        for kc in range(nk):
            self.MM(ps[0:M, :NT], wv[:, kc, c0:c0 + M], hT[kc][:, :NT], start=(kc == 0), stop=(kc == nk - 1))
        return ps

    def tm_proj(self, wv, c0, ncol, hT, b, TB):
        ps = self.PS.get()
        nk = len(hT)
        for kc in range(nk):
            self.MM(ps[0:TB, :ncol], hT[kc][:, b * TB:(b + 1) * TB], wv[:, kc, c0:c0 + ncol], start=(kc == 0), stop=(kc == nk - 1))
        return ps

    def rope_tables(self, sq, pos0):
        NT, TB, nb = sq.NT, sq.TB, sq.nb
        tw = 2 * math.pi
        for mode in ('T', 'K'):
            if mode == 'T':
                ang = self.ropetmp
                iv = self.ropei
                self.IOTA(iv[:, 0, :NT], [[1, NT]], pos0, 0)
                self.CP('dve', ang[:, 1, :NT], iv[:, 0, :NT])
                self.TSC('dve', ang[:, 1, :NT], ang[:, 1, :NT], self.invp[:, 0:1], None, ALU.mult)
                self.TSC('dve', ang[:, 0, :NT], ang[:, 1, :NT], math.pi / 2, None, ALU.add)
                a = ang[:, :, :NT]
                ii = iv[:, :, :NT]
                dst = self.ropeT[:, :, :NT]
                tmp = self.ropeT[:, :, :NT]
            else:
                angf = self.ropetmp[:, 0, 0:2 * nb * 16].re("p (s b j) -> p s b j", s=2, j=16)
                ivf = self.ropei[:, 0, 0:2 * nb * 16].re("p (s b j) -> p s b j", s=2, j=16)
                self.IOTA(ivf[:, 1, :, 0], [[128, nb]], pos0, 1)
                self.CP('dve', angf[:, 0, :, 0], ivf[:, 1, :, 0])
                self.TT('dve', angf[:TB, 1], angf[:TB, 0, :, 0:1].bc([TB, nb, 16]), self.invr[:TB].us(1).bc([TB, nb, 16]), ALU.mult)
                self.TSC('dve', angf[:TB, 0], angf[:TB, 1], math.pi / 2, None, ALU.add)
                a = angf[:TB]
                ii = ivf[:TB]
                dst = self.ropeK[:TB, :, :nb, :]
                tmp = self.ropeK[:TB, :, :nb, :]
            self.TSC('dve', tmp, a, 1.0 / tw, None, ALU.mult)
            self.CP('dve', ii, tmp)
            self.CP('dve', tmp, ii)
            self.STT(a, tmp, -tw, a, ALU.mult, ALU.add)
            self.TSC('dve', tmp, a, math.pi, tw, ALU.is_gt, ALU.mult)
            self.TT('dve', a, a, tmp, ALU.subtract)
            self.TSC('dve', tmp, a, -math.pi, tw, ALU.is_lt, ALU.mult)
            self.TT('dve', a, a, tmp, ALU.add)
            self.TSC('dve', a, a, math.pi, -math.pi, ALU.min, ALU.max)
            self.ACT(dst, a, AF.Sin)

    def mla_kv(self, sq, l, ckvn, kper, tok0, NT, TB, nb):
        xn = self.xn[:, 0:nb * 128].re("p (b d) -> p b d", d=128)
        self.CP('dve', xn[:TB], ckvn[:TB])
        ckT = self.transpose_T(xn, 128, None, NT, TB, nb)[0]
        self.CP('dve', self.kpad[:TB, :nb, 64:96], kper[:TB])
        pT2 = self.next_pT()
        for b in range(nb):
            self.TR(pT2[0:96, b * TB:(b + 1) * TB], self.kpad[:TB, b, :], self.ident_bf[:TB, :TB])
        blk0 = tok0 // 128
        va = self.vaug[0]
        vav = va[:, 0:nb * 272].re("p (b h d) -> p b h d", h=4, d=68)
        for b in range(nb):
            ps = self.PS.get()
            self.MM(ps[0:TB, 0:256], ckT[:, b * TB:(b + 1) * TB], self.wuv)
            self.CP(self.ev(), vav[:TB, b, :, 0:64], ps[0:TB, 0:256].re("p (h d) -> p h d", d=64))
            self.PS.put(ps)
        for h in range(4):
            ps = self.PS.get()
            self.MM(ps[0:64, :NT], self.wuk[:, h * 64:(h + 1) * 64], ckT[:, :NT])
            kt = self.Bp.get()
            self.CP('act', kt[0:64, :NT], ps[0:64, :NT])
            self.CP('dve', kt[64:96, :NT], pT2[64:96, :NT])
            self.PS.put(ps)
            self.ST(f'sk{h}', sq.KTs[1][h][:, tok0:tok0 + NT], kt[0:96, :NT])
            self.Bp.put(kt)
            self.ST(f'sv{h}', sq.Vs[1][h][:TB, blk0:blk0 + nb, :], vav[:TB, :, h, :])
        self.Bp.put(ckT)

    def phaseA(self, l, sq):
        S = self.S
        S.barrier()
        self.carve('A')
        I, O = self.I, self.O
        pre = sq.name
        NT, TB, nb = sq.NT, sq.TB, sq.nb
        self.wplan_reset(self.plan_A(l) * sq.ntiles)
        for va in self.vaug:
            self.MEMSET('pool', va, 1.0)
        self.MEMSET('pool', self.kpad, 0.0)
        if sq.P == 0:
            self.MEMSET('pool', self.hcar, 0.0)
            self.MEMSET('pool', self.ccar, 0.0)
            self.MEMSET('pool', self.xpad[:, :, 0:3], 0.0)
        else:
            self.LD('par', self.hcar, Tl(I['state_lru_h'].ap[l].rearrange("(c p) -> p c", p=128), 'x'), allow_slow_non_contiguous=True)
            for c in range(2):
                self.LD('par', self.xpad[:, c, 0:3], Tl(I['state_lru_conv'].ap[l][:, c * 128:(c + 1) * 128].rearrange("j p -> p j"), 'x'),
                        allow_slow_non_contiguous=True)
            self.sample_prep(l, sq)
        x_src = sq.x_in if l == 0 else sq.x_mid

        def prologue(qt_):
            t0_ = qt_ * NT
            self.LD('x', self.xres[:TB, :nb, :], Tl(x_src.ap[t0_:t0_ + NT, :].rearrange("(b p) d -> p b d", p=TB), x_src.k))
            hT_ = self.rmsnorm_T(sq, self.xres, D, self.gpre[:, 0, :])
            for kc in range(8):
                self.ST(f'sh{kc % 2}', sq.HT[qt_, :, kc, :], hT_[kc][:, :NT])
            return hT_
        hT_next = None
        for qt in range(sq.ntiles):
            t0 = qt * NT
            g0 = sq.P + t0
            blk0 = g0 // 128
            hT = hT_next if hT_next is not None else prologue(qt)
            self.ck(1)
            self.rope_tables(sq, g0)
            self.ck(2)
            wv = self.wnext('in1')
            for ty, qcol, kcol, outn in ((0, 0, 256, 'fox_k'),):
                self.qk_proj(sq, l, wv, ty, hT, t0, g0, outn)
            self.ck(3)
            wv = self.wnext('in2')
            st = [self.A.get(), self.A.get()]
            va = self.vaug[0]
            vav = va[:, 0:nb * 272].re("p (b h d) -> p b h d", h=4, d=68)
            VV = os.environ.get('MK_VAR', '')
            for b in range(nb):
                if 'a' in VV:
                    break
                ps = self.tm_proj(wv, 0, 256, hT, b, TB)
                if 'b' not in VV:
                    self.CP('act', st[b // 2][:TB, (b % 2) * 256:(b % 2) * 256 + 256], ps[0:TB, 0:256])
                if 'c' not in VV:
                    self.CP('dve', vav[:TB, b, :, 0:64], ps[0:TB, 0:256].re("p (h d) -> p h d", d=64))
                self.PS.put(ps)
            self.ck(30)
            self.store_tm(O[f'{pre}_fox_v'], l, t0, st, 256, TB, nb)
            self.ck(301)
            for h in range(4):
                self.ST(f'sv{h}', sq.Vs[0][h][:TB, blk0:blk0 + nb, :], vav[:TB, :, h, :])
            self.A.put(*st)
            self.ck(31)
            ps = self.fm_proj(wv, 256, 4, hT, NT)
            lf = self.A.get()
            self.ACT(lf[0:4, :NT], ps[0:4, :NT], AF.Exp, bias=self.nbf[0:4, 0:1], scale=-1.0)
            self.PS.put(ps)
            self.ACT(lf[0:4, :NT], lf[0:4, :NT], AF.Ln, bias=1.0)
            self.TSC('dve', lf[0:4, :NT], lf[0:4, :NT], -1.0, None, ALU.mult)
            cT = self.A.get()
            self.SCAN(cT[0:4, :NT], self.ones_f[0:4, :NT], lf[0:4, :NT], self.ccar[0:4, 0:1], ALU.mult, ALU.add)
            self.CP('dve', self.ccar[0:4, 0:1], cT[0:4, NT - 1:NT])
            ref = 256 if NT == 512 else 0
            dqb = self.Bp.get()
            self.TSC('dve', dqb[0:4, :NT], cT[0:4, :NT], cT[0:4, ref:ref + 1], None, ALU.subtract)
            for h in range(4):
                self.ST(f'sq{h % 2}', sq.Qs[0][h][64:65, t0:t0 + NT], dqb[h:h + 1, :NT])
                self.ST(f'sk{h}', sq.KTs[0][h][64:65, g0:g0 + NT], self.ones_row[0:1, :NT])
            self.Bp.put(dqb)
            self.ck(32)
            ps = self.PS.get()
            for b in range(nb):
                self.TR(ps[0:TB, b * 4:b * 4 + 4], lf[0:4, b * TB:(b + 1) * TB], self.ident_f[0:4, 0:4])
                self.TR(ps[0:TB, 64 + b * 4:64 + b * 4 + 4], cT[0:4, b * TB:(b + 1) * TB], self.ident_f[0:4, 0:4])
            lft = self.A.get()
            self.CP('dve', lft[:TB, 0:nb * 4], ps[0:TB, 0:nb * 4])
            self.CP('dve', self.ctok[:TB, :, blk0:blk0 + nb], ps[0:TB, 64:64 + nb * 4].re("p (b h) -> p h b", h=4))
            self.PS.put(ps)
            self.ck(33)
            self.ST('slf', Tl(O[f'{pre}_fox_logf'].ap[l, t0:t0 + NT, :].rearrange("(b p) h -> p b h", p=TB), O[f'{pre}_fox_logf'].k),
                    lft[:TB, 0:nb * 4].re("p (b h) -> p b h", h=4))
            self.A.put(lf, cT, lft)
            self.ck(4)
            wv = self.wnext('in3')
            gs = self.lru_proj(sq, wv, hT)
            self.ck(5)
            wv = self.wnext('in4')
            for b in range(nb):
                ps = self.tm_proj(wv, 0, 416, hT, b, TB)
                self.CP(self.ev(), self.tm4[:TB, b, :], ps[0:TB, 0:416])
                self.PS.put(ps)
            self.mla_new(sq, l, t0, g0)
            self.ck(6)
            wv = self.wnext('in5')
            self.qk_proj(sq, l, wv, 2, hT, t0, g0, 'sb_k')
            wv = self.wnext('in6')
            st = [self.A.get(), self.A.get()]
            va = self.vaug[1]
            vav = va[:, 0:nb * 272].re("p (b h d) -> p b h d", h=4, d=68)
            for b in range(nb):
                ps = self.tm_proj(wv, 0, 256, hT, b, TB)
                self.CP('act', st[b // 2][:TB, (b % 2) * 256:(b % 2) * 256 + 256], ps[0:TB, 0:256])
                self.CP('dve', vav[:TB, b, :, 0:64], ps[0:TB, 0:256].re("p (h d) -> p h d", d=64))
                self.PS.put(ps)
            self.store_tm(O[f'{pre}_sb_v'], l, t0, st, 256, TB, nb)
            for h in range(4):
                self.ST(f'sv{h}', sq.Vs[2][h][:TB, blk0:blk0 + nb, :], vav[:TB, :, h, :])
            self.A.put(*st)
            self.Bp.put(*hT)
            hT_next = prologue(qt + 1) if qt + 1 < sq.ntiles else None
            self.lru(sq, l, gs, qt, t0)

    def store_tm(self, dst, l, t0, st, W, TB, nb):
        for j in range((nb + 1) // 2):
            nbb = min(2, nb - 2 * j)
            self.ST('stm', Tl(dst.ap[l, t0 + 2 * j * TB:t0 + (2 * j + nbb) * TB, :].rearrange("(b p) c -> p b c", p=TB), dst.k),
                    st[j][:TB, 0:nbb * W].re("p (b c) -> p b c", c=W))

    def qk_proj(self, sq, l, wv, ty, hT, t0, g0, outn):
        NT, TB, nb = sq.NT, sq.TB, sq.nb
        O = self.O
        for h in range(4):
            ps = self.fm_proj(wv, h * 64, 64, hT, NT)
            qb = self.Bp.get()
            self.TSC('dve', qb[0:64, :NT], ps[0:64, :NT], 0.125, None, ALU.mult)
            self.PS.put(ps)
            self.ST(f'sq{h % 2}', sq.Qs[ty][h][0:64, t0:t0 + NT], qb[0:64, :NT])
            self.Bp.put(qb)
            ps = self.fm_proj(wv, 256 + h * 64, 64, hT, NT)
            kb = self.Bp.get()
            self.CP('act', kb[0:64, :NT], ps[0:64, :NT])
            self.PS.put(ps)
            self.ST(f'sk{h}', sq.KTs[ty][h][0:64, g0:g0 + NT], kb[0:64, :NT])
            self.Bp.put(kb)
        st = [self.A.get(), self.A.get()]
        for b in range(nb):
            ps = self.tm_proj(wv, 256, 256, hT, b, TB)
            self.CP(self.ev(), st[b // 2][:TB, (b % 2) * 256:(b % 2) * 256 + 256], ps[0:TB, 0:256])
            self.PS.put(ps)
        self.store_tm(O[f'{sq.name}_{outn}'], l, t0, st, 256, TB, nb)
        self.A.put(*st)

    def lru_proj(self, sq, wv, hT):
        NT = sq.NT
        gs = []
        for c in range(2):
            ps = self.fm_proj(wv, c * 128, 128, hT, NT)
            self.CP('act', self.xpad[:, c, 3:3 + NT], ps[:, :NT])
            self.PS.put(ps)
            ps = self.fm_proj(wv, 256 + c * 128, 128, hT, NT)
            g = self.A.get()
            self.ACT(g[:, :NT], ps[:, :NT], AF.Gelu_apprx_tanh)
            self.PS.put(ps)
            gs.append(g)
        return gs

    def lru(self, sq, l, gs, qt, t0):
        NT = sq.NT
        O = self.O
        pre = sq.name
        last = (qt == sq.ntiles - 1)

        def chain(c):
            g = gs[c]
            xc = self.A.get()
            self.TSC('dve', xc[:, :NT], self.xpad[:, c, 0:NT], self.cw[:, c, 0:1], self.cbias[:, c:c + 1], ALU.mult, ALU.add)
            yield
            for tap in range(1, 4):
                self.STT(xc[:, :NT], self.xpad[:, c, tap:tap + NT], self.cw[:, c, tap:tap + 1], xc[:, :NT], ALU.mult, ALU.add)
                yield
            xcb = self.Bp.get()
            self.CP('act', xcb[:, :NT], xc[:, :NT])
            yield
            ps = self.PS.get()
            self.MM(ps[:, :NT], self.wr_bd[:, c, :], xcb[:, :NT])
            r = self.A.get()
            self.ACT(r[:, :NT], ps[:, :NT], AF.Sigmoid, bias=self.br[:, c:c + 1])
            self.PS.put(ps)
            yield
            ps = self.PS.get()
            self.MM(ps[:, :NT], self.wi_bd[:, c, :], xcb[:, :NT])
            ig = self.A.get()
            self.ACT(ig[:, :NT], ps[:, :NT], AF.Sigmoid, bias=self.bi[:, c:c + 1])
            self.PS.put(ps)
            self.Bp.put(xcb)
            yield
            a = self.A.get()
            self.ACT(a[:, :NT], r[:, :NT], AF.Exp, scale=self.sl[:, c:c + 1])
            yield
            self.TT('dve', r[:, :NT], a[:, :NT], a[:, :NT], ALU.mult)
            yield
            self.ACT(r[:, :NT], r[:, :NT], AF.Sqrt, bias=1.0, scale=-1.0)
            yield
            self.TT('dve', ig[:, :NT], ig[:, :NT], xc[:, :NT], ALU.mult)
            yield
            self.TT('dve', r[:, :NT], r[:, :NT], ig[:, :NT], ALU.mult)
            yield
            if sq.P == 0 and qt == 0:
                self.MEMSET('pool', a[:, 0:1], 0.0)
                self.CP('dve', r[:, 0:1], ig[:, 0:1])
            hs = xc
            self.SCAN(hs[:, :NT], a[:, :NT], r[:, :NT], self.hcar[:, c:c + 1], ALU.mult, ALU.add)
            yield
            self.CP('dve', self.hcar[:, c:c + 1], hs[:, NT - 1:NT])
            ob = self.Bp.get()
            self.TT('dve', ob[:, :NT], hs[:, :NT], g[:, :NT], ALU.mult)
            self.ST(f'sob{c}', sq.OB[:, c, t0:t0 + NT], ob[:, :NT])
            self.Bp.put(ob)
            self.A.put(g, xc, r, ig, a)

        gens = [chain(0), chain(1)]
        while gens:
            for gen in list(gens):
                try:
                    next(gen)
                except StopIteration:
                    gens.remove(gen)
        if last:
            self.ST('slh', Tl(O[pre + '_lru_h'].ap[l].rearrange("(c p) -> p c", p=128), O[pre + '_lru_h'].k), self.hcar,
                    allow_slow_non_contiguous=True)
            for c in range(2):
                self.ST('slc', Tl(O[pre + '_lru_conv'].ap[l][:, c * 128:(c + 1) * 128].rearrange("j p -> p j"), O[pre + '_lru_conv'].k),
                        self.xpad[:, c, NT:NT + 3], allow_slow_non_contiguous=True)
        else:
            t = self.A.get()
            self.CP('dve', t[:, 0:6].re("p (c j) -> p c j", j=3), self.xpad[:, :, NT:NT + 3])
            self.CP('dve', self.xpad[:, :, 0:3], t[:, 0:6].re("p (c j) -> p c j", j=3))
            self.A.put(t)

    def mla_new(self, sq, l, t0, g0):
        NT, TB, nb = sq.NT, sq.TB, sq.nb
        O = self.O
        pre = sq.name
        tm4 = self.tm4
        cqT = self.rmsnorm_T(sq, tm4[:, :, 0:256], 256, self.qng)
        for h in range(4):
            psq = self.fm_proj(self.wuq, h * 96, 96, cqT, NT)
            psr = self.PS.get()
            for kc in range(2):
                self.MM(psr[0:96, :NT], self.wuq_rot[:, kc, h, :], cqT[kc][:, :NT], start=(kc == 0), stop=(kc == 1))
            qb = self.Bp.get()
            self.CP('act', qb[0:64, :NT], psq[0:64, :NT])
            t1 = self.A.get()
            t2 = self.A.get()
            self.TT('dve', t1[64:96, :NT], psq[64:96, :NT], self.ropeT[64:96, 0, :NT], ALU.mult)
            self.TT('dve', t2[64:96, :NT], psr[64:96, :NT], self.ropeT[64:96, 1, :NT], ALU.mult)
            self.TT('dve', qb[64:96, :NT], t1[64:96, :NT], t2[64:96, :NT], ALU.add)
            self.PS.put(psq, psr)
            self.A.put(t1, t2)
            self.ST(f'sq{h % 2}', sq.Qs[1][h][:, t0:t0 + NT], qb[0:96, :NT])
            self.Bp.put(qb)
        self.Bp.put(*cqT)
        ss, rstd = self.ss, self.rstd
        for b in range(nb):
            self.ACT(self.junk[:TB, :128], tm4[:TB, b, 256:384], AF.Square, accum=ss[:TB, b:b + 1])
        self.TSC('dve', rstd[:TB, :nb], ss[:TB, :nb], 1.0 / 128, EPS, ALU.mult, ALU.add)
        self.ACT(rstd[:TB, :nb], rstd[:TB, :nb], AF.Sqrt)
        self.RECIP(rstd[:TB, :nb], rstd[:TB, :nb])
        ck = self.A.get()
        ckv = ck[:, 0:nb * 128].re("p (b d) -> p b d", d=128)
        for b in range(nb):
            self.STT(ckv[:TB, b, :], tm4[:TB, b, 256:384], rstd[:TB, b:b + 1], self.kvg[:TB, :], ALU.mult, ALU.mult)
        self.ST('sck', Tl(O[pre + '_mla_ckv'].ap[l, t0:t0 + NT, :].rearrange("(b p) c -> p b c", p=TB), O[pre + '_mla_ckv'].k), ckv[:TB])
        kp = self.A.get()
        kpv = kp[:, 0:nb * 32].re("p (b d) -> p b d", d=32)
        t1 = self.A.get()
        t1v = t1[:, 0:nb * 32].re("p (b d) -> p b d", d=32)
        cosk = self.ropeK[:TB, 0, :nb, :]
        sink = self.ropeK[:TB, 1, :nb, :]
        x1 = tm4[:TB, :nb, 384:400]
        x2 = tm4[:TB, :nb, 400:416]
        self.TT('dve', kpv[:TB, :, 0:16], x1, cosk, ALU.mult)
        self.TT('dve', t1v[:TB, :, 0:16], x2, sink, ALU.mult)
        self.TT('dve', kpv[:TB, :, 0:16], kpv[:TB, :, 0:16], t1v[:TB, :, 0:16], ALU.subtract)
        self.TT('dve', kpv[:TB, :, 16:32], x1, sink, ALU.mult)
        self.TT('dve', t1v[:TB, :, 16:32], x2, cosk, ALU.mult)
        self.TT('dve', kpv[:TB, :, 16:32], kpv[:TB, :, 16:32], t1v[:TB, :, 16:32], ALU.add)
        self.ST('skp', Tl(O[pre + '_mla_kpe'].ap[l, t0:t0 + NT, :].rearrange("(b p) c -> p b c", p=TB), O[pre + '_mla_kpe'].k), kpv[:TB])
        self.mla_kv(sq, l, ckv, kpv, g0, NT, TB, nb)
        self.A.put(ck, kp, t1)

    def sample_prep(self, l, sq):
        I = self.I
        for ty, kn, vn in ((0, 'cache_fox_k', 'cache_fox_v'), (2, 'cache_sb_k', 'cache_sb_v')):
            for j in range(PAST // 512):
                kk = [self.A.get(), self.A.get()]
                vv = [self.A.get(), self.A.get()]
                va = self.vaug[j % 2]
                vav = va[:, 0:4 * 272].re("p (b h d) -> p b h d", h=4, d=68)
                for jj in range(2):
                    r0 = j * 512 + jj * 256
                    self.LD('ck', kk[jj].re("p (b c) -> p b c", c=256), Tl(I[kn].ap[l, r0:r0 + 256, :].rearrange("(b p) c -> p b c", p=128), 'x'))
                    self.LD('cv', vv[jj].re("p (b c) -> p b c", c=256), Tl(I[vn].ap[l, r0:r0 + 256, :].rearrange("(b p) c -> p b c", p=128), 'x'))
                for b in range(4):
                    self.CP(self.ev(), vav[:, b, :, 0:64], vv[b // 2][:, (b % 2) * 256:(b % 2) * 256 + 256].re("p (h d) -> p h d", d=64))
                for h in range(4):
                    self.ST(f'sv{h}', sq.Vs[ty][h][:, 4 * j:4 * j + 4, :], vav[:, :, h, :])
                    ps = self.PS.get()
                    for b in range(4):
                        self.TR(ps[0:64, b * 128:(b + 1) * 128], kk[b // 2][:, (b % 2) * 256 + h * 64:(b % 2) * 256 + h * 64 + 64], self.ident_f)
                    kb = self.Bp.get()
                    self.CP(self.ev(), kb[0:64, :], ps[0:64, :])
                    self.PS.put(ps)
                    self.ST(f'sk{h}', sq.KTs[ty][h][0:64, j * 512:(j + 1) * 512], kb[0:64, :])
                    if ty == 0:
                        self.ST(f'sk{h}', sq.KTs[0][h][64:65, j * 512:(j + 1) * 512], self.ones_row[0:1, :])
                    self.Bp.put(kb)
                self.A.put(*kk, *vv)
        for j in range(PAST // 512):
            ck = self.A.get()
            kp = self.A.get()
            ckv = ck.re("p (b d) -> p b d", d=128)
            kpv = kp[:, 0:128].re("p (b d) -> p b d", d=32)
            self.LD('ck', ckv, Tl(I['cache_mla_ckv'].ap[l, j * 512:(j + 1) * 512, :].rearrange("(b p) c -> p b c", p=128), 'x'))
            self.LD('cv', kpv, Tl(I['cache_mla_kpe'].ap[l, j * 512:(j + 1) * 512, :].rearrange("(b p) c -> p b c", p=128), 'x'))
            self.mla_kv(sq, l, ckv, kpv, j * 512, 512, 128, 4)
            self.A.put(ck, kp)
        lt = self.A.get()
        self.LD('ck', lt[:, 0:64].re("p (b h) -> p b h", h=4), Tl(I['cache_fox_logf'].ap[l].rearrange("(b p) h -> p b h", p=128), 'x'))
        self.MEMSET('pool', self.ccar, 0.0)
        for j in range(PAST // 512):
            ps = self.PS.get()
            for b in range(4):
                self.TR(ps[0:4, b * 128:(b + 1) * 128], lt[:, (4 * j + b) * 4:(4 * j + b) * 4 + 4], self.ident_f)
            lf = self.A.get()
            self.CP('act', lf[0:4, :], ps[0:4, :])
            self.PS.put(ps)
            cT = self.A.get()
            self.SCAN(cT[0:4, :], self.ones_f[0:4, :], lf[0:4, :], self.ccar[0:4, 0:1], ALU.mult, ALU.add)
            self.CP('dve', self.ccar[0:4, 0:1], cT[0:4, 511:512])
            ps = self.PS.get()
            for b in range(4):
                self.TR(ps[:, b * 4:b * 4 + 4], cT[0:4, b * 128:(b + 1) * 128], self.ident_f[0:4, 0:4])
            self.CP('dve', self.ctok[:, :, 4 * j:4 * j + 4], ps[:, 0:16].re("p (b h) -> p h b", h=4))
            self.PS.put(ps)
            self.A.put(lf, cT)
        self.A.put(lt)

    def phaseB(self, l, sq):
        S = self.S
        S.barrier()
        self.carve('B')
        self.make_masks()
        NT, TB = sq.NT, sq.TB
        nblk = sq.nblk
        Ttot = sq.Ttot
        for qt in range(sq.ntiles):
            g = sq.P + qt * NT + (256 if NT == 512 else 0)
            ps = self.PS.get()
            self.MM(ps[:, 0:4], self.E0, self.ctok[:, :, g // 128])
            self.CP('dve', self.rb[:, qt, :], ps[:, 0:4])
            self.PS.put(ps)
        cnt = 0
        for ty in range(3):
            dk = (65, 96, 64)[ty]
            for h in range(4):
                kt = self.ktbuf[cnt % 2]
                vb = self.vbuf[cnt % 2]
                cnt += 1
                self.LD(f'kt{cnt % 2}', kt[0:dk, 0:Ttot], sq.KTs[ty][h])
                vv = vb[:, 0:nblk * 68].re("p (b d) -> p b d", d=68)
                nfull = Ttot // 128
                self.LD(f'vb{cnt % 2}', vv[:, 0:nfull, :], sq.Vs[ty][h][:, 0:nfull, :])
                if Ttot % 128:
                    self.LD(f'vb{cnt % 2}', vv[0:Ttot % 128, nfull, :], sq.Vs[ty][h][0:Ttot % 128, nfull, :])
                for qt in range(sq.ntiles):
                    t0 = qt * NT
                    qb = self.Bp.get()
                    self.LD(f'q{qt % 2}', qb[0:dk, :NT], sq.Qs[ty][h][:, t0:t0 + NT])
                    if sq.P == 0:
                        nkb = 4 * qt + 4
                        diag0 = 4 * qt
                    else:
                        nkb = nblk
                        diag0 = nblk - 1
                    if ty == 0:
                        fb = self.fb[qt % 2]
                        self.TSC('dve', fb[:, 0:nkb], self.ctok[:, h, 0:nkb], -1.0, self.rb[:, qt, h:h + 1], ALU.mult, ALU.add)
                    acc = self.PS.get()
                    if ty == 2:
                        car = self.A.get()
                        self.MEMSET('pool', car[:, :NT], 0.0)
                    order = list(range(nkb - 1, -1, -1))
                    units = []
                    for i, kb in enumerate(order):
                        m = kb - diag0
                        mask = None
                        if m >= 0 and not (ty == 1 and sq.P > 0):
                            mask = self.masks[ty][m]
                        q0 = 128 * m if (m > 0 and sq.P == 0) else 0
                        units.append(dict(kb=kb, nk=min(128, Ttot - kb * 128), first=(i == 0), last=(i == len(order) - 1), mask=mask, q0=q0))
                    nu = len(units)

                    def s_qk(u):
                        nk, kb, mask = u['nk'], u['kb'], u['mask']
                        z = u['z'] = self.PS.get()
                        self.MM(z[0:nk, u['q0']:NT], kt[0:dk, kb * 128:kb * 128 + nk], qb[0:dk, u['q0']:NT], start=True, stop=(mask is None))
                        if mask is not None:
                            self.MM(z[0:nk, u['q0']:NT], self.ident_bf[0:nk, 0:nk], mask[0:nk, u['q0']:NT], start=False, stop=True)

                    def s_pv(u):
                        nk, kb, z = u['nk'], u['kb'], u['z']
                        p = self.Bp.get()
                        if ty == 0:
                            self.ACT(p[0:nk, u['q0']:NT], z[0:nk, u['q0']:NT], AF.Exp, bias=fb[0:nk, kb:kb + 1])
                        else:
                            self.ACT(p[0:nk, u['q0']:NT], z[0:nk, u['q0']:NT], AF.Exp, scale=96 ** -0.5)
                        self.PS.put(z)
                        self.MM(acc[0:65, u['q0']:NT], vv[0:nk, kb, 0:65], p[0:nk, u['q0']:NT], start=u['first'], stop=u['last'], sgc=True)
                        self.Bp.put(p)

                    def sb_a(u):
                        s_qk(u)
                        nk, z = u['nk'], u['z']
                        e = u['e'] = self.A.get()
                        self.ACT(e[0:nk, u['q0']:NT], z[0:nk, u['q0']:NT], AF.Exp)
                        sp = u['sp'] = self.Bp.get()
                        self.ACT(sp[0:nk, u['q0']:NT], e[0:nk, u['q0']:NT], AF.Ln, bias=1.0)

                    def sb_b(u):
                        nk, z, e, sp = u['nk'], u['z'], u['e'], u['sp']
                        cb = self.PS.get()
                        self.MM(cb[0:nk, u['q0']:NT], self.U[0:nk, 0:nk], sp[0:nk, u['q0']:NT])
                        if not u['last']:
                            ob = self.PS.get()
                            self.MM(ob[:, u['q0']:NT], self.ones_bf[0:nk, :], sp[0:nk, u['q0']:NT])
                        t1 = e
                        self.TT('dve', t1[0:nk, u['q0']:NT], z[0:nk, u['q0']:NT], car[0:nk, u['q0']:NT], ALU.subtract)
                        self.PS.put(z)
                        self.TT('dve', t1[0:nk, u['q0']:NT], t1[0:nk, u['q0']:NT], cb[0:nk, u['q0']:NT], ALU.subtract)
                        self.PS.put(cb)
                        if not u['last']:
                            self.TT('dve', car[:, u['q0']:NT], car[:, u['q0']:NT], ob[:, u['q0']:NT], ALU.add)
                            self.PS.put(ob)
                        a = u['a'] = self.Bp.get()
                        self.ACT(a[0:nk, u['q0']:NT], t1[0:nk, u['q0']:NT], AF.Exp)
                        self.A.put(e)

                    def sb_c(u):
                        nk, kb = u['nk'], u['kb']
                        self.MM(acc[0:64, u['q0']:NT], vv[0:nk, kb, 0:64], u['a'][0:nk, u['q0']:NT], start=u['first'], stop=u['last'], sgc=True)
                        self.Bp.put(u['sp'], u['a'])

                    if ty != 2:
                        for st_ in range(nu + 2):
                            if st_ < nu:
                                s_qk(units[st_])
                            if 0 <= st_ - 2 < nu:
                                s_pv(units[st_ - 2])
                    else:
                        for st_ in range(nu + 3):
                            if st_ < nu:
                                sb_a(units[st_])
                            if 0 <= st_ - 2 < nu:
                                sb_b(units[st_ - 2])
                            if 0 <= st_ - 3 < nu:
                                sb_c(units[st_ - 3])
                    self.Bp.put(qb)
                    ob_ = self.Bp.get()
                    if ty == 2:
                        self.CP('act', ob_[0:64, :NT], acc[0:64, :NT])
                        self.A.put(car)
                        self.PS.put(acc)
                    else:
                        accs = self.A.get()
                        self.CP('act', accs[0:65, :NT], acc[0:65, :NT])
                        self.PS.put(acc)
                        self.RECIP(accs[64:65, :NT], accs[64:65, :NT])
                        bc = self.PS.get()
                        self.MM(bc[0:64, :NT], self.ones_f[64:65, 0:64], accs[64:65, :NT])
                        self.TT('dve', ob_[0:64, :NT], accs[0:64, :NT], bc[0:64, :NT], ALU.mult)
                        self.PS.put(bc)
                        self.A.put(accs)
                    self.ST(f'so{qt % 2}', sq.Os[ty][h][:, t0:t0 + NT], ob_[0:64, :NT])
                    self.Bp.put(ob_)

    def tm_linear_norm_res(self, sq, inT, pieces, gain):
        NT, TB, nb = sq.NT, sq.TB, sq.nb
        ss, rstd = self.ss, self.rstd
        h0 = []
        banks = None
        for hf in range(2):
            banks = [self.PS.get() for _ in range(nb)]
            k0 = 0
            nkg = len(pieces[hf])
            for kg, (nm, nk) in enumerate(pieces[hf]):
                wv = self.wnext(nm)
                for b in range(nb):
                    for kk in range(nk):
                        x = inT[k0 + kk]
                        K = x.ap.shape[0]
                        self.MM(banks[b][0:TB, :], x[:, b * TB:(b + 1) * TB], wv[0:K, kk, :],
                                start=(kg == 0 and kk == 0), stop=(kg == nkg - 1 and kk == nk - 1))
                k0 += nk
            for b in range(nb):
                self.ACT(self.junk[:TB, 0:512], banks[b][0:TB, :], AF.Square, accum=ss[:TB, hf * 4 + b:hf * 4 + b + 1])
            for b in range(nb):
                t = self.A.get()
                self.TT('dve', t[:TB, :], banks[b][0:TB, :], gain[:TB, hf * 512:(hf + 1) * 512], ALU.mult)
                h0.append(t)
            self.PS.put(*banks)
        self.TT('dve', rstd[:TB, 0:nb], ss[:TB, 0:nb], ss[:TB, 4:4 + nb], ALU.add)
        self.TSC('dve', rstd[:TB, :nb], rstd[:TB, :nb], 1.0 / D, EPS, ALU.mult, ALU.add)
        self.ACT(rstd[:TB, :nb], rstd[:TB, :nb], AF.Sqrt)
        self.RECIP(rstd[:TB, :nb], rstd[:TB, :nb])
        for hf in range(2):
            for b in range(nb):
                xs = self.xres[:TB, b, hf * 512:(hf + 1) * 512]
                self.STT(xs, h0[hf * nb + b][:TB, :], rstd[:TB, b:b + 1], xs, ALU.mult, ALU.add)
        self.A.put(*h0)

    def phaseC(self, l, sq):
        S = self.S
        S.barrier()
        self.carve('C')
        NT, TB, nb = sq.NT, sq.TB, sq.nb
        self.wplan_reset(self.plan_C(l) * sq.ntiles)
        x_src = sq.x_in if l == 0 else sq.x_mid
        x_dst = sq.x_mid if l == 0 else sq.y
        for qt in range(sq.ntiles):
            t0 = qt * NT
            self.LD('x', self.xres[:TB, :nb, :], Tl(x_src.ap[t0:t0 + NT, :].rearrange("(b p) d -> p b d", p=TB), x_src.k))
            hT = [self.Bp.get() for _ in range(8)]
            for kc in range(8):
                self.LD(f'lh{kc % 2}', hT[kc][:, :NT], sq.HT[qt, :, kc, :])
            oT = []
            for ty in range(3):
                lst = []
                for j in range(2):
                    o = self.Bp.get()
                    self.LD(f'lo{j}', o[0:64, :NT], sq.Os[ty][2 * j][:, t0:t0 + NT])
                    self.LD(f'lo{j}', o[64:128, :NT], sq.Os[ty][2 * j + 1][:, t0:t0 + NT])
                    lst.append(o[:, :NT])
                oT.append(lst)
            lst = []
            for c in range(2):
                o = self.Bp.get()
                self.LD(f'lo{c}', o[:, :NT], sq.OB[:, c, t0:t0 + NT])
                lst.append(o[:, :NT])
            branch_in = [oT[0], lst, oT[1], oT[2]]
            merged = [self.A.get() for _ in range(8)]
            for n in range(4):
                wb = self.wnext(f'wb{n}')
                xin = branch_in[n]
                for hf in range(2):
                    gv = self.wnext(f'gate{n}{hf}')
                    for cc in range(4):
                        c = hf * 4 + cc
                        pg = self.fm_proj(gv, cc * 128, 128, hT, NT)
                        pp = self.PS.get()
                        for j, x in enumerate(xin):
                            K = x.ap.shape[0]
                            self.MM(pp[:, :NT], wb[0:K, j, c * 128:(c + 1) * 128], x, start=(j == 0), stop=(j == len(xin) - 1))
                        gt = self.A.get()
                        self.ACT(gt[:, :NT], pg[:, :NT], AF.Sigmoid)
                        self.PS.put(pg)
                        if n == 0:
                            self.TT('dve', merged[c][:, :NT], gt[:, :NT], pp[:, :NT], ALU.mult)
                        else:
                            self.TT('dve', gt[:, :NT], gt[:, :NT], pp[:, :NT], ALU.mult)
                            self.TT('pool', merged[c][:, :NT], merged[c][:, :NT], gt[:, :NT], ALU.add)
                        self.PS.put(pp)
                        self.A.put(gt)
            self._put_by_key([x.k for lst_ in branch_in for x in lst_])
            mb = []
            for c in range(8):
                b_ = self.Bp.get()
                self.CP('act', b_[:, :NT], merged[c][:, :NT])
                mb.append(b_[:, :NT])
            self.A.put(*merged)
            self._put_by_key([x.k for x in hT])
            self.tm_linear_norm_res(sq, mb, [[('wout0', 8)], [('wout1', 8)]], self.gpost[0])
            self._put_by_key([x.k for x in mb])
            self.ckc(1, x_dst, t0, sq)
            h2 = self.rmsnorm_T(sq, self.xres, D, self.gpre[:, 1, :])
            wq = self.wnext('wq')
            om = []
            for h in range(4):
                ps = self.fm_proj(wq, h * 128, 128, h2, NT)
                qb = self.Bp.get()
                self.CP('act', qb[:, :NT], ps[:, :NT])
                self.PS.put(ps)
                oacc = self.PS.get()
                dacc = self.PS.get()
                for mbk in range(2):
                    s = self.PS.get()
                    self.MM(s[:, :NT], self.MKT[:, h, mbk * 128:(mbk + 1) * 128], qb[:, :NT])
                    p = self.Bp.get()
                    self.ACT(p[:, :NT], s[:, :NT], AF.Exp, scale=128 ** -0.5)
                    self.PS.put(s)
                    self.MM(oacc[:, :NT], self.MV[:, mbk, h * 128:(h + 1) * 128], p[:, :NT], start=(mbk == 0), stop=(mbk == 1))
                    self.MM(dacc[:, :NT], self.ones_bf, p[:, :NT], start=(mbk == 0), stop=(mbk == 1))
                    self.Bp.put(p)
                rd = self.A.get()
                self.RECIP(rd[:, :NT], dacc[:, :NT])
                o = self.Bp.get()
                self.TT('dve', o[:, :NT], oacc[:, :NT], rd[:, :NT], ALU.mult)
                self.PS.put(oacc, dacc)
                self.A.put(rd)
                self.Bp.put(qb)
                om.append(o[:, :NT])
            self.Bp.put(*h2)
            self.tm_linear_norm_res(sq, om, [[('wo0', 4)], [('wo1', 4)]], self.gpost[1])
            self._put_by_key([x.k for x in om])
            self.ckc(2, x_dst, t0, sq)
            h3 = self.rmsnorm_T(sq, self.xres, D, self.gpre[:, 2, :])
            act = []
            for j in range(6):
                ncol = 512 if j < 5 else 256
                wg = self.wnext(f'wg{j}')
                wu = self.wnext(f'wu{j}')
                for cc in range(ncol // 128):
                    pg = self.fm_proj(wg, cc * 128, 128, h3, NT)
                    pu = self.fm_proj(wu, cc * 128, 128, h3, NT)
                    sg = self.A.get()
                    self.ACT(sg[:, :NT], pg[:, :NT], AF.Silu)
                    self.PS.put(pg)
                    a_ = self.Bp.get()
                    self.TT('dve', a_[:, :NT], sg[:, :NT], pu[:, :NT], ALU.mult)
                    self.PS.put(pu)
                    self.A.put(sg)
                    act.append(a_[:, :NT])
            self.Bp.put(*h3)
            self.tm_linear_norm_res(sq, act, [[(f'wd{hf}{kg}', nk) for kg, nk in enumerate((8, 8, 6))] for hf in range(2)], self.gpost[2])
            self._put_by_key([x.k for x in act])
            self.ST('sx', Tl(x_dst.ap[t0:t0 + NT, :].rearrange("(b p) d -> p b d", p=TB), x_dst.k), self.xres[:TB, :nb, :])

    def ckc(self, n, x_dst, t0, sq):
        if int(os.environ.get('MK_SUBC', '999')) == n:
            self.ST('sx', Tl(x_dst.ap[t0:t0 + sq.NT, :].rearrange("(b p) d -> p b d", p=sq.TB), x_dst.k), self.xres[:sq.TB, :sq.nb, :])
            raise StopBuild()

    def _put_by_key(self, keys):
        for k in keys:
            self.Bp.put(self._bt[k])

    def mem_kv(self, l, sq):
        I, O = self.I, self.O
        if sq.P == 0:
            self.wplan_reset([('wk', self.wsrc('mem_wk', l, 0, 8, 0, 512), 128, 8, 512),
                              ('wv', self.wsrc('mem_wv', l, 0, 8, 0, 512), 128, 8, 512)])
            self.LD('x', self.xres[:, 0:2, :], Tl(I['mem_prompt'].ap.rearrange("(b p) d -> p b d", p=128), 'x'))
            mT = self.rmsnorm_T(sq, self.xres, D, self.gpre[:, 3, :], NT=256, TB=128, nb=2)
            wk = self.wnext('wk')
            for h in range(4):
                ps = self.fm_proj(wk, h * 128, 128, mT, 256)
                self.CP(self.ev(), self.MKT[:, h, :], ps[:, 0:256])
                self.PS.put(ps)
            for nm, wv_, on in (('k', wk, 'p_mem_k'), ('v', None, 'p_mem_v')):
                if wv_ is None:
                    wv_ = self.wnext('wv')
                for b in range(2):
                    ps = self.tm_proj(wv_, 0, 512, mT, b, 128)
                    t = self.A.get()
                    self.CP('act', t, ps)
                    if nm == 'v':
                        self.CP('dve', self.MV[:, b, :], ps)
                    self.PS.put(ps)
                    self.ST('smk', Tl(O[on].ap[l, b * 128:(b + 1) * 128, :], O[on].k), t)
                    self.A.put(t)
            self.Bp.put(*mT)
        else:
            for b in range(2):
                tk = self.A.get()
                tv = self.A.get()
                self.LD('ck', tk, Tl(I['cache_mem_k'].ap[l, b * 128:(b + 1) * 128, :], 'x'))
                self.LD('cv', tv, Tl(I['cache_mem_v'].ap[l, b * 128:(b + 1) * 128, :], 'x'))
                self.CP('dve', self.MV[:, b, :], tv)
                ps = self.PS.get()
                for h in range(4):
                    self.TR(ps[:, h * 128:(h + 1) * 128], tk[:, h * 128:(h + 1) * 128], self.ident_f)
                self.CP('act', self.MKT[:, :, b * 128:(b + 1) * 128], ps.re("p (h m) -> p h m", m=128))
                self.PS.put(ps)
                self.A.put(tk, tv)

    def build(self):
        S = self.S
        self.carve('A')
        stage = int(os.environ.get('MK_STAGE', '999'))
        st = [0]

        def go():
            st[0] += 1
            return st[0] <= stage
        self.setup_consts()
        try:
            self.build_body(go)
        except StopBuild:
            pass
        S.finish()
        return self.nc

    def build_body(self, go):
        S = self.S
        if go():
            self.cast_weights()
        for l in range(L):
            S.barrier()
            self.carve('A')
            if go():
                self.load_layer_params(l)
            for sq in self.seqs:
                if go():
                    self.phaseA(l, sq)
                if go():
                    self.phaseB(l, sq)
                S.barrier()
                self.carve('C')
                if go():
                    self.mem_kv(l, sq)
                if go():
                    self.phaseC(l, sq)


_CACHE = {}


def get_program(T):
    if T not in _CACHE:
        _CACHE[T] = Builder(T).build()
    return _CACHE[T]


def kernel(**inputs):
    inputs = {k: np.asarray(v) for k, v in inputs.items()}
    B, T, _ = inputs['x_prompt'].shape
    nc = get_program(T)
    f32 = lambda a: np.ascontiguousarray(a, dtype=np.float32)
    in_maps = []
    wnames = ['ln_mix_pre', 'ln_mix_post', 'w_in', 'fox_bf', 'lru_conv_w', 'lru_conv_b', 'lru_wr', 'lru_br', 'lru_wi', 'lru_bi',
              'lru_lam', 'mla_q_norm', 'mla_w_uq', 'mla_kv_norm', 'mla_w_uk', 'mla_w_uv', 'w_branch', 'w_out', 'ln_mem_pre',
              'ln_mem_post', 'mem_norm', 'mem_wq', 'mem_wk', 'mem_wv', 'mem_wo', 'ln_ffn_pre', 'ln_ffn_post', 'ffn_wg', 'ffn_wu', 'ffn_wd']
    shared = {}
    for n in wnames:
        a = inputs[n]
        if n in ('lru_wr', 'lru_wi'):
            a = a.reshape(L, 256, 64)
        elif n in ('lru_br', 'lru_bi'):
            a = a.reshape(L, 256)
        elif n == 'w_branch':
            a = a.reshape(L, 1024, 1024)
        shared[n] = f32(a)
    for c in range(B):
        m = dict(shared)
        m['x_prompt'] = f32(inputs['x_prompt'][c])
        m['x_sample'] = f32(inputs['x_sample'][c])
        for n in ('cache_fox_k', 'cache_fox_v', 'cache_sb_k', 'cache_sb_v'):
            m[n] = f32(inputs[n][:, c].reshape(L, PAST, 256))
        m['cache_fox_logf'] = f32(inputs['cache_fox_logf'][:, c])
        m['state_lru_h'] = f32(inputs['state_lru_h'][:, c])
        m['state_lru_conv'] = f32(inputs['state_lru_conv'][:, c])
        m['cache_mla_ckv'] = f32(inputs['cache_mla_ckv'][:, c])
        m['cache_mla_kpe'] = f32(inputs['cache_mla_kpe'][:, c])
        m['cache_mem_k'] = f32(inputs['cache_mem_k'][:, c].reshape(L, MEM, 512))
        m['cache_mem_v'] = f32(inputs['cache_mem_v'][:, c].reshape(L, MEM, 512))
        m['mem_prompt'] = f32(inputs['mem_prompt'][c])
        in_maps.append(m)
    res = run_bass_kernel_spmd(nc, in_maps, core_ids=list(range(B)))
    R = res.results
    global _LAST
    _LAST = R

    def gather(name, shape_tail, batch_axis):
        return np.stack([np.asarray(R[c][name], dtype=np.float32) for c in range(B)], axis=batch_axis).reshape(shape_tail)

    outs = []
    outs.append(gather('p_y', (B, T, D), 0))
    outs.append(gather('s_y', (B, TS_, D), 0))
    for pre, t in (('p', T), ('s', TS_)):
        lst = [gather(f'{pre}_fox_k', (L, B, t, 4, 64), 1), gather(f'{pre}_fox_v', (L, B, t, 4, 64), 1),
               gather(f'{pre}_fox_logf', (L, B, t, 4), 1), gather(f'{pre}_lru_h', (L, B, 256), 1),
               gather(f'{pre}_lru_conv', (L, B, 3, 256), 1), gather(f'{pre}_mla_ckv', (L, B, t, 128), 1),
               gather(f'{pre}_mla_kpe', (L, B, t, 32), 1), gather(f'{pre}_sb_k', (L, B, t, 4, 64), 1),
               gather(f'{pre}_sb_v', (L, B, t, 4, 64), 1)]
        if pre == 'p':
            lst += [gather('p_mem_k', (L, B, MEM, 4, 128), 1), gather('p_mem_v', (L, B, MEM, 4, 128), 1)]
        outs += lst
    return tuple(outs)
```

```python
import math
import os
import traceback
from collections import deque
import numpy as np
import concourse.bass as bass
import concourse.mybir as mybir
from concourse.bass_utils import run_bass_kernel_spmd

F32 = mybir.dt.float32
BF16 = mybir.dt.bfloat16
I32 = mybir.dt.int32
AF = mybir.ActivationFunctionType
ALU = mybir.AluOpType

D = 1024
L = 2
DFF = 2816
DIN = 6564
PAST = 2048
TS_ = 32
MEM = 256
EPS = 1e-6
NEG = -30000.0
NW = 5


DEBUG = bool(os.environ.get('MK_DEBUG'))


def _site():
    return [f"{f.name}:{f.lineno}" for f in traceback.extract_stack(limit=6)[:-2]]


class Sched:
    ENG = ('pe', 'act', 'dve', 'pool', 'sp')

    def __init__(self, nc):
        self.nc = nc
        self.q = {e: [] for e in self.ENG}
        self.sem = {}
        self.cnt = {}
        for e in ('pe', 'act', 'dve', 'pool'):
            self.sem[e] = nc.alloc_semaphore(name=f"c_{e}")
            self.cnt[e] = 0
        self.seen = {e: {} for e in self.ENG}
        self.res = {}
        self.nops = 0

    def chan(self, name):
        if name not in self.sem:
            self.sem[name] = self.nc.alloc_semaphore(name=f"d_{name}")
            self.cnt[name] = 0
        return name

    def _need(self, eng, toks, waits):
        for t in toks:
            if t is None:
                continue
            s, v = t
            if eng == 'pe' and s == 'pe':
                continue
            if self.seen[eng].get(s, 0) >= v:
                continue
            if waits.get(s, 0) < v:
                waits[s] = v

    def _deps(self, eng, reads, writes):
        waits = {}
        for k in reads:
            r = self.res.get(k)
            if r:
                self._need(eng, [r[0]], waits)
        for k in writes:
            r = self.res.get(k)
            if r:
                self._need(eng, [r[0]] + r[1], waits)
        for s, v in waits.items():
            self.seen[eng][s] = v
        return [(self.sem[s], v) for s, v in waits.items()]

    def _mark(self, tok, reads, writes):
        for k in reads:
            r = self.res.setdefault(k, [None, []])
            r[1].append(tok)
            if len(r[1]) > 16:
                m = {}
                for s, v in r[1]:
                    if m.get(s, 0) < v:
                        m[s] = v
                r[1] = list(m.items())
        for k in writes:
            self.res[k] = [tok, []]

    def op(self, eng, fn, reads=(), writes=()):
        pr = [k for k in reads if k[:2] in ('ps', 'pT')]
        if pr and eng != 'pe':
            writes = list(writes) + pr
            reads = [k for k in reads if k not in pr]
        waits = self._deps(eng, reads, writes)
        self.cnt[eng] += 1
        tok = (eng, self.cnt[eng])
        sem = self.sem[eng]

        site = _site() if DEBUG else None

        def emit(h, waits=waits, fn=fn, sem=sem, site=site):
            for s, v in waits:
                h.wait_ge(s, v)
            try:
                fn(h).then_inc(sem, 1)
            except Exception:
                print('FAILED OP SITE:', site)
                raise
        self.q[eng].append(emit)
        self._mark(tok, reads, writes)
        self.nops += 1
        return tok

    def dma(self, queue, chan, pairs, reads=(), writes=(), **kw):
        self.chan(chan)
        waits = self._deps(queue, reads, writes)
        prev = self.cnt[chan]
        w2 = {}
        if prev > 0:
            self._need(queue, [(chan, prev)], w2)
        for s, v in w2.items():
            self.seen[queue][s] = v
        waits = waits + [(self.sem[s], v) for s, v in w2.items()]
        self.cnt[chan] += 16 * len(pairs)
        tok = (chan, self.cnt[chan])
        sem = self.sem[chan]

        site = _site() if DEBUG else None

        def emit(h, waits=waits, pairs=pairs, sem=sem, kw=kw, site=site):
            for s, v in waits:
                h.wait_ge(s, v)
            try:
                for o, i in pairs:
                    h.dma_start(out=o, in_=i, **kw).then_inc(sem, 16)
            except Exception:
                print('FAILED DMA SITE:', site)
                raise
        self.q[queue].append(emit)
        self._mark(tok, reads, writes)
        self.nops += len(pairs)
        return tok

    def barrier(self):
        snap = {s: v for s, v in self.cnt.items() if v > 0}
        for e in self.ENG:
            waits = []
            for s, v in snap.items():
                if self.seen[e].get(s, 0) < v:
                    waits.append((self.sem[s], v))
                    self.seen[e][s] = v

            def emit(h, waits=waits):
                for s, v in waits:
                    h.wait_ge(s, v)
            self.q[e].append(emit)
        self.res = {}

    def finish(self):
        self.barrier()
        nc = self.nc
        q = self.q
        with nc.Block() as block:
            @block.tensor
            def _(h):
                for f in q['pe']:
                    f(h)

            @block.scalar
            def _(h):
                for f in q['act']:
                    f(h)

            @block.vector
            def _(h):
                for f in q['dve']:
                    f(h)

            @block.gpsimd
            def _(h):
                for f in q['pool']:
                    f(h)

            @block.sync
            def _(h):
                for f in q['sp']:
                    f(h)


class Tl:
    __slots__ = ('ap', 'k')

    def __init__(self, ap, k):
        self.ap = ap
        self.k = k

    def __getitem__(self, idx):
        return Tl(self.ap[idx], self.k)

    def re(self, pat_, **kw):
        return Tl(self.ap.rearrange(pat_, **kw), self.k)

    def bc(self, shape):
        return Tl(self.ap.broadcast_to(shape), self.k)

    def us(self, ax):
        return Tl(self.ap.unsqueeze(ax), self.k)


def _ap(x):
    return x.ap if isinstance(x, Tl) else x


def _keys(*xs):
    return [x.k for x in xs if isinstance(x, Tl)]


class TilePool:
    def __init__(self, tiles):
        self.free = deque(tiles)

    def get(self):
        return self.free.popleft()

    def put(self, *ts):
        for t in ts:
            self.free.append(t)


class Seq:
    pass


class StopBuild(Exception):
    pass


class Builder:
    def __init__(self, T):
        self.T = T
        self.nc = nc = bass.Bass("TRN2", target_bir_lowering=False)
        self.S = Sched(nc)
        self._rr = 0
        self.declare_dram()
        self.alloc_sbuf()

    def din(self, name, shape):
        return Tl(self.nc.dram_tensor(name, list(shape), F32, kind="ExternalInput").ap(), 'in_' + name)

    def dout(self, name, shape):
        return Tl(self.nc.dram_tensor(name, list(shape), F32, kind="ExternalOutput").ap(), 'out_' + name)

    def dscr(self, name, shape, dt):
        kind = "ExternalOutput" if (os.environ.get('MK_DBGOUT') and not name.startswith('b_')) else "Internal"
        return Tl(self.nc.dram_tensor(name, list(shape), dt, kind=kind).ap(), 'scr_' + name)

    def declare_dram(self):
        T = self.T
        I = self.I = {}
        O = self.O = {}
        I['x_prompt'] = self.din('x_prompt', [T, D])
        I['x_sample'] = self.din('x_sample', [TS_, D])
        for n in ('cache_fox_k', 'cache_fox_v', 'cache_sb_k', 'cache_sb_v'):
            I[n] = self.din(n, [L, PAST, 256])
        I['cache_fox_logf'] = self.din('cache_fox_logf', [L, PAST, 4])
        I['state_lru_h'] = self.din('state_lru_h', [L, 256])
        I['state_lru_conv'] = self.din('state_lru_conv', [L, 3, 256])
        I['cache_mla_ckv'] = self.din('cache_mla_ckv', [L, PAST, 128])
        I['cache_mla_kpe'] = self.din('cache_mla_kpe', [L, PAST, 32])
        I['cache_mem_k'] = self.din('cache_mem_k', [L, MEM, 512])
        I['cache_mem_v'] = self.din('cache_mem_v', [L, MEM, 512])
        I['mem_prompt'] = self.din('mem_prompt', [MEM, D])
        wshapes = dict(
            ln_mix_pre=[L, D], ln_mix_post=[L, D], w_in=[L, D, DIN], fox_bf=[L, 4], lru_conv_w=[L, 4, 256],
            lru_conv_b=[L, 256], lru_wr=[L, 256, 64], lru_br=[L, 256], lru_wi=[L, 256, 64], lru_bi=[L, 256],
            lru_lam=[L, 256], mla_q_norm=[L, 256], mla_w_uq=[L, 256, 384], mla_kv_norm=[L, 128],
            mla_w_uk=[L, 128, 256], mla_w_uv=[L, 128, 256], w_branch=[L, 1024, 1024], w_out=[L, D, D],
            ln_mem_pre=[L, D], ln_mem_post=[L, D], mem_norm=[L, D], mem_wq=[L, D, 512], mem_wk=[L, D, 512],
            mem_wv=[L, D, 512], mem_wo=[L, 512, D], ln_ffn_pre=[L, D], ln_ffn_post=[L, D],
            ffn_wg=[L, D, DFF], ffn_wu=[L, D, DFF], ffn_wd=[L, DFF, D])
        self.wshapes = wshapes
        for n, s in wshapes.items():
            I[n] = self.din(n, s)
        for pre, t in (('p', T), ('s', TS_)):
            O[pre + '_y'] = self.dout(pre + '_y', [t, D])
            for n in ('fox_k', 'fox_v', 'sb_k', 'sb_v'):
                O[f'{pre}_{n}'] = self.dout(f'{pre}_{n}', [L, t, 256])
            O[pre + '_fox_logf'] = self.dout(pre + '_fox_logf', [L, t, 4])
            O[pre + '_lru_h'] = self.dout(pre + '_lru_h', [L, 256])
            O[pre + '_lru_conv'] = self.dout(pre + '_lru_conv', [L, 3, 256])
            O[pre + '_mla_ckv'] = self.dout(pre + '_mla_ckv', [L, t, 128])
            O[pre + '_mla_kpe'] = self.dout(pre + '_mla_kpe', [L, t, 32])
        O['p_mem_k'] = self.dout('p_mem_k', [L, MEM, 512])
        O['p_mem_v'] = self.dout('p_mem_v', [L, MEM, 512])
        self.big = ['w_in', 'w_branch', 'w_out', 'mem_wq', 'mem_wk', 'mem_wv', 'mem_wo', 'ffn_wg', 'ffn_wu', 'ffn_wd']
        self.Wb = {n: self.dscr('b_' + n, wshapes[n], BF16) for n in self.big}
        self.seqs = []
        for name, t, p in (('p', T, 0), ('s', TS_, PAST)):
            sq = Seq()
            sq.name = name
            sq.T = t
            sq.P = p
            sq.NT = min(512, t)
            sq.TB = min(128, t)
            sq.nb = sq.NT // sq.TB
            sq.ntiles = t // sq.NT
            sq.Ttot = p + t
            sq.nblk = (sq.Ttot + 127) // 128
            sq.x_in = I['x_prompt'] if name == 'p' else I['x_sample']
            sq.x_mid = self.dscr(name + '_xmid', [t, D], F32)
            sq.y = O[name + '_y']
            sq.HT = self.dscr(name + '_HT', [sq.ntiles, 128, 8, sq.NT], BF16)
            sq.Qs = [[self.dscr(f'{name}_Q{ty}{h}', [(65, 96, 64)[ty], t], BF16) for h in range(4)] for ty in range(3)]
            sq.KTs = [[self.dscr(f'{name}_K{ty}{h}', [(65, 96, 64)[ty], sq.Ttot], BF16) for h in range(4)] for ty in range(3)]
            sq.Vs = [[self.dscr(f'{name}_V{ty}{h}', [128, sq.nblk, 68], BF16) for h in range(4)] for ty in range(3)]
            sq.Os = [[self.dscr(f'{name}_O{ty}{h}', [64, t], BF16) for h in range(4)] for ty in range(3)]
            sq.OB = self.dscr(name + '_OB', [128, 2, t], BF16)
            self.seqs.append(sq)

    def sb(self, name, shape, dt):
        return Tl(self.nc.alloc_sbuf_tensor(name, list(shape), dt).ap(), name)

    def alloc_sbuf(self):
        nc = self.nc
        self.wslots = [self.sb(f'wslot{i}', [128, 4096], BF16) for i in range(NW)]
        NA, NB = 16, 24
        self.A = TilePool([self.sb(f'A{i}', [128, 512], F32) for i in range(NA)])
        self.Bp = TilePool([self.sb(f'B{i}', [128, 512], BF16) for i in range(NB)])
        self.gpost = [self.sb(f'gpost{i}', [128, D], F32) for i in range(3)]
        self.ident_bf = self.sb('ident_bf', [128, 128], BF16)
        self.ident_f = self.sb('ident_f', [128, 128], F32)
        self.E0 = self.sb('E0', [128, 128], F32)
        self.U = self.sb('U', [128, 128], BF16)
        self.ones_bf = self.sb('ones_bf', [128, 128], BF16)
        self.ones_f = self.sb('ones_f', [128, 512], F32)
        self.ones_row = self.sb('ones_row', [1, 512], BF16)
        self.ctok = self.sb('ctok', [128, 4, 68], F32)
        self.rb = self.sb('rb', [128, 16, 4], F32)
        self.fb = [self.sb(f'fb{i}', [128, 68], F32) for i in range(2)]
        self.gpre = self.sb('gpre', [128, 4, 8], F32)
        self.qng = self.sb('qng', [128, 2], F32)
        self.kvg = self.sb('kvg', [128, 128], F32)
        self.nbf = self.sb('nbf', [4, 1], F32)
        self.cw = self.sb('cw', [128, 2, 4], F32)
        self.cbias = self.sb('cbias', [128, 2], F32)
        self.br = self.sb('br', [128, 2], F32)
        self.bi = self.sb('bi', [128, 2], F32)
        self.sl = self.sb('sl', [128, 2], F32)
        self.wr_bd = self.sb('wr_bd', [128, 2, 128], BF16)
        self.wi_bd = self.sb('wi_bd', [128, 2, 128], BF16)
        self.wuq = self.sb('wuq', [128, 2, 384], BF16)
        self.wuq_rot = self.sb('wuq_rot', [128, 2, 4, 96], BF16)
        self.wuk = self.sb('wuk', [128, 256], BF16)
        self.wuv = self.sb('wuv', [128, 256], BF16)
        self.MKT = self.sb('MKT', [128, 4, 256], BF16)
        self.MV = self.sb('MV', [128, 2, 512], BF16)
        self.ss = self.sb('ss', [128, 8], F32)
        self.rstd = self.sb('rstd', [128, 8], F32)
        self.hcar = self.sb('hcar', [128, 2], F32)
        self.ccar = self.sb('ccar', [4, 1], F32)
        self.invp = self.sb('invp', [128, 1], F32)
        self.invr = self.sb('invr', [128, 16], F32)
        RSZ = 31 * 1024
        self.R = self.sb('R', [128, RSZ], BF16)
        self.RSZ = RSZ
        banks = [Tl(nc.alloc_psum_tensor(f'ps{i}', [128, 512], F32).ap(), f'ps{i}') for i in range(8)]
        self.PS = TilePool(banks[0:6])
        self.ps_extra = banks[6:8]
        self.pT = [Tl(banks[6].ap.bitcast(BF16)[:, 0:512], 'pT0'), Tl(banks[7].ap.bitcast(BF16)[:, 0:512], 'pT1')]
        self._pTi = 0

    def carve(self, phase):
        R = self.R.ap
        off = [0]

        def take(n_bf16, key, dt=BF16):
            a = R[:, off[0]:off[0] + n_bf16]
            off[0] += n_bf16
            if dt is not BF16:
                a = a.bitcast(dt)
            return Tl(a, key)
        base = [t for t in self.Bp.free if not t.k.startswith('BX')]
        assert len(base) == 24, len(base)
        assert len(self.A.free) == 16, len(self.A.free)
        psb = [t for t in self.PS.free if t.k not in ('ps6', 'ps7')]
        assert len(psb) == 6, len(psb)
        self.PS.free = deque(psb + (self.ps_extra if phase == 'B' else []))
        self.Bp.free = deque(base)
        if phase in ('A', 'C'):
            self.xres = take(4 * D * 2, 'xres', F32).re("p (b d) -> p b d", d=D)
            self.xn = take(4 * D, 'xn')
            self.junk = take(D, 'junk')
            self.junk2 = take(D, 'junk2')
            if phase == 'A':
                self.tm4 = take(4 * 416 * 2, 'tm4', F32).re("p (b d) -> p b d", d=416)
                self.xpad = take(2 * 516 * 2, 'xpad', F32).re("p (c t) -> p c t", t=516)
                self.vaug = [take(4 * 4 * 68, f'vaug{i}') for i in range(2)]
                self.kpad = take(4 * 96, 'kpad').re("p (b d) -> p b d", d=96)
                self.ropeT = take(2 * 512 * 2, 'ropeT', F32).re("p (s t) -> p s t", t=512)
                self.ropeK = take(2 * 64 * 2, 'ropeK', F32).re("p (s b j) -> p s b j", s=2, j=16)
                self.ropetmp = take(2 * 512 * 2, 'ropetmp', F32).re("p (s t) -> p s t", t=512)
                self.ropei = take(2 * 512 * 2, 'ropei', I32).re("p (s t) -> p s t", t=512)
            i = 0
            while off[0] + 512 <= self.RSZ:
                self.Bp.put(take(512, f'BX{i}'))
                i += 1
        elif phase == 'B':
            self.ktbuf = [take(8192, f'ktbuf{i}') for i in range(2)]
            self.vbuf = [take(65 * 68, f'vbuf{i}') for i in range(2)]
            self.masks = [[take(512, f'mask{ty}{m}') for m in range(4)] for ty in range(3)]
        self._bt = {t.k: t for t in self.Bp.free}

    def MM(self, out, lhsT, rhs, start=True, stop=True, sgc=False):
        o, a, b = _ap(out), _ap(lhsT), _ap(rhs)
        if sgc:
            self.S.op('pe', lambda h: h.matmul(o, a, b, start=start, stop=stop, skip_group_check=True), reads=_keys(lhsT, rhs), writes=_keys(out))
        else:
            self.S.op('pe', lambda h: h.matmul(o, a, b, start=start, stop=stop), reads=_keys(lhsT, rhs), writes=_keys(out))

    def TR(self, out, in_, ident):
        o, a, b = _ap(out), _ap(in_), _ap(ident)
        self.S.op('pe', lambda h: h.transpose(o, a, b), reads=_keys(in_, ident), writes=_keys(out))

    def ACT(self, out, in_, func, bias=None, scale=None, accum=None):
        kw = {}
        if bias is not None:
            kw['bias'] = _ap(bias)
        if scale is not None:
            kw['scale'] = _ap(scale)
        if accum is not None:
            kw['accum_out'] = _ap(accum)
        o, a = _ap(out), _ap(in_)
        self.S.op('act', lambda h: h.activation(out=o, in_=a, func=func, **kw),
                  reads=_keys(in_, bias, scale), writes=_keys(out, accum))

    def TSC(self, eng, out, in0, s1, s2, op0, op1=None):
        o, a, x1, x2 = _ap(out), _ap(in0), _ap(s1), _ap(s2)
        if op1 is None:
            self.S.op(eng, lambda h: h.tensor_scalar(o, a, x1, None, op0), reads=_keys(in0, s1), writes=_keys(out))
        else:
            self.S.op(eng, lambda h: h.tensor_scalar(o, a, x1, x2, op0, op1), reads=_keys(in0, s1, s2), writes=_keys(out))

    def TT(self, eng, out, in0, in1, op):
        o, a, b = _ap(out), _ap(in0), _ap(in1)
        self.S.op(eng, lambda h: h.tensor_tensor(out=o, in0=a, in1=b, op=op), reads=_keys(in0, in1), writes=_keys(out))

    def STT(self, out, in0, scalar, in1, op0, op1):
        o, a, s, b = _ap(out), _ap(in0), _ap(scalar), _ap(in1)
        self.S.op('dve', lambda h: h.scalar_tensor_tensor(out=o, in0=a, scalar=s, in1=b, op0=op0, op1=op1),
                  reads=_keys(in0, scalar, in1), writes=_keys(out))

    def CP(self, eng, out, in_):
        o, a = _ap(out), _ap(in_)
        if eng == 'act':
            self.S.op('act', lambda h: h.copy(o, a), reads=_keys(in_), writes=_keys(out))
        else:
            self.S.op(eng, lambda h: h.tensor_copy(o, a), reads=_keys(in_), writes=_keys(out))

    def SCAN(self, out, d0, d1, init, op0, op1):
        o, a, b, i = _ap(out), _ap(d0), _ap(d1), _ap(init)
        self.S.op('dve', lambda h: h.tensor_tensor_scan(out=o, data0=a, data1=b, initial=i, op0=op0, op1=op1),
                  reads=_keys(d0, d1, init), writes=_keys(out))

    def RECIP(self, out, in_):
        o, a = _ap(out), _ap(in_)
        self.S.op('dve', lambda h: h.reciprocal(o, a), reads=_keys(in_), writes=_keys(out))

    def MEMSET(self, eng, out, val):
        o = _ap(out)
        self.S.op(eng, lambda h: h.memset(o, val), writes=_keys(out))

    def ASEL(self, out, in_, pattern, cmp, fill, base, cm):
        o, a = _ap(out), _ap(in_)
        regs = self.__dict__.setdefault('_fillregs', {})

        def fn(h):
            if fill not in regs:
                regs[fill] = h.to_reg(fill)
            return h.affine_select(out=o, in_=a, pattern=pattern, compare_op=cmp, fill=regs[fill], base=base, channel_multiplier=cm)
        self.S.op('pool', fn, reads=_keys(in_), writes=_keys(out))

    def IOTA(self, out, pattern, base, cm):
        o = _ap(out)
        self.S.op('pool', lambda h: h.iota(o, pattern, base=base, channel_multiplier=cm), writes=_keys(out))

    def LD(self, chan, out, in_, **kw):
        self.S.dma('sp', chan, [(_ap(out), _ap(in_))], reads=_keys(in_), writes=_keys(out), **kw)

    def ST(self, chan, out, in_, **kw):
        self.S.dma('pool', chan, [(_ap(out), _ap(in_))], reads=_keys(in_), writes=_keys(out), **kw)

    def ck(self, n):
        if int(os.environ.get('MK_SUB', '999')) == n:
            raise StopBuild()

    def ev(self):
        self._rr ^= 1
        return 'act' if self._rr else 'dve'

    def next_pT(self):
        self._pTi ^= 1
        return self.pT[self._pTi]

    def wplan_reset(self, plan):
        self.wplan = plan
        self.wissued = 0
        self.wused = 0

    def _wissue(self, i):
        name, src, npart, nk, ncol = self.wplan[i]
        slot = self.wslots[i % NW]
        view = slot[:npart, 0:nk * ncol].re("p (k c) -> p k c", c=ncol)
        self.LD(f'w{i % NW}', view, src)

    def wnext(self, expect):
        i = self.wused
        assert self.wplan[i][0] == expect, (self.wplan[i][0], expect)
        while self.wissued < min(len(self.wplan), i + NW - 2):
            self._wissue(self.wissued)
            self.wissued += 1
        self.wused += 1
        name, src, npart, nk, ncol = self.wplan[i]
        return self.wslots[i % NW][:npart, 0:nk * ncol].re("p (k c) -> p k c", c=ncol)

    def wsrc(self, name, l, r0, nk, c0, ncol, p=128):
        w = self.Wb[name]
        return Tl(w.ap[l, r0:r0 + nk * p, c0:c0 + ncol].rearrange("(k q) c -> q k c", q=p), w.k)

    def plan_A(self, l):
        pl = []
        for nm, c0, ncol in (('in1', 0, 512), ('in2', 512, 260), ('in3', 772, 512), ('in4', 1284, 416),
                             ('in5', 1700, 512), ('in6', 2212, 256)):
            pl.append((nm, self.wsrc('w_in', l, 0, 8, c0, ncol), 128, 8, ncol))
        return pl

    def plan_C(self, l):
        pl = []
        for n in range(4):
            pl.append((f'wb{n}', self.wsrc('w_branch', l, n * 256, 2, 0, 1024), 128, 2, 1024))
            for hf in range(2):
                pl.append((f'gate{n}{hf}', self.wsrc('w_in', l, 0, 8, 2468 + n * 1024 + hf * 512, 512), 128, 8, 512))
        for hf in range(2):
            pl.append((f'wout{hf}', self.wsrc('w_out', l, 0, 8, hf * 512, 512), 128, 8, 512))
        pl.append(('wq', self.wsrc('mem_wq', l, 0, 8, 0, 512), 128, 8, 512))
        for hf in range(2):
            pl.append((f'wo{hf}', self.wsrc('mem_wo', l, 0, 4, hf * 512, 512), 128, 4, 512))
        for j in range(6):
            ncol = 512 if j < 5 else 256
            pl.append((f'wg{j}', self.wsrc('ffn_wg', l, 0, 8, j * 512, ncol), 128, 8, ncol))
            pl.append((f'wu{j}', self.wsrc('ffn_wu', l, 0, 8, j * 512, ncol), 128, 8, ncol))
        for hf in range(2):
            for kg, nk in enumerate((8, 8, 6)):
                pl.append((f'wd{hf}{kg}', self.wsrc('ffn_wd', l, kg * 1024, nk, hf * 512, 512), 128, nk, 512))
        return pl

    def setup_consts(self):
        z = self.A.get()
        self.MEMSET('pool', z, 0.0)
        self.MEMSET('pool', self.ones_f, 1.0)
        self.MEMSET('pool', self.ones_bf, 1.0)
        self.MEMSET('pool', self.ones_row, 1.0)
        self.MEMSET('pool', self.ctok, 0.0)
        self.MEMSET('pool', self.ident_f, 0.0)
        self.ASEL(self.ident_f, self.ident_f, [[-1, 128]], ALU.not_equal, 1.0, 0, 1)
        self.CP('dve', self.ident_bf, self.ident_f)
        self.MEMSET('pool', self.E0, 0.0)
        self.ASEL(self.E0, self.E0, [[0, 128]], ALU.not_equal, 1.0, 0, 1)
        self.ASEL(self.U, self.ones_bf, [[-1, 128]], ALU.is_ge, 0.0, 0, 1)
        ti = self.A.get()
        tiv = Tl(ti.ap.bitcast(I32), ti.k)
        self.IOTA(tiv[:, 0:1], [[0, 1]], 0, 1)
        self.S.op('dve', lambda h: h.tensor_single_scalar(out=tiv.ap[:, 0:1], in_=tiv.ap[:, 0:1], scalar=15, op=ALU.bitwise_and),
                  reads=[tiv.k], writes=[tiv.k])
        tf = self.A.get()
        self.CP('dve', tf[:, 0:1], tiv[:, 0:1])
        self.ACT(self.invp, tf[:, 0:1], AF.Exp, scale=-math.log(10000.0) / 16.0)
        self.IOTA(tiv[:, 16:32], [[1, 16]], 0, 0)
        self.CP('dve', tf[:, 16:32], tiv[:, 16:32])
        self.ACT(self.invr, tf[:, 16:32], AF.Exp, scale=-math.log(10000.0) / 16.0)
        self.A.put(z, ti, tf)

    def make_masks(self):
        zb = self.Bp.get()
        sm = self.Bp.get()
        self.MEMSET('pool', zb, 0.0)
        for m in range(4):
            self.ASEL(self.masks[0][m], zb, [[1, 512]], ALU.is_ge, NEG, -128 * m, -1)
            self.ASEL(sm[:, 8 * m:8 * m + 8], zb[:, 0:8], [[64, 8]], ALU.is_ge, NEG, 63 - 128 * m, -1)
            self.CP('dve', self.masks[1][m].re("p (a b) -> p a b", b=64), sm[:, 8 * m:8 * m + 8].us(2).bc([128, 8, 64]))
            self.ASEL(self.masks[2][m], zb, [[1, 512]], ALU.is_ge, NEG, -128 * m - 1, -1)
        self.Bp.put(zb, sm)

    def cast_weights(self):
        i = 0
        for n in self.big:
            src = self.I[n]
            dst = self.Wb[n]
            shp = self.wshapes[n]
            tot = int(np.prod(shp))
            per = tot // 128
            dims = " ".join("abc"[:len(shp)])
            sv = src.ap.rearrange(f"{dims} -> ({dims})").rearrange("(p x) -> p x", p=128)
            dv = dst.ap.rearrange(f"{dims} -> ({dims})").rearrange("(p x) -> p x", p=128)
            CH = 2048
            for x0 in range(0, per, CH):
                w = min(CH, per - x0)
                for y0 in range(x0, x0 + w, 512):
                    ww = min(512, x0 + w - y0)
                    a = self.A.get()
                    b = self.Bp.get()
                    self.LD(f'cv{i % 4}', a[:, :ww], Tl(sv[:, y0:y0 + ww], src.k))
                    self.CP(('act', 'dve', 'pool')[i % 3] if i % 3 != 2 else 'dve', b[:, :ww], a[:, :ww])
                    self.S.dma('pool', f'cs{i % 4}', [(dv[:, y0:y0 + ww], b.ap[:, :ww])], reads=[b.k], writes=[])
                    self.A.put(a)
                    self.Bp.put(b)
                    i += 1
        self.S.barrier()

    def load_layer_params(self, l):
        I = self.I
        ld = lambda out, in_, **kw: self.LD('par', out, in_, **kw)
        for j, n in enumerate(('ln_mix_pre', 'ln_mem_pre', 'ln_ffn_pre', 'mem_norm')):
            ld(self.gpre[:, j, :], Tl(I[n].ap[l].rearrange("(k p) -> p k", p=128), I[n].k), allow_slow_non_contiguous=True)
        for j, n in enumerate(('ln_mix_post', 'ln_mem_post', 'ln_ffn_post')):
            ld(self.gpost[j], Tl(I[n].ap[l].partition_broadcast(128), I[n].k))
        ld(self.qng, Tl(I['mla_q_norm'].ap[l].rearrange("(k p) -> p k", p=128), 'x'), allow_slow_non_contiguous=True)
        ld(self.kvg, Tl(I['mla_kv_norm'].ap[l].partition_broadcast(128), 'x'))
        t = self.A.get()
        ld(t[0:4, 0:1], Tl(I['fox_bf'].ap[l].rearrange("(p o) -> p o", o=1), 'x'))
        self.TSC('dve', self.nbf, t[0:4, 0:1], -1.0, None, ALU.mult)
        for c in range(2):
            ld(self.cw[:, c, :], Tl(I['lru_conv_w'].ap[l][:, c * 128:(c + 1) * 128].rearrange("t p -> p t"), 'x'), allow_slow_non_contiguous=True)
        for dst, n in ((self.cbias, 'lru_conv_b'), (self.br, 'lru_br'), (self.bi, 'lru_bi')):
            ld(dst, Tl(I[n].ap[l].rearrange("(c p) -> p c", p=128), 'x'), allow_slow_non_contiguous=True)
        ld(t[:, 8:10], Tl(I['lru_lam'].ap[l].rearrange("(c p) -> p c", p=128), 'x'), allow_slow_non_contiguous=True)
        self.ACT(t[:, 8:10], t[:, 8:10], AF.Exp, scale=-1.0)
        self.ACT(t[:, 8:10], t[:, 8:10], AF.Ln, bias=1.0)
        self.TSC('dve', self.sl, t[:, 8:10], -8.0, None, ALU.mult)
        for dst, n in ((self.wr_bd, 'lru_wr'), (self.wi_bd, 'lru_wi')):
            ld(t[:, 64:192].re("p (c j) -> p c j", j=64), Tl(I[n].ap[l].rearrange("(c p) j -> p c j", p=128), 'x'))
            self.MEMSET('pool', dst, 0.0)
            for c in range(2):
                self.CP('dve', dst[0:64, c, 0:64], t[0:64, 64 + c * 64:128 + c * 64])
                self.CP('dve', dst[64:128, c, 64:128], t[64:128, 64 + c * 64:128 + c * 64])
        t2 = self.A.get()
        t3 = self.A.get()
        ld(t2[:, 0:384], Tl(I['mla_w_uq'].ap[l, 0:128, :], 'x'))
        ld(t3[:, 0:384], Tl(I['mla_w_uq'].ap[l, 128:256, :], 'x'))
        self.MEMSET('pool', self.wuq_rot, 0.0)
        for kc, tt in enumerate((t2, t3)):
            self.CP('dve', self.wuq[:, kc, :], tt[:, 0:384])
            tv = tt[:, 0:384].re("p (h c) -> p h c", c=96)
            self.TSC('dve', self.wuq_rot[:, kc, :, 64:80], tv[:, :, 80:96], -1.0, None, ALU.mult)
            self.CP('dve', self.wuq_rot[:, kc, :, 80:96], tv[:, :, 64:80])
        t4 = self.A.get()
        ld(t4[:, 0:256], Tl(I['mla_w_uk'].ap[l], 'x'))
        ld(t4[:, 256:512], Tl(I['mla_w_uv'].ap[l], 'x'))
        self.CP('dve', self.wuk, t4[:, 0:256])
        self.CP('dve', self.wuv, t4[:, 256:512])
        self.A.put(t, t2, t3, t4)

    def rmsnorm_T(self, sq, src, W, gain_pp, NT=None, TB=None, nb=None):
        NT = NT or sq.NT
        TB = TB or sq.TB
        nb = nb or sq.nb
        nkc = W // 128
        ss, rstd = self.ss, self.rstd
        for b in range(nb):
            if b % 2 == 0 or W != D:
                self.ACT(self.junk[:TB, :W], src[:TB, b, :], AF.Square, accum=ss[:TB, b:b + 1])
            else:
                o_, a_, acc_ = _ap(self.junk2[:TB, :W]), _ap(src[:TB, b, :]), _ap(ss[:TB, b:b + 1])
                self.S.op('dve', lambda h, o_=o_, a_=a_, acc_=acc_: h.scalar_tensor_tensor(out=o_, in0=a_, scalar=1.0, in1=a_, op0=ALU.mult, op1=ALU.mult, accum_out=acc_),
                          reads=[src.k], writes=[self.junk2.k, ss.k])
        self.TSC('dve', rstd[:TB, :nb], ss[:TB, :nb], 1.0 / W, EPS, ALU.mult, ALU.add)
        self.ACT(rstd[:TB, :nb], rstd[:TB, :nb], AF.Sqrt)
        self.RECIP(rstd[:TB, :nb], rstd[:TB, :nb])
        xn = self.xn[:, 0:nb * W].re("p (b d) -> p b d", d=W)
        for b in range(nb):
            if b % 2 == 0:
                self.TSC('dve', xn[:TB, b, :], src[:TB, b, :], rstd[:TB, b:b + 1], None, ALU.mult)
            else:
                self.ACT(xn[:TB, b, :], src[:TB, b, :], AF.Copy, scale=rstd[:TB, b:b + 1])
        return self.transpose_T(xn, W, gain_pp, NT, TB, nb)

    def transpose_T(self, xn, W, gain_pp, NT, TB, nb):
        outs = []
        for kc in range(W // 128):
            pT = self.next_pT()
            for b in range(nb):
                self.TR(pT[:, b * TB:(b + 1) * TB], xn[:TB, b, kc * 128:(kc + 1) * 128], self.ident_bf[:TB, :TB])
            o = self.Bp.get()
            if gain_pp is None:
                self.CP(self.ev(), o[:, :NT], pT[:, :NT])
            elif kc % 2 == 0:
                self.TSC('dve', o[:, :NT], pT[:, :NT], gain_pp[:, kc:kc + 1], None, ALU.mult)
            else:
                self.ACT(o[:, :NT], pT[:, :NT], AF.Copy, scale=gain_pp[:, kc:kc + 1])
            outs.append(o)
        return outs

    def fm_proj(self, wv, c0, M, hT, NT, nk=None):
        ps = self.PS.get()
        nk = nk or len(hT)
        for kc in range(nk):
            self.MM(ps[0:M, :NT], wv[:, kc, c0:c0 + M], hT[kc][:, :NT], start=(kc == 0), stop=(kc == nk - 1))
        return ps

    def tm_proj(self, wv, c0, ncol, hT, b, TB):
        ps = self.PS.get()
        nk = len(hT)
        for kc in range(nk):
            self.MM(ps[0:TB, :ncol], hT[kc][:, b * TB:(b + 1) * TB], wv[:, kc, c0:c0 + ncol], start=(kc == 0), stop=(kc == nk - 1))
        return ps

    def rope_tables(self, sq, pos0):
        NT, TB, nb = sq.NT, sq.TB, sq.nb
        tw = 2 * math.pi
        for mode in ('T', 'K'):
            if mode == 'T':
                ang = self.ropetmp
                iv = self.ropei
                self.IOTA(iv[:, 0, :NT], [[1, NT]], pos0, 0)
                self.CP('dve', ang[:, 1, :NT], iv[:, 0, :NT])
                self.TSC('dve', ang[:, 1, :NT], ang[:, 1, :NT], self.invp[:, 0:1], None, ALU.mult)
                self.TSC('dve', ang[:, 0, :NT], ang[:, 1, :NT], math.pi / 2, None, ALU.add)
                a = ang[:, :, :NT]
                ii = iv[:, :, :NT]
                dst = self.ropeT[:, :, :NT]
                tmp = self.ropeT[:, :, :NT]
            else:
                angf = self.ropetmp[:, 0, 0:2 * nb * 16].re("p (s b j) -> p s b j", s=2, j=16)
                ivf = self.ropei[:, 0, 0:2 * nb * 16].re("p (s b j) -> p s b j", s=2, j=16)
                self.IOTA(ivf[:, 1, :, 0], [[128, nb]], pos0, 1)
                self.CP('dve', angf[:, 0, :, 0], ivf[:, 1, :, 0])
                self.TT('dve', angf[:TB, 1], angf[:TB, 0, :, 0:1].bc([TB, nb, 16]), self.invr[:TB].us(1).bc([TB, nb, 16]), ALU.mult)
                self.TSC('dve', angf[:TB, 0], angf[:TB, 1], math.pi / 2, None, ALU.add)
                a = angf[:TB]
                ii = ivf[:TB]
                dst = self.ropeK[:TB, :, :nb, :]
                tmp = self.ropeK[:TB, :, :nb, :]
            self.TSC('dve', tmp, a, 1.0 / tw, None, ALU.mult)
            self.CP('dve', ii, tmp)
            self.CP('dve', tmp, ii)
            self.STT(a, tmp, -tw, a, ALU.mult, ALU.add)
            self.TSC('dve', tmp, a, math.pi, tw, ALU.is_gt, ALU.mult)
            self.TT('dve', a, a, tmp, ALU.subtract)
            self.TSC('dve', tmp, a, -math.pi, tw, ALU.is_lt, ALU.mult)
            self.TT('dve', a, a, tmp, ALU.add)
            self.TSC('dve', a, a, math.pi, -math.pi, ALU.min, ALU.max)
            self.ACT(dst, a, AF.Sin)

    def mla_kv(self, sq, l, ckvn, kper, tok0, NT, TB, nb):
        xn = self.xn[:, 0:nb * 128].re("p (b d) -> p b d", d=128)
        self.CP('dve', xn[:TB], ckvn[:TB])
        ckT = self.transpose_T(xn, 128, None, NT, TB, nb)[0]
        self.CP('dve', self.kpad[:TB, :nb, 64:96], kper[:TB])
        pT2 = self.next_pT()
        for b in range(nb):
            self.TR(pT2[0:96, b * TB:(b + 1) * TB], self.kpad[:TB, b, :], self.ident_bf[:TB, :TB])
        blk0 = tok0 // 128
        va = self.vaug[0]
        vav = va[:, 0:nb * 272].re("p (b h d) -> p b h d", h=4, d=68)
        for b in range(nb):
            ps = self.PS.get()
            self.MM(ps[0:TB, 0:256], ckT[:, b * TB:(b + 1) * TB], self.wuv)
            self.CP(self.ev(), vav[:TB, b, :, 0:64], ps[0:TB, 0:256].re("p (h d) -> p h d", d=64))
            self.PS.put(ps)
        for h in range(4):
            ps = self.PS.get()
            self.MM(ps[0:64, :NT], self.wuk[:, h * 64:(h + 1) * 64], ckT[:, :NT])
            kt = self.Bp.get()
            self.CP('act', kt[0:64, :NT], ps[0:64, :NT])
            self.CP('dve', kt[64:96, :NT], pT2[64:96, :NT])
            self.PS.put(ps)
            self.ST(f'sk{h}', sq.KTs[1][h][:, tok0:tok0 + NT], kt[0:96, :NT])
            self.Bp.put(kt)
            self.ST(f'sv{h}', sq.Vs[1][h][:TB, blk0:blk0 + nb, :], vav[:TB, :, h, :])
        self.Bp.put(ckT)

    def phaseA(self, l, sq):
        S = self.S
        S.barrier()
        self.carve('A')
        I, O = self.I, self.O
        pre = sq.name
        NT, TB, nb = sq.NT, sq.TB, sq.nb
        self.wplan_reset(self.plan_A(l) * sq.ntiles)
        for va in self.vaug:
            self.MEMSET('pool', va, 1.0)
        self.MEMSET('pool', self.kpad, 0.0)
        if sq.P == 0:
            self.MEMSET('pool', self.hcar, 0.0)
            self.MEMSET('pool', self.ccar, 0.0)
            self.MEMSET('pool', self.xpad[:, :, 0:3], 0.0)
        else:
            self.LD('par', self.hcar, Tl(I['state_lru_h'].ap[l].rearrange("(c p) -> p c", p=128), 'x'), allow_slow_non_contiguous=True)
            for c in range(2):
                self.LD('par', self.xpad[:, c, 0:3], Tl(I['state_lru_conv'].ap[l][:, c * 128:(c + 1) * 128].rearrange("j p -> p j"), 'x'),
                        allow_slow_non_contiguous=True)
            self.sample_prep(l, sq)
        x_src = sq.x_in if l == 0 else sq.x_mid

        def prologue(qt_):
            t0_ = qt_ * NT
            self.LD('x', self.xres[:TB, :nb, :], Tl(x_src.ap[t0_:t0_ + NT, :].rearrange("(b p) d -> p b d", p=TB), x_src.k))
            hT_ = self.rmsnorm_T(sq, self.xres, D, self.gpre[:, 0, :])
            for kc in range(8):
                self.ST(f'sh{kc % 2}', sq.HT[qt_, :, kc, :], hT_[kc][:, :NT])
            return hT_
        hT_next = None
        for qt in range(sq.ntiles):
            t0 = qt * NT
            g0 = sq.P + t0
            blk0 = g0 // 128
            hT = hT_next if hT_next is not None else prologue(qt)
            self.ck(1)
            self.rope_tables(sq, g0)
            self.ck(2)
            wv = self.wnext('in1')
            for ty, qcol, kcol, outn in ((0, 0, 256, 'fox_k'),):
                self.qk_proj(sq, l, wv, ty, hT, t0, g0, outn)
            self.ck(3)
            wv = self.wnext('in2')
            st = [self.A.get(), self.A.get()]
            va = self.vaug[0]
            vav = va[:, 0:nb * 272].re("p (b h d) -> p b h d", h=4, d=68)
            VV = os.environ.get('MK_VAR', '')
            for b in range(nb):
                if 'a' in VV:
                    break
                ps = self.tm_proj(wv, 0, 256, hT, b, TB)
                if 'b' not in VV:
                    self.CP('act', st[b // 2][:TB, (b % 2) * 256:(b % 2) * 256 + 256], ps[0:TB, 0:256])
                if 'c' not in VV:
                    self.CP('dve', vav[:TB, b, :, 0:64], ps[0:TB, 0:256].re("p (h d) -> p h d", d=64))
                self.PS.put(ps)
            self.ck(30)
            self.store_tm(O[f'{pre}_fox_v'], l, t0, st, 256, TB, nb)
            self.ck(301)
            for h in range(4):
                self.ST(f'sv{h}', sq.Vs[0][h][:TB, blk0:blk0 + nb, :], vav[:TB, :, h, :])
            self.A.put(*st)
            self.ck(31)
            ps = self.fm_proj(wv, 256, 4, hT, NT)
            lf = self.A.get()
            self.ACT(lf[0:4, :NT], ps[0:4, :NT], AF.Exp, bias=self.nbf[0:4, 0:1], scale=-1.0)
            self.PS.put(ps)
            self.ACT(lf[0:4, :NT], lf[0:4, :NT], AF.Ln, bias=1.0)
            self.TSC('dve', lf[0:4, :NT], lf[0:4, :NT], -1.0, None, ALU.mult)
            cT = self.A.get()
            self.SCAN(cT[0:4, :NT], self.ones_f[0:4, :NT], lf[0:4, :NT], self.ccar[0:4, 0:1], ALU.mult, ALU.add)
            self.CP('dve', self.ccar[0:4, 0:1], cT[0:4, NT - 1:NT])
            ref = 256 if NT == 512 else 0
            dqb = self.Bp.get()
            self.TSC('dve', dqb[0:4, :NT], cT[0:4, :NT], cT[0:4, ref:ref + 1], None, ALU.subtract)
            for h in range(4):
                self.ST(f'sq{h % 2}', sq.Qs[0][h][64:65, t0:t0 + NT], dqb[h:h + 1, :NT])
                self.ST(f'sk{h}', sq.KTs[0][h][64:65, g0:g0 + NT], self.ones_row[0:1, :NT])
            self.Bp.put(dqb)
            self.ck(32)
            ps = self.PS.get()
            for b in range(nb):
                self.TR(ps[0:TB, b * 4:b * 4 + 4], lf[0:4, b * TB:(b + 1) * TB], self.ident_f[0:4, 0:4])
                self.TR(ps[0:TB, 64 + b * 4:64 + b * 4 + 4], cT[0:4, b * TB:(b + 1) * TB], self.ident_f[0:4, 0:4])
            lft = self.A.get()
            self.CP('dve', lft[:TB, 0:nb * 4], ps[0:TB, 0:nb * 4])
            self.CP('dve', self.ctok[:TB, :, blk0:blk0 + nb], ps[0:TB, 64:64 + nb * 4].re("p (b h) -> p h b", h=4))
            self.PS.put(ps)
            self.ck(33)
            self.ST('slf', Tl(O[f'{pre}_fox_logf'].ap[l, t0:t0 + NT, :].rearrange("(b p) h -> p b h", p=TB), O[f'{pre}_fox_logf'].k),
                    lft[:TB, 0:nb * 4].re("p (b h) -> p b h", h=4))
            self.A.put(lf, cT, lft)
            self.ck(4)
            wv = self.wnext('in3')
            gs = self.lru_proj(sq, wv, hT)
            self.ck(5)
            wv = self.wnext('in4')
            for b in range(nb):
                ps = self.tm_proj(wv, 0, 416, hT, b, TB)
                self.CP(self.ev(), self.tm4[:TB, b, :], ps[0:TB, 0:416])
                self.PS.put(ps)
            self.mla_new(sq, l, t0, g0)
            self.ck(6)
            wv = self.wnext('in5')
            self.qk_proj(sq, l, wv, 2, hT, t0, g0, 'sb_k')
            wv = self.wnext('in6')
            st = [self.A.get(), self.A.get()]
            va = self.vaug[1]
            vav = va[:, 0:nb * 272].re("p (b h d) -> p b h d", h=4, d=68)
            for b in range(nb):
                ps = self.tm_proj(wv, 0, 256, hT, b, TB)
                self.CP('act', st[b // 2][:TB, (b % 2) * 256:(b % 2) * 256 + 256], ps[0:TB, 0:256])
                self.CP('dve', vav[:TB, b, :, 0:64], ps[0:TB, 0:256].re("p (h d) -> p h d", d=64))
                self.PS.put(ps)
            self.store_tm(O[f'{pre}_sb_v'], l, t0, st, 256, TB, nb)
            for h in range(4):
                self.ST(f'sv{h}', sq.Vs[2][h][:TB, blk0:blk0 + nb, :], vav[:TB, :, h, :])
            self.A.put(*st)
            self.Bp.put(*hT)
            hT_next = prologue(qt + 1) if qt + 1 < sq.ntiles else None
            self.lru(sq, l, gs, qt, t0)

    def store_tm(self, dst, l, t0, st, W, TB, nb):
        for j in range((nb + 1) // 2):
            nbb = min(2, nb - 2 * j)
            self.ST('stm', Tl(dst.ap[l, t0 + 2 * j * TB:t0 + (2 * j + nbb) * TB, :].rearrange("(b p) c -> p b c", p=TB), dst.k),
                    st[j][:TB, 0:nbb * W].re("p (b c) -> p b c", c=W))

    def qk_proj(self, sq, l, wv, ty, hT, t0, g0, outn):
        NT, TB, nb = sq.NT, sq.TB, sq.nb
        O = self.O
        for h in range(4):
            ps = self.fm_proj(wv, h * 64, 64, hT, NT)
            qb = self.Bp.get()
            self.TSC('dve', qb[0:64, :NT], ps[0:64, :NT], 0.125, None, ALU.mult)
            self.PS.put(ps)
            self.ST(f'sq{h % 2}', sq.Qs[ty][h][0:64, t0:t0 + NT], qb[0:64, :NT])
            self.Bp.put(qb)
            ps = self.fm_proj(wv, 256 + h * 64, 64, hT, NT)
            kb = self.Bp.get()
            self.CP('act', kb[0:64, :NT], ps[0:64, :NT])
            self.PS.put(ps)
            self.ST(f'sk{h}', sq.KTs[ty][h][0:64, g0:g0 + NT], kb[0:64, :NT])
            self.Bp.put(kb)
        st = [self.A.get(), self.A.get()]
        for b in range(nb):
            ps = self.tm_proj(wv, 256, 256, hT, b, TB)
            self.CP(self.ev(), st[b // 2][:TB, (b % 2) * 256:(b % 2) * 256 + 256], ps[0:TB, 0:256])
            self.PS.put(ps)
        self.store_tm(O[f'{sq.name}_{outn}'], l, t0, st, 256, TB, nb)
        self.A.put(*st)

    def lru_proj(self, sq, wv, hT):
        NT = sq.NT
        gs = []
        for c in range(2):
            ps = self.fm_proj(wv, c * 128, 128, hT, NT)
            self.CP('act', self.xpad[:, c, 3:3 + NT], ps[:, :NT])
            self.PS.put(ps)
            ps = self.fm_proj(wv, 256 + c * 128, 128, hT, NT)
            g = self.A.get()
            self.ACT(g[:, :NT], ps[:, :NT], AF.Gelu_apprx_tanh)
            self.PS.put(ps)
            gs.append(g)
        return gs

    def lru(self, sq, l, gs, qt, t0):
        NT = sq.NT
        O = self.O
        pre = sq.name
        last = (qt == sq.ntiles - 1)

        def chain(c):
            g = gs[c]
            xc = self.A.get()
            self.TSC('dve', xc[:, :NT], self.xpad[:, c, 0:NT], self.cw[:, c, 0:1], self.cbias[:, c:c + 1], ALU.mult, ALU.add)
            yield
            for tap in range(1, 4):
                self.STT(xc[:, :NT], self.xpad[:, c, tap:tap + NT], self.cw[:, c, tap:tap + 1], xc[:, :NT], ALU.mult, ALU.add)
                yield
            xcb = self.Bp.get()
            self.CP('act', xcb[:, :NT], xc[:, :NT])
            yield
            ps = self.PS.get()
            self.MM(ps[:, :NT], self.wr_bd[:, c, :], xcb[:, :NT])
            r = self.A.get()
            self.ACT(r[:, :NT], ps[:, :NT], AF.Sigmoid, bias=self.br[:, c:c + 1])
            self.PS.put(ps)
            yield
            ps = self.PS.get()
            self.MM(ps[:, :NT], self.wi_bd[:, c, :], xcb[:, :NT])
            ig = self.A.get()
            self.ACT(ig[:, :NT], ps[:, :NT], AF.Sigmoid, bias=self.bi[:, c:c + 1])
            self.PS.put(ps)
            self.Bp.put(xcb)
            yield
            a = self.A.get()
            self.ACT(a[:, :NT], r[:, :NT], AF.Exp, scale=self.sl[:, c:c + 1])
            yield
            self.TT('dve', r[:, :NT], a[:, :NT], a[:, :NT], ALU.mult)
            yield
            self.ACT(r[:, :NT], r[:, :NT], AF.Sqrt, bias=1.0, scale=-1.0)
            yield
            self.TT('dve', ig[:, :NT], ig[:, :NT], xc[:, :NT], ALU.mult)
            yield
            self.TT('dve', r[:, :NT], r[:, :NT], ig[:, :NT], ALU.mult)
            yield
            if sq.P == 0 and qt == 0:
                self.MEMSET('pool', a[:, 0:1], 0.0)
                self.CP('dve', r[:, 0:1], ig[:, 0:1])
            hs = xc
            self.SCAN(hs[:, :NT], a[:, :NT], r[:, :NT], self.hcar[:, c:c + 1], ALU.mult, ALU.add)
            yield
            self.CP('dve', self.hcar[:, c:c + 1], hs[:, NT - 1:NT])
            ob = self.Bp.get()
            self.TT('dve', ob[:, :NT], hs[:, :NT], g[:, :NT], ALU.mult)
            self.ST(f'sob{c}', sq.OB[:, c, t0:t0 + NT], ob[:, :NT])
            self.Bp.put(ob)
            self.A.put(g, xc, r, ig, a)

        gens = [chain(0), chain(1)]
        while gens:
            for gen in list(gens):
                try:
                    next(gen)
                except StopIteration:
                    gens.remove(gen)
        if last:
            self.ST('slh', Tl(O[pre + '_lru_h'].ap[l].rearrange("(c p) -> p c", p=128), O[pre + '_lru_h'].k), self.hcar,
                    allow_slow_non_contiguous=True)
            for c in range(2):
                self.ST('slc', Tl(O[pre + '_lru_conv'].ap[l][:, c * 128:(c + 1) * 128].rearrange("j p -> p j"), O[pre + '_lru_conv'].k),
                        self.xpad[:, c, NT:NT + 3], allow_slow_non_contiguous=True)
        else:
            t = self.A.get()
            self.CP('dve', t[:, 0:6].re("p (c j) -> p c j", j=3), self.xpad[:, :, NT:NT + 3])
            self.CP('dve', self.xpad[:, :, 0:3], t[:, 0:6].re("p (c j) -> p c j", j=3))
            self.A.put(t)

    def mla_new(self, sq, l, t0, g0):
        NT, TB, nb = sq.NT, sq.TB, sq.nb
        O = self.O
        pre = sq.name
        tm4 = self.tm4
        cqT = self.rmsnorm_T(sq, tm4[:, :, 0:256], 256, self.qng)
        for h in range(4):
            psq = self.fm_proj(self.wuq, h * 96, 96, cqT, NT)
            psr = self.PS.get()
            for kc in range(2):
                self.MM(psr[0:96, :NT], self.wuq_rot[:, kc, h, :], cqT[kc][:, :NT], start=(kc == 0), stop=(kc == 1))
            qb = self.Bp.get()
            self.CP('act', qb[0:64, :NT], psq[0:64, :NT])
            t1 = self.A.get()
            t2 = self.A.get()
            self.TT('dve', t1[64:96, :NT], psq[64:96, :NT], self.ropeT[64:96, 0, :NT], ALU.mult)
            self.TT('dve', t2[64:96, :NT], psr[64:96, :NT], self.ropeT[64:96, 1, :NT], ALU.mult)
            self.TT('dve', qb[64:96, :NT], t1[64:96, :NT], t2[64:96, :NT], ALU.add)
            self.PS.put(psq, psr)
            self.A.put(t1, t2)
            self.ST(f'sq{h % 2}', sq.Qs[1][h][:, t0:t0 + NT], qb[0:96, :NT])
            self.Bp.put(qb)
        self.Bp.put(*cqT)
        ss, rstd = self.ss, self.rstd
        for b in range(nb):
            self.ACT(self.junk[:TB, :128], tm4[:TB, b, 256:384], AF.Square, accum=ss[:TB, b:b + 1])
        self.TSC('dve', rstd[:TB, :nb], ss[:TB, :nb], 1.0 / 128, EPS, ALU.mult, ALU.add)
        self.ACT(rstd[:TB, :nb], rstd[:TB, :nb], AF.Sqrt)
        self.RECIP(rstd[:TB, :nb], rstd[:TB, :nb])
        ck = self.A.get()
        ckv = ck[:, 0:nb * 128].re("p (b d) -> p b d", d=128)
        for b in range(nb):
            self.STT(ckv[:TB, b, :], tm4[:TB, b, 256:384], rstd[:TB, b:b + 1], self.kvg[:TB, :], ALU.mult, ALU.mult)
        self.ST('sck', Tl(O[pre + '_mla_ckv'].ap[l, t0:t0 + NT, :].rearrange("(b p) c -> p b c", p=TB), O[pre + '_mla_ckv'].k), ckv[:TB])
        kp = self.A.get()
        kpv = kp[:, 0:nb * 32].re("p (b d) -> p b d", d=32)
        t1 = self.A.get()
        t1v = t1[:, 0:nb * 32].re("p (b d) -> p b d", d=32)
        cosk = self.ropeK[:TB, 0, :nb, :]
        sink = self.ropeK[:TB, 1, :nb, :]
        x1 = tm4[:TB, :nb, 384:400]
        x2 = tm4[:TB, :nb, 400:416]
        self.TT('dve', kpv[:TB, :, 0:16], x1, cosk, ALU.mult)
        self.TT('dve', t1v[:TB, :, 0:16], x2, sink, ALU.mult)
        self.TT('dve', kpv[:TB, :, 0:16], kpv[:TB, :, 0:16], t1v[:TB, :, 0:16], ALU.subtract)
        self.TT('dve', kpv[:TB, :, 16:32], x1, sink, ALU.mult)
        self.TT('dve', t1v[:TB, :, 16:32], x2, cosk, ALU.mult)
        self.TT('dve', kpv[:TB, :, 16:32], kpv[:TB, :, 16:32], t1v[:TB, :, 16:32], ALU.add)
        self.ST('skp', Tl(O[pre + '_mla_kpe'].ap[l, t0:t0 + NT, :].rearrange("(b p) c -> p b c", p=TB), O[pre + '_mla_kpe'].k), kpv[:TB])
        self.mla_kv(sq, l, ckv, kpv, g0, NT, TB, nb)
        self.A.put(ck, kp, t1)

    def sample_prep(self, l, sq):
        I = self.I
        for ty, kn, vn in ((0, 'cache_fox_k', 'cache_fox_v'), (2, 'cache_sb_k', 'cache_sb_v')):
            for j in range(PAST // 512):
                kk = [self.A.get(), self.A.get()]
                vv = [self.A.get(), self.A.get()]
                va = self.vaug[j % 2]
                vav = va[:, 0:4 * 272].re("p (b h d) -> p b h d", h=4, d=68)
                for jj in range(2):
                    r0 = j * 512 + jj * 256
                    self.LD('ck', kk[jj].re("p (b c) -> p b c", c=256), Tl(I[kn].ap[l, r0:r0 + 256, :].rearrange("(b p) c -> p b c", p=128), 'x'))
                    self.LD('cv', vv[jj].re("p (b c) -> p b c", c=256), Tl(I[vn].ap[l, r0:r0 + 256, :].rearrange("(b p) c -> p b c", p=128), 'x'))
                for b in range(4):
                    self.CP(self.ev(), vav[:, b, :, 0:64], vv[b // 2][:, (b % 2) * 256:(b % 2) * 256 + 256].re("p (h d) -> p h d", d=64))
                for h in range(4):
                    self.ST(f'sv{h}', sq.Vs[ty][h][:, 4 * j:4 * j + 4, :], vav[:, :, h, :])
                    ps = self.PS.get()
                    for b in range(4):
                        self.TR(ps[0:64, b * 128:(b + 1) * 128], kk[b // 2][:, (b % 2) * 256 + h * 64:(b % 2) * 256 + h * 64 + 64], self.ident_f)
                    kb = self.Bp.get()
                    self.CP(self.ev(), kb[0:64, :], ps[0:64, :])
                    self.PS.put(ps)
                    self.ST(f'sk{h}', sq.KTs[ty][h][0:64, j * 512:(j + 1) * 512], kb[0:64, :])
                    if ty == 0:
                        self.ST(f'sk{h}', sq.KTs[0][h][64:65, j * 512:(j + 1) * 512], self.ones_row[0:1, :])
                    self.Bp.put(kb)
                self.A.put(*kk, *vv)
        for j in range(PAST // 512):
            ck = self.A.get()
            kp = self.A.get()
            ckv = ck.re("p (b d) -> p b d", d=128)
            kpv = kp[:, 0:128].re("p (b d) -> p b d", d=32)
            self.LD('ck', ckv, Tl(I['cache_mla_ckv'].ap[l, j * 512:(j + 1) * 512, :].rearrange("(b p) c -> p b c", p=128), 'x'))
            self.LD('cv', kpv, Tl(I['cache_mla_kpe'].ap[l, j * 512:(j + 1) * 512, :].rearrange("(b p) c -> p b c", p=128), 'x'))
            self.mla_kv(sq, l, ckv, kpv, j * 512, 512, 128, 4)
            self.A.put(ck, kp)
        lt = self.A.get()
        self.LD('ck', lt[:, 0:64].re("p (b h) -> p b h", h=4), Tl(I['cache_fox_logf'].ap[l].rearrange("(b p) h -> p b h", p=128), 'x'))
        self.MEMSET('pool', self.ccar, 0.0)
        for j in range(PAST // 512):
            ps = self.PS.get()
            for b in range(4):
                self.TR(ps[0:4, b * 128:(b + 1) * 128], lt[:, (4 * j + b) * 4:(4 * j + b) * 4 + 4], self.ident_f)
            lf = self.A.get()
            self.CP('act', lf[0:4, :], ps[0:4, :])
            self.PS.put(ps)
            cT = self.A.get()
            self.SCAN(cT[0:4, :], self.ones_f[0:4, :], lf[0:4, :], self.ccar[0:4, 0:1], ALU.mult, ALU.add)
            self.CP('dve', self.ccar[0:4, 0:1], cT[0:4, 511:512])
            ps = self.PS.get()
            for b in range(4):
                self.TR(ps[:, b * 4:b * 4 + 4], cT[0:4, b * 128:(b + 1) * 128], self.ident_f[0:4, 0:4])
            self.CP('dve', self.ctok[:, :, 4 * j:4 * j + 4], ps[:, 0:16].re("p (b h) -> p h b", h=4))
            self.PS.put(ps)
            self.A.put(lf, cT)
        self.A.put(lt)

    def phaseB(self, l, sq):
        S = self.S
        S.barrier()
        self.carve('B')
        self.make_masks()
        NT, TB = sq.NT, sq.TB
        nblk = sq.nblk
        Ttot = sq.Ttot
        for qt in range(sq.ntiles):
            g = sq.P + qt * NT + (256 if NT == 512 else 0)
            ps = self.PS.get()
            self.MM(ps[:, 0:4], self.E0, self.ctok[:, :, g // 128])
            self.CP('dve', self.rb[:, qt, :], ps[:, 0:4])
            self.PS.put(ps)
        cnt = 0
        for ty in range(3):
            dk = (65, 96, 64)[ty]
            for h in range(4):
                kt = self.ktbuf[cnt % 2]
                vb = self.vbuf[cnt % 2]
                cnt += 1
                self.LD(f'kt{cnt % 2}', kt[0:dk, 0:Ttot], sq.KTs[ty][h])
                vv = vb[:, 0:nblk * 68].re("p (b d) -> p b d", d=68)
                nfull = Ttot // 128
                self.LD(f'vb{cnt % 2}', vv[:, 0:nfull, :], sq.Vs[ty][h][:, 0:nfull, :])
                if Ttot % 128:
                    self.LD(f'vb{cnt % 2}', vv[0:Ttot % 128, nfull, :], sq.Vs[ty][h][0:Ttot % 128, nfull, :])
                for qt in range(sq.ntiles):
                    t0 = qt * NT
                    qb = self.Bp.get()
                    self.LD(f'q{qt % 2}', qb[0:dk, :NT], sq.Qs[ty][h][:, t0:t0 + NT])
                    if sq.P == 0:
                        nkb = 4 * qt + 4
                        diag0 = 4 * qt
                    else:
                        nkb = nblk
                        diag0 = nblk - 1
                    if ty == 0:
                        fb = self.fb[qt % 2]
                        self.TSC('dve', fb[:, 0:nkb], self.ctok[:, h, 0:nkb], -1.0, self.rb[:, qt, h:h + 1], ALU.mult, ALU.add)
                    acc = self.PS.get()
                    if ty == 2:
                        car = self.A.get()
                        self.MEMSET('pool', car[:, :NT], 0.0)
                    order = list(range(nkb - 1, -1, -1))
                    units = []
                    for i, kb in enumerate(order):
                        m = kb - diag0
                        mask = None
                        if m >= 0 and not (ty == 1 and sq.P > 0):
                            mask = self.masks[ty][m]
                        q0 = 128 * m if (m > 0 and sq.P == 0) else 0
                        units.append(dict(kb=kb, nk=min(128, Ttot - kb * 128), first=(i == 0), last=(i == len(order) - 1), mask=mask, q0=q0))
                    nu = len(units)

                    def s_qk(u):
                        nk, kb, mask = u['nk'], u['kb'], u['mask']
                        z = u['z'] = self.PS.get()
                        self.MM(z[0:nk, u['q0']:NT], kt[0:dk, kb * 128:kb * 128 + nk], qb[0:dk, u['q0']:NT], start=True, stop=(mask is None))
                        if mask is not None:
                            self.MM(z[0:nk, u['q0']:NT], self.ident_bf[0:nk, 0:nk], mask[0:nk, u['q0']:NT], start=False, stop=True)

                    def s_pv(u):
                        nk, kb, z = u['nk'], u['kb'], u['z']
                        p = self.Bp.get()
                        if ty == 0:
                            self.ACT(p[0:nk, u['q0']:NT], z[0:nk, u['q0']:NT], AF.Exp, bias=fb[0:nk, kb:kb + 1])
                        else:
                            self.ACT(p[0:nk, u['q0']:NT], z[0:nk, u['q0']:NT], AF.Exp, scale=96 ** -0.5)
                        self.PS.put(z)
                        self.MM(acc[0:65, u['q0']:NT], vv[0:nk, kb, 0:65], p[0:nk, u['q0']:NT], start=u['first'], stop=u['last'], sgc=True)
                        self.Bp.put(p)

                    def sb_a(u):
                        s_qk(u)
                        nk, z = u['nk'], u['z']
                        e = u['e'] = self.A.get()
                        self.ACT(e[0:nk, u['q0']:NT], z[0:nk, u['q0']:NT], AF.Exp)
                        sp = u['sp'] = self.Bp.get()
                        self.ACT(sp[0:nk, u['q0']:NT], e[0:nk, u['q0']:NT], AF.Ln, bias=1.0)

                    def sb_b(u):
                        nk, z, e, sp = u['nk'], u['z'], u['e'], u['sp']
                        cb = self.PS.get()
                        self.MM(cb[0:nk, u['q0']:NT], self.U[0:nk, 0:nk], sp[0:nk, u['q0']:NT])
                        if not u['last']:
                            ob = self.PS.get()
                            self.MM(ob[:, u['q0']:NT], self.ones_bf[0:nk, :], sp[0:nk, u['q0']:NT])
                        t1 = e
                        self.TT('dve', t1[0:nk, u['q0']:NT], z[0:nk, u['q0']:NT], car[0:nk, u['q0']:NT], ALU.subtract)
                        self.PS.put(z)
                        self.TT('dve', t1[0:nk, u['q0']:NT], t1[0:nk, u['q0']:NT], cb[0:nk, u['q0']:NT], ALU.subtract)
                        self.PS.put(cb)
                        if not u['last']:
                            self.TT('dve', car[:, u['q0']:NT], car[:, u['q0']:NT], ob[:, u['q0']:NT], ALU.add)
                            self.PS.put(ob)
                        a = u['a'] = self.Bp.get()
                        self.ACT(a[0:nk, u['q0']:NT], t1[0:nk, u['q0']:NT], AF.Exp)
                        self.A.put(e)

                    def sb_c(u):
                        nk, kb = u['nk'], u['kb']
                        self.MM(acc[0:64, u['q0']:NT], vv[0:nk, kb, 0:64], u['a'][0:nk, u['q0']:NT], start=u['first'], stop=u['last'], sgc=True)
                        self.Bp.put(u['sp'], u['a'])

                    if ty != 2:
                        for st_ in range(nu + 3):
                            if st_ < nu:
                                s_qk(units[st_])
                            if 0 <= st_ - 3 < nu:
                                s_pv(units[st_ - 3])
                    else:
                        for st_ in range(nu + 4):
                            if st_ < nu:
                                sb_a(units[st_])
                            if 0 <= st_ - 3 < nu:
                                sb_b(units[st_ - 3])
                            if 0 <= st_ - 4 < nu:
                                sb_c(units[st_ - 4])
                    self.Bp.put(qb)
                    ob_ = self.Bp.get()
                    if ty == 2:
                        self.CP('act', ob_[0:64, :NT], acc[0:64, :NT])
                        self.A.put(car)
                        self.PS.put(acc)
                    else:
                        accs = self.A.get()
                        self.CP('act', accs[0:65, :NT], acc[0:65, :NT])
                        self.PS.put(acc)
                        self.RECIP(accs[64:65, :NT], accs[64:65, :NT])
                        bc = self.PS.get()
                        self.MM(bc[0:64, :NT], self.ones_f[64:65, 0:64], accs[64:65, :NT])
                        self.TT('dve', ob_[0:64, :NT], accs[0:64, :NT], bc[0:64, :NT], ALU.mult)
                        self.PS.put(bc)
                        self.A.put(accs)
                    self.ST(f'so{qt % 2}', sq.Os[ty][h][:, t0:t0 + NT], ob_[0:64, :NT])
                    self.Bp.put(ob_)

    def tm_linear_norm_res(self, sq, inT, pieces, gain):
        NT, TB, nb = sq.NT, sq.TB, sq.nb
        ss, rstd = self.ss, self.rstd
        h0 = []
        banks = None
        for hf in range(2):
            banks = [self.PS.get() for _ in range(nb)]
            k0 = 0
            nkg = len(pieces[hf])
            for kg, (nm, nk) in enumerate(pieces[hf]):
                wv = self.wnext(nm)
                for b in range(nb):
                    for kk in range(nk):
                        x = inT[k0 + kk]
                        K = x.ap.shape[0]
                        self.MM(banks[b][0:TB, :], x[:, b * TB:(b + 1) * TB], wv[0:K, kk, :],
                                start=(kg == 0 and kk == 0), stop=(kg == nkg - 1 and kk == nk - 1))
                k0 += nk
            for b in range(nb):
                self.ACT(self.junk[:TB, 0:512], banks[b][0:TB, :], AF.Square, accum=ss[:TB, hf * 4 + b:hf * 4 + b + 1])
            for b in range(nb):
                t = self.A.get()
                self.TT('dve', t[:TB, :], banks[b][0:TB, :], gain[:TB, hf * 512:(hf + 1) * 512], ALU.mult)
                h0.append(t)
            self.PS.put(*banks)
        self.TT('dve', rstd[:TB, 0:nb], ss[:TB, 0:nb], ss[:TB, 4:4 + nb], ALU.add)
        self.TSC('dve', rstd[:TB, :nb], rstd[:TB, :nb], 1.0 / D, EPS, ALU.mult, ALU.add)
        self.ACT(rstd[:TB, :nb], rstd[:TB, :nb], AF.Sqrt)
        self.RECIP(rstd[:TB, :nb], rstd[:TB, :nb])
        for hf in range(2):
            for b in range(nb):
                xs = self.xres[:TB, b, hf * 512:(hf + 1) * 512]
                self.STT(xs, h0[hf * nb + b][:TB, :], rstd[:TB, b:b + 1], xs, ALU.mult, ALU.add)
        self.A.put(*h0)

    def phaseC(self, l, sq):
        S = self.S
        S.barrier()
        self.carve('C')
        NT, TB, nb = sq.NT, sq.TB, sq.nb
        self.wplan_reset(self.plan_C(l) * sq.ntiles)
        x_src = sq.x_in if l == 0 else sq.x_mid
        x_dst = sq.x_mid if l == 0 else sq.y
        for qt in range(sq.ntiles):
            t0 = qt * NT
            self.LD('x', self.xres[:TB, :nb, :], Tl(x_src.ap[t0:t0 + NT, :].rearrange("(b p) d -> p b d", p=TB), x_src.k))
            hT = [self.Bp.get() for _ in range(8)]
            for kc in range(8):
                self.LD(f'lh{kc % 2}', hT[kc][:, :NT], sq.HT[qt, :, kc, :])
            oT = []
            for ty in range(3):
                lst = []
                for j in range(2):
                    o = self.Bp.get()
                    self.LD(f'lo{j}', o[0:64, :NT], sq.Os[ty][2 * j][:, t0:t0 + NT])
                    self.LD(f'lo{j}', o[64:128, :NT], sq.Os[ty][2 * j + 1][:, t0:t0 + NT])
                    lst.append(o[:, :NT])
                oT.append(lst)
            lst = []
            for c in range(2):
                o = self.Bp.get()
                self.LD(f'lo{c}', o[:, :NT], sq.OB[:, c, t0:t0 + NT])
                lst.append(o[:, :NT])
            branch_in = [oT[0], lst, oT[1], oT[2]]
            merged = [self.A.get() for _ in range(8)]
            for n in range(4):
                wb = self.wnext(f'wb{n}')
                xin = branch_in[n]
                for hf in range(2):
                    gv = self.wnext(f'gate{n}{hf}')
                    for cc in range(4):
                        c = hf * 4 + cc
                        pg = self.fm_proj(gv, cc * 128, 128, hT, NT)
                        pp = self.PS.get()
                        for j, x in enumerate(xin):
                            K = x.ap.shape[0]
                            self.MM(pp[:, :NT], wb[0:K, j, c * 128:(c + 1) * 128], x, start=(j == 0), stop=(j == len(xin) - 1))
                        gt = self.A.get()
                        self.ACT(gt[:, :NT], pg[:, :NT], AF.Sigmoid)
                        self.PS.put(pg)
                        if n == 0:
                            self.TT('dve', merged[c][:, :NT], gt[:, :NT], pp[:, :NT], ALU.mult)
                        else:
                            self.TT('dve', gt[:, :NT], gt[:, :NT], pp[:, :NT], ALU.mult)
                            self.TT('pool', merged[c][:, :NT], merged[c][:, :NT], gt[:, :NT], ALU.add)
                        self.PS.put(pp)
                        self.A.put(gt)
            self._put_by_key([x.k for lst_ in branch_in for x in lst_])
            mb = []
            for c in range(8):
                b_ = self.Bp.get()
                self.CP('act', b_[:, :NT], merged[c][:, :NT])
                mb.append(b_[:, :NT])
            self.A.put(*merged)
            self._put_by_key([x.k for x in hT])
            self.tm_linear_norm_res(sq, mb, [[('wout0', 8)], [('wout1', 8)]], self.gpost[0])
            self._put_by_key([x.k for x in mb])
            self.ckc(1, x_dst, t0, sq)
            h2 = self.rmsnorm_T(sq, self.xres, D, self.gpre[:, 1, :])
            wq = self.wnext('wq')
            om = []
            for h in range(4):
                ps = self.fm_proj(wq, h * 128, 128, h2, NT)
                qb = self.Bp.get()
                self.CP('act', qb[:, :NT], ps[:, :NT])
                self.PS.put(ps)
                oacc = self.PS.get()
                dacc = self.PS.get()
                for mbk in range(2):
                    s = self.PS.get()
                    self.MM(s[:, :NT], self.MKT[:, h, mbk * 128:(mbk + 1) * 128], qb[:, :NT])
                    p = self.Bp.get()
                    self.ACT(p[:, :NT], s[:, :NT], AF.Exp, scale=128 ** -0.5)
                    self.PS.put(s)
                    self.MM(oacc[:, :NT], self.MV[:, mbk, h * 128:(h + 1) * 128], p[:, :NT], start=(mbk == 0), stop=(mbk == 1))
                    self.MM(dacc[:, :NT], self.ones_bf, p[:, :NT], start=(mbk == 0), stop=(mbk == 1))
                    self.Bp.put(p)
                rd = self.A.get()
                self.RECIP(rd[:, :NT], dacc[:, :NT])
                o = self.Bp.get()
                self.TT('dve', o[:, :NT], oacc[:, :NT], rd[:, :NT], ALU.mult)
                self.PS.put(oacc, dacc)
                self.A.put(rd)
                self.Bp.put(qb)
                om.append(o[:, :NT])
            self.Bp.put(*h2)
            self.tm_linear_norm_res(sq, om, [[('wo0', 4)], [('wo1', 4)]], self.gpost[1])
            self._put_by_key([x.k for x in om])
            self.ckc(2, x_dst, t0, sq)
            h3 = self.rmsnorm_T(sq, self.xres, D, self.gpre[:, 2, :])
            act = []
            for j in range(6):
                ncol = 512 if j < 5 else 256
                wg = self.wnext(f'wg{j}')
                wu = self.wnext(f'wu{j}')
                for cc in range(ncol // 128):
                    pg = self.fm_proj(wg, cc * 128, 128, h3, NT)
                    pu = self.fm_proj(wu, cc * 128, 128, h3, NT)
                    sg = self.A.get()
                    self.ACT(sg[:, :NT], pg[:, :NT], AF.Silu)
                    self.PS.put(pg)
                    a_ = self.Bp.get()
                    self.TT('dve', a_[:, :NT], sg[:, :NT], pu[:, :NT], ALU.mult)
                    self.PS.put(pu)
                    self.A.put(sg)
                    act.append(a_[:, :NT])
            self.Bp.put(*h3)
            self.tm_linear_norm_res(sq, act, [[(f'wd{hf}{kg}', nk) for kg, nk in enumerate((8, 8, 6))] for hf in range(2)], self.gpost[2])
            self._put_by_key([x.k for x in act])
            self.ST('sx', Tl(x_dst.ap[t0:t0 + NT, :].rearrange("(b p) d -> p b d", p=TB), x_dst.k), self.xres[:TB, :nb, :])

    def ckc(self, n, x_dst, t0, sq):
        if int(os.environ.get('MK_SUBC', '999')) == n:
            self.ST('sx', Tl(x_dst.ap[t0:t0 + sq.NT, :].rearrange("(b p) d -> p b d", p=sq.TB), x_dst.k), self.xres[:sq.TB, :sq.nb, :])
            raise StopBuild()

    def _put_by_key(self, keys):
        for k in keys:
            self.Bp.put(self._bt[k])

    def mem_kv(self, l, sq):
        I, O = self.I, self.O
        if sq.P == 0:
            self.wplan_reset([('wk', self.wsrc('mem_wk', l, 0, 8, 0, 512), 128, 8, 512),
                              ('wv', self.wsrc('mem_wv', l, 0, 8, 0, 512), 128, 8, 512)])
            self.LD('x', self.xres[:, 0:2, :], Tl(I['mem_prompt'].ap.rearrange("(b p) d -> p b d", p=128), 'x'))
            mT = self.rmsnorm_T(sq, self.xres, D, self.gpre[:, 3, :], NT=256, TB=128, nb=2)
            wk = self.wnext('wk')
            for h in range(4):
                ps = self.fm_proj(wk, h * 128, 128, mT, 256)
                self.CP(self.ev(), self.MKT[:, h, :], ps[:, 0:256])
                self.PS.put(ps)
            for nm, wv_, on in (('k', wk, 'p_mem_k'), ('v', None, 'p_mem_v')):
                if wv_ is None:
                    wv_ = self.wnext('wv')
                for b in range(2):
                    ps = self.tm_proj(wv_, 0, 512, mT, b, 128)
                    t = self.A.get()
                    self.CP('act', t, ps)
                    if nm == 'v':
                        self.CP('dve', self.MV[:, b, :], ps)
                    self.PS.put(ps)
                    self.ST('smk', Tl(O[on].ap[l, b * 128:(b + 1) * 128, :], O[on].k), t)
                    self.A.put(t)
            self.Bp.put(*mT)
        else:
            for b in range(2):
                tk = self.A.get()
                tv = self.A.get()
                self.LD('ck', tk, Tl(I['cache_mem_k'].ap[l, b * 128:(b + 1) * 128, :], 'x'))
                self.LD('cv', tv, Tl(I['cache_mem_v'].ap[l, b * 128:(b + 1) * 128, :], 'x'))
                self.CP('dve', self.MV[:, b, :], tv)
                ps = self.PS.get()
                for h in range(4):
                    self.TR(ps[:, h * 128:(h + 1) * 128], tk[:, h * 128:(h + 1) * 128], self.ident_f)
                self.CP('act', self.MKT[:, :, b * 128:(b + 1) * 128], ps.re("p (h m) -> p h m", m=128))
                self.PS.put(ps)
                self.A.put(tk, tv)

    def build(self):
        S = self.S
        self.carve('A')
        stage = int(os.environ.get('MK_STAGE', '999'))
        st = [0]

        def go():
            st[0] += 1
            return st[0] <= stage
        self.setup_consts()
        try:
            self.build_body(go)
        except StopBuild:
            pass
        S.finish()
        return self.nc

    def build_body(self, go):
        S = self.S
        if go():
            self.cast_weights()
        for l in range(L):
            S.barrier()
            self.carve('A')
            if go():
                self.load_layer_params(l)
            for sq in self.seqs:
                if go():
                    self.phaseA(l, sq)
                if go():
                    self.phaseB(l, sq)
                S.barrier()
                self.carve('C')
                if go():
                    self.mem_kv(l, sq)
                if go():
                    self.phaseC(l, sq)


_CACHE = {}


def get_program(T):
    if T not in _CACHE:
        _CACHE[T] = Builder(T).build()
    return _CACHE[T]


def kernel(**inputs):
    inputs = {k: np.asarray(v) for k, v in inputs.items()}
    B, T, _ = inputs['x_prompt'].shape
    nc = get_program(T)
    f32 = lambda a: np.ascontiguousarray(a, dtype=np.float32)
    in_maps = []
    wnames = ['ln_mix_pre', 'ln_mix_post', 'w_in', 'fox_bf', 'lru_conv_w', 'lru_conv_b', 'lru_wr', 'lru_br', 'lru_wi', 'lru_bi',
              'lru_lam', 'mla_q_norm', 'mla_w_uq', 'mla_kv_norm', 'mla_w_uk', 'mla_w_uv', 'w_branch', 'w_out', 'ln_mem_pre',
              'ln_mem_post', 'mem_norm', 'mem_wq', 'mem_wk', 'mem_wv', 'mem_wo', 'ln_ffn_pre', 'ln_ffn_post', 'ffn_wg', 'ffn_wu', 'ffn_wd']
    shared = {}
    for n in wnames:
        a = inputs[n]
        if n in ('lru_wr', 'lru_wi'):
            a = a.reshape(L, 256, 64)
        elif n in ('lru_br', 'lru_bi'):
            a = a.reshape(L, 256)
        elif n == 'w_branch':
            a = a.reshape(L, 1024, 1024)
        shared[n] = f32(a)
    for c in range(B):
        m = dict(shared)
        m['x_prompt'] = f32(inputs['x_prompt'][c])
        m['x_sample'] = f32(inputs['x_sample'][c])
        for n in ('cache_fox_k', 'cache_fox_v', 'cache_sb_k', 'cache_sb_v'):
            m[n] = f32(inputs[n][:, c].reshape(L, PAST, 256))
        m['cache_fox_logf'] = f32(inputs['cache_fox_logf'][:, c])
        m['state_lru_h'] = f32(inputs['state_lru_h'][:, c])
        m['state_lru_conv'] = f32(inputs['state_lru_conv'][:, c])
        m['cache_mla_ckv'] = f32(inputs['cache_mla_ckv'][:, c])
        m['cache_mla_kpe'] = f32(inputs['cache_mla_kpe'][:, c])
        m['cache_mem_k'] = f32(inputs['cache_mem_k'][:, c].reshape(L, MEM, 512))
        m['cache_mem_v'] = f32(inputs['cache_mem_v'][:, c].reshape(L, MEM, 512))
        m['mem_prompt'] = f32(inputs['mem_prompt'][c])
        in_maps.append(m)
    res = run_bass_kernel_spmd(nc, in_maps, core_ids=list(range(B)))
    R = res.results
    global _LAST
    _LAST = R

    def gather(name, shape_tail, batch_axis):
        return np.stack([np.asarray(R[c][name], dtype=np.float32) for c in range(B)], axis=batch_axis).reshape(shape_tail)

    outs = []
    outs.append(gather('p_y', (B, T, D), 0))
    outs.append(gather('s_y', (B, TS_, D), 0))
    for pre, t in (('p', T), ('s', TS_)):
        lst = [gather(f'{pre}_fox_k', (L, B, t, 4, 64), 1), gather(f'{pre}_fox_v', (L, B, t, 4, 64), 1),
               gather(f'{pre}_fox_logf', (L, B, t, 4), 1), gather(f'{pre}_lru_h', (L, B, 256), 1),
               gather(f'{pre}_lru_conv', (L, B, 3, 256), 1), gather(f'{pre}_mla_ckv', (L, B, t, 128), 1),
               gather(f'{pre}_mla_kpe', (L, B, t, 32), 1), gather(f'{pre}_sb_k', (L, B, t, 4, 64), 1),
               gather(f'{pre}_sb_v', (L, B, t, 4, 64), 1)]
        if pre == 'p':
            lst += [gather('p_mem_k', (L, B, MEM, 4, 128), 1), gather('p_mem_v', (L, B, MEM, 4, 128), 1)]
        outs += lst
    return tuple(outs)
```

```python
import math
import os
import traceback
from collections import deque
import numpy as np
import concourse.bass as bass
import concourse.mybir as mybir
from concourse.bass_utils import run_bass_kernel_spmd

F32 = mybir.dt.float32
BF16 = mybir.dt.bfloat16
I32 = mybir.dt.int32
AF = mybir.ActivationFunctionType
ALU = mybir.AluOpType

D = 1024
L = 2
DFF = 2816
DIN = 6564
PAST = 2048
TS_ = 32
MEM = 256
EPS = 1e-6
NEG = -30000.0
NW = 5


DEBUG = bool(os.environ.get('MK_DEBUG'))


def _site():
    return [f"{f.name}:{f.lineno}" for f in traceback.extract_stack(limit=6)[:-2]]


class Sched:
    ENG = ('pe', 'act', 'dve', 'pool', 'sp')

    def __init__(self, nc):
        self.nc = nc
        self.q = {e: [] for e in self.ENG}
        self.sem = {}
        self.cnt = {}
        for e in ('pe', 'act', 'dve', 'pool'):
            self.sem[e] = nc.alloc_semaphore(name=f"c_{e}")
            self.cnt[e] = 0
        self.seen = {e: {} for e in self.ENG}
        self.res = {}
        self.nops = 0

    def chan(self, name):
        if name not in self.sem:
            self.sem[name] = self.nc.alloc_semaphore(name=f"d_{name}")
            self.cnt[name] = 0
        return name

    def _need(self, eng, toks, waits):
        for t in toks:
            if t is None:
                continue
            s, v = t
            if eng == 'pe' and s == 'pe':
                continue
            if self.seen[eng].get(s, 0) >= v:
                continue
            if waits.get(s, 0) < v:
                waits[s] = v

    def _deps(self, eng, reads, writes):
        waits = {}
        for k in reads:
            r = self.res.get(k)
            if r:
                self._need(eng, [r[0]], waits)
        for k in writes:
            r = self.res.get(k)
            if r:
                self._need(eng, [r[0]] + r[1], waits)
        for s, v in waits.items():
            self.seen[eng][s] = v
        return [(self.sem[s], v) for s, v in waits.items()]

    def _mark(self, tok, reads, writes):
        for k in reads:
            r = self.res.setdefault(k, [None, []])
            r[1].append(tok)
            if len(r[1]) > 16:
                m = {}
                for s, v in r[1]:
                    if m.get(s, 0) < v:
                        m[s] = v
                r[1] = list(m.items())
        for k in writes:
            self.res[k] = [tok, []]

    def op(self, eng, fn, reads=(), writes=()):
        pr = [k for k in reads if k[:2] in ('ps', 'pT')]
        if pr and eng != 'pe':
            writes = list(writes) + pr
            reads = [k for k in reads if k not in pr]
        waits = self._deps(eng, reads, writes)
        self.cnt[eng] += 1
        tok = (eng, self.cnt[eng])
        sem = self.sem[eng]

        site = _site() if DEBUG else None

        def emit(h, waits=waits, fn=fn, sem=sem, site=site):
            for s, v in waits:
                h.wait_ge(s, v)
            try:
                fn(h).then_inc(sem, 1)
            except Exception:
                print('FAILED OP SITE:', site)
                raise
        self.q[eng].append(emit)
        self._mark(tok, reads, writes)
        self.nops += 1
        return tok

    def dma(self, queue, chan, pairs, reads=(), writes=(), **kw):
        self.chan(chan)
        waits = self._deps(queue, reads, writes)
        prev = self.cnt[chan]
        w2 = {}
        if prev > 0:
            self._need(queue, [(chan, prev)], w2)
        for s, v in w2.items():
            self.seen[queue][s] = v
        waits = waits + [(self.sem[s], v) for s, v in w2.items()]
        self.cnt[chan] += 16 * len(pairs)
        tok = (chan, self.cnt[chan])
        sem = self.sem[chan]

        site = _site() if DEBUG else None

        def emit(h, waits=waits, pairs=pairs, sem=sem, kw=kw, site=site):
            for s, v in waits:
                h.wait_ge(s, v)
            try:
                for o, i in pairs:
                    h.dma_start(out=o, in_=i, **kw).then_inc(sem, 16)
            except Exception:
                print('FAILED DMA SITE:', site)
                raise
        self.q[queue].append(emit)
        self._mark(tok, reads, writes)
        self.nops += len(pairs)
        return tok

    def barrier(self):
        snap = {s: v for s, v in self.cnt.items() if v > 0}
        for e in self.ENG:
            waits = []
            for s, v in snap.items():
                if self.seen[e].get(s, 0) < v:
                    waits.append((self.sem[s], v))
                    self.seen[e][s] = v

            def emit(h, waits=waits):
                for s, v in waits:
                    h.wait_ge(s, v)
            self.q[e].append(emit)
        self.res = {}

    def finish(self):
        self.barrier()
        nc = self.nc
        q = self.q
        with nc.Block() as block:
            @block.tensor
            def _(h):
                for f in q['pe']:
                    f(h)

            @block.scalar
            def _(h):
                for f in q['act']:
                    f(h)

            @block.vector
            def _(h):
                for f in q['dve']:
                    f(h)

            @block.gpsimd
            def _(h):
                for f in q['pool']:
                    f(h)

            @block.sync
            def _(h):
                for f in q['sp']:
                    f(h)


class Tl:
    __slots__ = ('ap', 'k')

    def __init__(self, ap, k):
        self.ap = ap
        self.k = k

    def __getitem__(self, idx):
        return Tl(self.ap[idx], self.k)

    def re(self, pat_, **kw):
        return Tl(self.ap.rearrange(pat_, **kw), self.k)

    def bc(self, shape):
        return Tl(self.ap.broadcast_to(shape), self.k)

    def us(self, ax):
        return Tl(self.ap.unsqueeze(ax), self.k)


def _ap(x):
    return x.ap if isinstance(x, Tl) else x


def _keys(*xs):
    return [x.k for x in xs if isinstance(x, Tl)]


class TilePool:
    def __init__(self, tiles):
        self.free = deque(tiles)

    def get(self):
        return self.free.popleft()

    def put(self, *ts):
        for t in ts:
            self.free.append(t)


class Seq:
    pass


class StopBuild(Exception):
    pass


class Builder:
    def __init__(self, T):
        self.T = T
        self.nc = nc = bass.Bass("TRN2", target_bir_lowering=False)
        self.S = Sched(nc)
        self._rr = 0
        self.declare_dram()
        self.alloc_sbuf()

    def din(self, name, shape):
        return Tl(self.nc.dram_tensor(name, list(shape), F32, kind="ExternalInput").ap(), 'in_' + name)

    def dout(self, name, shape):
        return Tl(self.nc.dram_tensor(name, list(shape), F32, kind="ExternalOutput").ap(), 'out_' + name)

    def dscr(self, name, shape, dt):
        kind = "ExternalOutput" if (os.environ.get('MK_DBGOUT') and not name.startswith('b_')) else "Internal"
        return Tl(self.nc.dram_tensor(name, list(shape), dt, kind=kind).ap(), 'scr_' + name)

    def declare_dram(self):
        T = self.T
        I = self.I = {}
        O = self.O = {}
        I['x_prompt'] = self.din('x_prompt', [T, D])
        I['x_sample'] = self.din('x_sample', [TS_, D])
        for n in ('cache_fox_k', 'cache_fox_v', 'cache_sb_k', 'cache_sb_v'):
            I[n] = self.din(n, [L, PAST, 256])
        I['cache_fox_logf'] = self.din('cache_fox_logf', [L, PAST, 4])
        I['state_lru_h'] = self.din('state_lru_h', [L, 256])
        I['state_lru_conv'] = self.din('state_lru_conv', [L, 3, 256])
        I['cache_mla_ckv'] = self.din('cache_mla_ckv', [L, PAST, 128])
        I['cache_mla_kpe'] = self.din('cache_mla_kpe', [L, PAST, 32])
        I['cache_mem_k'] = self.din('cache_mem_k', [L, MEM, 512])
        I['cache_mem_v'] = self.din('cache_mem_v', [L, MEM, 512])
        I['mem_prompt'] = self.din('mem_prompt', [MEM, D])
        wshapes = dict(
            ln_mix_pre=[L, D], ln_mix_post=[L, D], w_in=[L, D, DIN], fox_bf=[L, 4], lru_conv_w=[L, 4, 256],
            lru_conv_b=[L, 256], lru_wr=[L, 256, 64], lru_br=[L, 256], lru_wi=[L, 256, 64], lru_bi=[L, 256],
            lru_lam=[L, 256], mla_q_norm=[L, 256], mla_w_uq=[L, 256, 384], mla_kv_norm=[L, 128],
            mla_w_uk=[L, 128, 256], mla_w_uv=[L, 128, 256], w_branch=[L, 1024, 1024], w_out=[L, D, D],
            ln_mem_pre=[L, D], ln_mem_post=[L, D], mem_norm=[L, D], mem_wq=[L, D, 512], mem_wk=[L, D, 512],
            mem_wv=[L, D, 512], mem_wo=[L, 512, D], ln_ffn_pre=[L, D], ln_ffn_post=[L, D],
            ffn_wg=[L, D, DFF], ffn_wu=[L, D, DFF], ffn_wd=[L, DFF, D])
        self.wshapes = wshapes
        for n, s in wshapes.items():
            I[n] = self.din(n, s)
        for pre, t in (('p', T), ('s', TS_)):
            O[pre + '_y'] = self.dout(pre + '_y', [t, D])
            for n in ('fox_k', 'fox_v', 'sb_k', 'sb_v'):
                O[f'{pre}_{n}'] = self.dout(f'{pre}_{n}', [L, t, 256])
            O[pre + '_fox_logf'] = self.dout(pre + '_fox_logf', [L, t, 4])
            O[pre + '_lru_h'] = self.dout(pre + '_lru_h', [L, 256])
            O[pre + '_lru_conv'] = self.dout(pre + '_lru_conv', [L, 3, 256])
            O[pre + '_mla_ckv'] = self.dout(pre + '_mla_ckv', [L, t, 128])
            O[pre + '_mla_kpe'] = self.dout(pre + '_mla_kpe', [L, t, 32])
        O['p_mem_k'] = self.dout('p_mem_k', [L, MEM, 512])
        O['p_mem_v'] = self.dout('p_mem_v', [L, MEM, 512])
        self.big = ['w_in', 'w_branch', 'w_out', 'mem_wq', 'mem_wk', 'mem_wv', 'mem_wo', 'ffn_wg', 'ffn_wu', 'ffn_wd']
        self.Wb = {n: self.dscr('b_' + n, wshapes[n], BF16) for n in self.big}
        self.seqs = []
        for name, t, p in (('p', T, 0), ('s', TS_, PAST)):
            sq = Seq()
            sq.name = name
            sq.T = t
            sq.P = p
            sq.NT = min(512, t)
            sq.TB = min(128, t)
            sq.nb = sq.NT // sq.TB
            sq.ntiles = t // sq.NT
            sq.Ttot = p + t
            sq.nblk = (sq.Ttot + 127) // 128
            sq.x_in = I['x_prompt'] if name == 'p' else I['x_sample']
            sq.x_mid = self.dscr(name + '_xmid', [t, D], F32)
            sq.y = O[name + '_y']
            sq.HT = self.dscr(name + '_HT', [sq.ntiles, 128, 8, sq.NT], BF16)
            sq.Qs = [[self.dscr(f'{name}_Q{ty}{h}', [(65, 96, 64)[ty], t], BF16) for h in range(4)] for ty in range(3)]
            sq.KTs = [[self.dscr(f'{name}_K{ty}{h}', [(65, 96, 64)[ty], sq.Ttot], BF16) for h in range(4)] for ty in range(3)]
            sq.Vs = [[self.dscr(f'{name}_V{ty}{h}', [128, sq.nblk, 68], BF16) for h in range(4)] for ty in range(3)]
            sq.Os = [[self.dscr(f'{name}_O{ty}{h}', [64, t], BF16) for h in range(4)] for ty in range(3)]
            sq.OB = self.dscr(name + '_OB', [128, 2, t], BF16)
            self.seqs.append(sq)

    def sb(self, name, shape, dt):
        return Tl(self.nc.alloc_sbuf_tensor(name, list(shape), dt).ap(), name)

    def alloc_sbuf(self):
        nc = self.nc
        self.wslots = [self.sb(f'wslot{i}', [128, 4096], BF16) for i in range(NW)]
        NA, NB = 16, 24
        self.A = TilePool([self.sb(f'A{i}', [128, 512], F32) for i in range(NA)])
        self.Bp = TilePool([self.sb(f'B{i}', [128, 512], BF16) for i in range(NB)])
        self.gpost = [self.sb(f'gpost{i}', [128, D], F32) for i in range(3)]
        self.ident_bf = self.sb('ident_bf', [128, 128], BF16)
        self.ident_f = self.sb('ident_f', [128, 128], F32)
        self.E0 = self.sb('E0', [128, 128], F32)
        self.U = self.sb('U', [128, 128], BF16)
        self.ones_bf = self.sb('ones_bf', [128, 128], BF16)
        self.ones_f = self.sb('ones_f', [128, 512], F32)
        self.ones_row = self.sb('ones_row', [1, 512], BF16)
        self.ctok = self.sb('ctok', [128, 4, 68], F32)
        self.rb = self.sb('rb', [128, 16, 4], F32)
        self.fb = [self.sb(f'fb{i}', [128, 68], F32) for i in range(2)]
        self.gpre = self.sb('gpre', [128, 4, 8], F32)
        self.qng = self.sb('qng', [128, 2], F32)
        self.kvg = self.sb('kvg', [128, 128], F32)
        self.nbf = self.sb('nbf', [4, 1], F32)
        self.cw = self.sb('cw', [128, 2, 4], F32)
        self.cbias = self.sb('cbias', [128, 2], F32)
        self.br = self.sb('br', [128, 2], F32)
        self.bi = self.sb('bi', [128, 2], F32)
        self.sl = self.sb('sl', [128, 2], F32)
        self.wr_bd = self.sb('wr_bd', [128, 2, 128], BF16)
        self.wi_bd = self.sb('wi_bd', [128, 2, 128], BF16)
        self.wuq = self.sb('wuq', [128, 2, 384], BF16)
        self.wuq_rot = self.sb('wuq_rot', [128, 2, 4, 96], BF16)
        self.wuk = self.sb('wuk', [128, 256], BF16)
        self.wuv = self.sb('wuv', [128, 256], BF16)
        self.MKT = self.sb('MKT', [128, 4, 256], BF16)
        self.MV = self.sb('MV', [128, 2, 512], BF16)
        self.ss = self.sb('ss', [128, 8], F32)
        self.rstd = self.sb('rstd', [128, 8], F32)
        self.hcar = self.sb('hcar', [128, 2], F32)
        self.ccar = self.sb('ccar', [4, 1], F32)
        self.invp = self.sb('invp', [128, 1], F32)
        self.invr = self.sb('invr', [128, 16], F32)
        RSZ = 31 * 1024
        self.R = self.sb('R', [128, RSZ], BF16)
        self.RSZ = RSZ
        banks = [Tl(nc.alloc_psum_tensor(f'ps{i}', [128, 512], F32).ap(), f'ps{i}') for i in range(8)]
        self.PS = TilePool(banks[0:6])
        self.ps_extra = banks[6:8]
        self.pT = [Tl(banks[6].ap.bitcast(BF16)[:, 0:512], 'pT0'), Tl(banks[7].ap.bitcast(BF16)[:, 0:512], 'pT1')]
        self._pTi = 0

    def carve(self, phase):
        R = self.R.ap
        off = [0]

        def take(n_bf16, key, dt=BF16):
            a = R[:, off[0]:off[0] + n_bf16]
            off[0] += n_bf16
            if dt is not BF16:
                a = a.bitcast(dt)
            return Tl(a, key)
        base = [t for t in self.Bp.free if not t.k.startswith('BX')]
        assert len(base) == 24, len(base)
        assert len(self.A.free) == 16, len(self.A.free)
        psb = [t for t in self.PS.free if t.k not in ('ps6', 'ps7')]
        assert len(psb) == 6, len(psb)
        self.PS.free = deque(psb + (self.ps_extra if phase == 'B' else []))
        self.Bp.free = deque(base)
        if phase in ('A', 'C'):
            self.xres = take(4 * D * 2, 'xres', F32).re("p (b d) -> p b d", d=D)
            self.xn = take(4 * D, 'xn')
            self.junk = take(D, 'junk')
            self.junk2 = take(D, 'junk2')
            if phase == 'A':
                self.tm4 = take(4 * 416 * 2, 'tm4', F32).re("p (b d) -> p b d", d=416)
                self.xpad = take(2 * 516 * 2, 'xpad', F32).re("p (c t) -> p c t", t=516)
                self.vaug = [take(4 * 4 * 68, f'vaug{i}') for i in range(2)]
                self.kpad = take(4 * 96, 'kpad').re("p (b d) -> p b d", d=96)
                self.ropeT = take(2 * 512 * 2, 'ropeT', F32).re("p (s t) -> p s t", t=512)
                self.ropeK = take(2 * 64 * 2, 'ropeK', F32).re("p (s b j) -> p s b j", s=2, j=16)
                self.ropetmp = take(2 * 512 * 2, 'ropetmp', F32).re("p (s t) -> p s t", t=512)
                self.ropei = take(2 * 512 * 2, 'ropei', I32).re("p (s t) -> p s t", t=512)
            i = 0
            while off[0] + 512 <= self.RSZ:
                self.Bp.put(take(512, f'BX{i}'))
                i += 1
        elif phase == 'B':
            self.ktbuf = [take(8192, f'ktbuf{i}') for i in range(2)]
            self.vbuf = [take(65 * 68, f'vbuf{i}') for i in range(2)]
            self.masks = [[take(512, f'mask{ty}{m}') for m in range(4)] for ty in range(3)]
        self._bt = {t.k: t for t in self.Bp.free}

    def MM(self, out, lhsT, rhs, start=True, stop=True, sgc=False):
        o, a, b = _ap(out), _ap(lhsT), _ap(rhs)
        if sgc:
            self.S.op('pe', lambda h: h.matmul(o, a, b, start=start, stop=stop, skip_group_check=True), reads=_keys(lhsT, rhs), writes=_keys(out))
        else:
            self.S.op('pe', lambda h: h.matmul(o, a, b, start=start, stop=stop), reads=_keys(lhsT, rhs), writes=_keys(out))

    def TR(self, out, in_, ident):
        o, a, b = _ap(out), _ap(in_), _ap(ident)
        self.S.op('pe', lambda h: h.transpose(o, a, b), reads=_keys(in_, ident), writes=_keys(out))

    def ACT(self, out, in_, func, bias=None, scale=None, accum=None):
        kw = {}
        if bias is not None:
            kw['bias'] = _ap(bias)
        if scale is not None:
            kw['scale'] = _ap(scale)
        if accum is not None:
            kw['accum_out'] = _ap(accum)
        o, a = _ap(out), _ap(in_)
        self.S.op('act', lambda h: h.activation(out=o, in_=a, func=func, **kw),
                  reads=_keys(in_, bias, scale), writes=_keys(out, accum))

    def TSC(self, eng, out, in0, s1, s2, op0, op1=None):
        o, a, x1, x2 = _ap(out), _ap(in0), _ap(s1), _ap(s2)
        if op1 is None:
            self.S.op(eng, lambda h: h.tensor_scalar(o, a, x1, None, op0), reads=_keys(in0, s1), writes=_keys(out))
        else:
            self.S.op(eng, lambda h: h.tensor_scalar(o, a, x1, x2, op0, op1), reads=_keys(in0, s1, s2), writes=_keys(out))

    def TT(self, eng, out, in0, in1, op):
        o, a, b = _ap(out), _ap(in0), _ap(in1)
        self.S.op(eng, lambda h: h.tensor_tensor(out=o, in0=a, in1=b, op=op), reads=_keys(in0, in1), writes=_keys(out))

    def STT(self, out, in0, scalar, in1, op0, op1):
        o, a, s, b = _ap(out), _ap(in0), _ap(scalar), _ap(in1)
        self.S.op('dve', lambda h: h.scalar_tensor_tensor(out=o, in0=a, scalar=s, in1=b, op0=op0, op1=op1),
                  reads=_keys(in0, scalar, in1), writes=_keys(out))

    def CP(self, eng, out, in_):
        o, a = _ap(out), _ap(in_)
        if eng == 'act':
            self.S.op('act', lambda h: h.copy(o, a), reads=_keys(in_), writes=_keys(out))
        else:
            self.S.op(eng, lambda h: h.tensor_copy(o, a), reads=_keys(in_), writes=_keys(out))

    def SCAN(self, out, d0, d1, init, op0, op1):
        o, a, b, i = _ap(out), _ap(d0), _ap(d1), _ap(init)
        self.S.op('dve', lambda h: h.tensor_tensor_scan(out=o, data0=a, data1=b, initial=i, op0=op0, op1=op1),
                  reads=_keys(d0, d1, init), writes=_keys(out))

    def RECIP(self, out, in_):
        o, a = _ap(out), _ap(in_)
        self.S.op('dve', lambda h: h.reciprocal(o, a), reads=_keys(in_), writes=_keys(out))

    def MEMSET(self, eng, out, val):
        o = _ap(out)
        self.S.op(eng, lambda h: h.memset(o, val), writes=_keys(out))

    def ASEL(self, out, in_, pattern, cmp, fill, base, cm):
        o, a = _ap(out), _ap(in_)
        regs = self.__dict__.setdefault('_fillregs', {})

        def fn(h):
            if fill not in regs:
                regs[fill] = h.to_reg(fill)
            return h.affine_select(out=o, in_=a, pattern=pattern, compare_op=cmp, fill=regs[fill], base=base, channel_multiplier=cm)
        self.S.op('pool', fn, reads=_keys(in_), writes=_keys(out))

    def IOTA(self, out, pattern, base, cm):
        o = _ap(out)
        self.S.op('pool', lambda h: h.iota(o, pattern, base=base, channel_multiplier=cm), writes=_keys(out))

    def LD(self, chan, out, in_, **kw):
        self.S.dma('sp', chan, [(_ap(out), _ap(in_))], reads=_keys(in_), writes=_keys(out), **kw)

    def ST(self, chan, out, in_, **kw):
        self.S.dma('pool', chan, [(_ap(out), _ap(in_))], reads=_keys(in_), writes=_keys(out), **kw)

    def ck(self, n):
        if int(os.environ.get('MK_SUB', '999')) == n:
            raise StopBuild()

    def ev(self):
        self._rr ^= 1
        return 'act' if self._rr else 'dve'

    def next_pT(self):
        self._pTi ^= 1
        return self.pT[self._pTi]

    def wplan_reset(self, plan):
        self.wplan = plan
        self.wissued = 0
        self.wused = 0

    def _wissue(self, i):
        name, src, npart, nk, ncol = self.wplan[i]
        slot = self.wslots[i % NW]
        view = slot[:npart, 0:nk * ncol].re("p (k c) -> p k c", c=ncol)
        self.LD(f'w{i % NW}', view, src)

    def wnext(self, expect):
        i = self.wused
        assert self.wplan[i][0] == expect, (self.wplan[i][0], expect)
        while self.wissued < min(len(self.wplan), i + NW - 2):
            self._wissue(self.wissued)
            self.wissued += 1
        self.wused += 1
        name, src, npart, nk, ncol = self.wplan[i]
        return self.wslots[i % NW][:npart, 0:nk * ncol].re("p (k c) -> p k c", c=ncol)

    def wsrc(self, name, l, r0, nk, c0, ncol, p=128):
        w = self.Wb[name]
        return Tl(w.ap[l, r0:r0 + nk * p, c0:c0 + ncol].rearrange("(k q) c -> q k c", q=p), w.k)

    def plan_A(self, l):
        pl = []
        for nm, c0, ncol in (('in1', 0, 512), ('in2', 512, 260), ('in3', 772, 512), ('in4', 1284, 416),
                             ('in5', 1700, 512), ('in6', 2212, 256)):
            pl.append((nm, self.wsrc('w_in', l, 0, 8, c0, ncol), 128, 8, ncol))
        return pl

    def plan_C(self, l):
        pl = []
        for n in range(4):
            pl.append((f'wb{n}', self.wsrc('w_branch', l, n * 256, 2, 0, 1024), 128, 2, 1024))
            for hf in range(2):
                pl.append((f'gate{n}{hf}', self.wsrc('w_in', l, 0, 8, 2468 + n * 1024 + hf * 512, 512), 128, 8, 512))
        for hf in range(2):
            pl.append((f'wout{hf}', self.wsrc('w_out', l, 0, 8, hf * 512, 512), 128, 8, 512))
        pl.append(('wq', self.wsrc('mem_wq', l, 0, 8, 0, 512), 128, 8, 512))
        for hf in range(2):
            pl.append((f'wo{hf}', self.wsrc('mem_wo', l, 0, 4, hf * 512, 512), 128, 4, 512))
        for j in range(6):
            ncol = 512 if j < 5 else 256
            pl.append((f'wg{j}', self.wsrc('ffn_wg', l, 0, 8, j * 512, ncol), 128, 8, ncol))
            pl.append((f'wu{j}', self.wsrc('ffn_wu', l, 0, 8, j * 512, ncol), 128, 8, ncol))
        for hf in range(2):
            for kg, nk in enumerate((8, 8, 6)):
                pl.append((f'wd{hf}{kg}', self.wsrc('ffn_wd', l, kg * 1024, nk, hf * 512, 512), 128, nk, 512))
        return pl

    def setup_consts(self):
        z = self.A.get()
        self.MEMSET('pool', z, 0.0)
        self.MEMSET('pool', self.ones_f, 1.0)
        self.MEMSET('pool', self.ones_bf, 1.0)
        self.MEMSET('pool', self.ones_row, 1.0)
        self.MEMSET('pool', self.ctok, 0.0)
        self.MEMSET('pool', self.ident_f, 0.0)
        self.ASEL(self.ident_f, self.ident_f, [[-1, 128]], ALU.not_equal, 1.0, 0, 1)
        self.CP('dve', self.ident_bf, self.ident_f)
        self.MEMSET('pool', self.E0, 0.0)
        self.ASEL(self.E0, self.E0, [[0, 128]], ALU.not_equal, 1.0, 0, 1)
        self.ASEL(self.U, self.ones_bf, [[-1, 128]], ALU.is_ge, 0.0, 0, 1)
        ti = self.A.get()
        tiv = Tl(ti.ap.bitcast(I32), ti.k)
        self.IOTA(tiv[:, 0:1], [[0, 1]], 0, 1)
        self.S.op('dve', lambda h: h.tensor_single_scalar(out=tiv.ap[:, 0:1], in_=tiv.ap[:, 0:1], scalar=15, op=ALU.bitwise_and),
                  reads=[tiv.k], writes=[tiv.k])
        tf = self.A.get()
        self.CP('dve', tf[:, 0:1], tiv[:, 0:1])
        self.ACT(self.invp, tf[:, 0:1], AF.Exp, scale=-math.log(10000.0) / 16.0)
        self.IOTA(tiv[:, 16:32], [[1, 16]], 0, 0)
        self.CP('dve', tf[:, 16:32], tiv[:, 16:32])
        self.ACT(self.invr, tf[:, 16:32], AF.Exp, scale=-math.log(10000.0) / 16.0)
        self.A.put(z, ti, tf)

    def make_masks(self):
        zb = self.Bp.get()
        sm = self.Bp.get()
        self.MEMSET('pool', zb, 0.0)
        for m in range(4):
            self.ASEL(self.masks[0][m], zb, [[1, 512]], ALU.is_ge, NEG, -128 * m, -1)
            self.ASEL(sm[:, 8 * m:8 * m + 8], zb[:, 0:8], [[64, 8]], ALU.is_ge, NEG, 63 - 128 * m, -1)
            self.CP('dve', self.masks[1][m].re("p (a b) -> p a b", b=64), sm[:, 8 * m:8 * m + 8].us(2).bc([128, 8, 64]))
            self.ASEL(self.masks[2][m], zb, [[1, 512]], ALU.is_ge, NEG, -128 * m - 1, -1)
        self.Bp.put(zb, sm)

    def cast_weights(self):
        i = 0
        for n in self.big:
            src = self.I[n]
            dst = self.Wb[n]
            shp = self.wshapes[n]
            tot = int(np.prod(shp))
            per = tot // 128
            dims = " ".join("abc"[:len(shp)])
            sv = src.ap.rearrange(f"{dims} -> ({dims})").rearrange("(p x) -> p x", p=128)
            dv = dst.ap.rearrange(f"{dims} -> ({dims})").rearrange("(p x) -> p x", p=128)
            CH = 2048
            for x0 in range(0, per, CH):
                w = min(CH, per - x0)
                for y0 in range(x0, x0 + w, 512):
                    ww = min(512, x0 + w - y0)
                    a = self.A.get()
                    b = self.Bp.get()
                    self.LD(f'cv{i % 4}', a[:, :ww], Tl(sv[:, y0:y0 + ww], src.k))
                    self.CP(('act', 'dve', 'pool')[i % 3] if i % 3 != 2 else 'dve', b[:, :ww], a[:, :ww])
                    self.S.dma('pool', f'cs{i % 4}', [(dv[:, y0:y0 + ww], b.ap[:, :ww])], reads=[b.k], writes=[])
                    self.A.put(a)
                    self.Bp.put(b)
                    i += 1
        self.S.barrier()

    def load_layer_params(self, l):
        I = self.I
        ld = lambda out, in_, **kw: self.LD('par', out, in_, **kw)
        for j, n in enumerate(('ln_mix_pre', 'ln_mem_pre', 'ln_ffn_pre', 'mem_norm')):
            ld(self.gpre[:, j, :], Tl(I[n].ap[l].rearrange("(k p) -> p k", p=128), I[n].k), allow_slow_non_contiguous=True)
        for j, n in enumerate(('ln_mix_post', 'ln_mem_post', 'ln_ffn_post')):
            ld(self.gpost[j], Tl(I[n].ap[l].partition_broadcast(128), I[n].k))
        ld(self.qng, Tl(I['mla_q_norm'].ap[l].rearrange("(k p) -> p k", p=128), 'x'), allow_slow_non_contiguous=True)
        ld(self.kvg, Tl(I['mla_kv_norm'].ap[l].partition_broadcast(128), 'x'))
        t = self.A.get()
        ld(t[0:4, 0:1], Tl(I['fox_bf'].ap[l].rearrange("(p o) -> p o", o=1), 'x'))
        self.TSC('dve', self.nbf, t[0:4, 0:1], -1.0, None, ALU.mult)
        for c in range(2):
            ld(self.cw[:, c, :], Tl(I['lru_conv_w'].ap[l][:, c * 128:(c + 1) * 128].rearrange("t p -> p t"), 'x'), allow_slow_non_contiguous=True)
        for dst, n in ((self.cbias, 'lru_conv_b'), (self.br, 'lru_br'), (self.bi, 'lru_bi')):
            ld(dst, Tl(I[n].ap[l].rearrange("(c p) -> p c", p=128), 'x'), allow_slow_non_contiguous=True)
        ld(t[:, 8:10], Tl(I['lru_lam'].ap[l].rearrange("(c p) -> p c", p=128), 'x'), allow_slow_non_contiguous=True)
        self.ACT(t[:, 8:10], t[:, 8:10], AF.Exp, scale=-1.0)
        self.ACT(t[:, 8:10], t[:, 8:10], AF.Ln, bias=1.0)
        self.TSC('dve', self.sl, t[:, 8:10], -8.0, None, ALU.mult)
        for dst, n in ((self.wr_bd, 'lru_wr'), (self.wi_bd, 'lru_wi')):
            ld(t[:, 64:192].re("p (c j) -> p c j", j=64), Tl(I[n].ap[l].rearrange("(c p) j -> p c j", p=128), 'x'))
            self.MEMSET('pool', dst, 0.0)
            for c in range(2):
                self.CP('dve', dst[0:64, c, 0:64], t[0:64, 64 + c * 64:128 + c * 64])
                self.CP('dve', dst[64:128, c, 64:128], t[64:128, 64 + c * 64:128 + c * 64])
        t2 = self.A.get()
        t3 = self.A.get()
        ld(t2[:, 0:384], Tl(I['mla_w_uq'].ap[l, 0:128, :], 'x'))
        ld(t3[:, 0:384], Tl(I['mla_w_uq'].ap[l, 128:256, :], 'x'))
        self.MEMSET('pool', self.wuq_rot, 0.0)
        for kc, tt in enumerate((t2, t3)):
            self.CP('dve', self.wuq[:, kc, :], tt[:, 0:384])
            tv = tt[:, 0:384].re("p (h c) -> p h c", c=96)
            self.TSC('dve', self.wuq_rot[:, kc, :, 64:80], tv[:, :, 80:96], -1.0, None, ALU.mult)
            self.CP('dve', self.wuq_rot[:, kc, :, 80:96], tv[:, :, 64:80])
        t4 = self.A.get()
        ld(t4[:, 0:256], Tl(I['mla_w_uk'].ap[l], 'x'))
        ld(t4[:, 256:512], Tl(I['mla_w_uv'].ap[l], 'x'))
        self.CP('dve', self.wuk, t4[:, 0:256])
        self.CP('dve', self.wuv, t4[:, 256:512])
        self.A.put(t, t2, t3, t4)

    def rmsnorm_T(self, sq, src, W, gain_pp, NT=None, TB=None, nb=None):
        NT = NT or sq.NT
        TB = TB or sq.TB
        nb = nb or sq.nb
        nkc = W // 128
        ss, rstd = self.ss, self.rstd
        for b in range(nb):
            if b % 2 == 0 or W != D:
                self.ACT(self.junk[:TB, :W], src[:TB, b, :], AF.Square, accum=ss[:TB, b:b + 1])
            else:
                o_, a_, acc_ = _ap(self.junk2[:TB, :W]), _ap(src[:TB, b, :]), _ap(ss[:TB, b:b + 1])
                self.S.op('dve', lambda h, o_=o_, a_=a_, acc_=acc_: h.scalar_tensor_tensor(out=o_, in0=a_, scalar=1.0, in1=a_, op0=ALU.mult, op1=ALU.mult, accum_out=acc_),
                          reads=[src.k], writes=[self.junk2.k, ss.k])
        self.TSC('dve', rstd[:TB, :nb], ss[:TB, :nb], 1.0 / W, EPS, ALU.mult, ALU.add)
        self.ACT(rstd[:TB, :nb], rstd[:TB, :nb], AF.Sqrt)
        self.RECIP(rstd[:TB, :nb], rstd[:TB, :nb])
        xn = self.xn[:, 0:nb * W].re("p (b d) -> p b d", d=W)
        for b in range(nb):
            if b % 2 == 0:
                self.TSC('dve', xn[:TB, b, :], src[:TB, b, :], rstd[:TB, b:b + 1], None, ALU.mult)
            else:
                self.ACT(xn[:TB, b, :], src[:TB, b, :], AF.Copy, scale=rstd[:TB, b:b + 1])
        return self.transpose_T(xn, W, gain_pp, NT, TB, nb)

    def transpose_T(self, xn, W, gain_pp, NT, TB, nb):
        outs = []
        for kc in range(W // 128):
            pT = self.next_pT()
            for b in range(nb):
                self.TR(pT[:, b * TB:(b + 1) * TB], xn[:TB, b, kc * 128:(kc + 1) * 128], self.ident_bf[:TB, :TB])
            o = self.Bp.get()
            if gain_pp is None:
                self.CP(self.ev(), o[:, :NT], pT[:, :NT])
            elif kc % 2 == 0:
                self.TSC('dve', o[:, :NT], pT[:, :NT], gain_pp[:, kc:kc + 1], None, ALU.mult)
            else:
                self.ACT(o[:, :NT], pT[:, :NT], AF.Copy, scale=gain_pp[:, kc:kc + 1])
            outs.append(o)
        return outs

    def fm_proj(self, wv, c0, M, hT, NT, nk=None):
        ps = self.PS.get()
        nk = nk or len(hT)
        for kc in range(nk):
            self.MM(ps[0:M, :NT], wv[:, kc, c0:c0 + M], hT[kc][:, :NT], start=(kc == 0), stop=(kc == nk - 1))
        return ps

    def tm_proj(self, wv, c0, ncol, hT, b, TB):
        ps = self.PS.get()
        nk = len(hT)
        for kc in range(nk):
            self.MM(ps[0:TB, :ncol], hT[kc][:, b * TB:(b + 1) * TB], wv[:, kc, c0:c0 + ncol], start=(kc == 0), stop=(kc == nk - 1))
        return ps

    def rope_tables(self, sq, pos0):
        NT, TB, nb = sq.NT, sq.TB, sq.nb
        tw = 2 * math.pi
        for mode in ('T', 'K'):
            if mode == 'T':
                ang = self.ropetmp
                iv = self.ropei
                self.IOTA(iv[:, 0, :NT], [[1, NT]], pos0, 0)
                self.CP('dve', ang[:, 1, :NT], iv[:, 0, :NT])
                self.TSC('dve', ang[:, 1, :NT], ang[:, 1, :NT], self.invp[:, 0:1], None, ALU.mult)
                self.TSC('dve', ang[:, 0, :NT], ang[:, 1, :NT], math.pi / 2, None, ALU.add)
                a = ang[:, :, :NT]
                ii = iv[:, :, :NT]
                dst = self.ropeT[:, :, :NT]
                tmp = self.ropeT[:, :, :NT]
            else:
                angf = self.ropetmp[:, 0, 0:2 * nb * 16].re("p (s b j) -> p s b j", s=2, j=16)
                ivf = self.ropei[:, 0, 0:2 * nb * 16].re("p (s b j) -> p s b j", s=2, j=16)
                self.IOTA(ivf[:, 1, :, 0], [[128, nb]], pos0, 1)
                self.CP('dve', angf[:, 0, :, 0], ivf[:, 1, :, 0])
                self.TT('dve', angf[:TB, 1], angf[:TB, 0, :, 0:1].bc([TB, nb, 16]), self.invr[:TB].us(1).bc([TB, nb, 16]), ALU.mult)
                self.TSC('dve', angf[:TB, 0], angf[:TB, 1], math.pi / 2, None, ALU.add)
                a = angf[:TB]
                ii = ivf[:TB]
                dst = self.ropeK[:TB, :, :nb, :]
                tmp = self.ropeK[:TB, :, :nb, :]
            self.TSC('dve', tmp, a, 1.0 / tw, None, ALU.mult)
            self.CP('dve', ii, tmp)
            self.CP('dve', tmp, ii)
            self.STT(a, tmp, -tw, a, ALU.mult, ALU.add)
            self.TSC('dve', tmp, a, math.pi, tw, ALU.is_gt, ALU.mult)
            self.TT('dve', a, a, tmp, ALU.subtract)
            self.TSC('dve', tmp, a, -math.pi, tw, ALU.is_lt, ALU.mult)
            self.TT('dve', a, a, tmp, ALU.add)
            self.TSC('dve', a, a, math.pi, -math.pi, ALU.min, ALU.max)
            self.ACT(dst, a, AF.Sin)

    def mla_kv(self, sq, l, ckvn, kper, tok0, NT, TB, nb):
        xn = self.xn[:, 0:nb * 128].re("p (b d) -> p b d", d=128)
        self.CP('dve', xn[:TB], ckvn[:TB])
        ckT = self.transpose_T(xn, 128, None, NT, TB, nb)[0]
        self.CP('dve', self.kpad[:TB, :nb, 64:96], kper[:TB])
        pT2 = self.next_pT()
        for b in range(nb):
            self.TR(pT2[0:96, b * TB:(b + 1) * TB], self.kpad[:TB, b, :], self.ident_bf[:TB, :TB])
        blk0 = tok0 // 128
        va = self.vaug[0]
        vav = va[:, 0:nb * 272].re("p (b h d) -> p b h d", h=4, d=68)
        for b in range(nb):
            ps = self.PS.get()
            self.MM(ps[0:TB, 0:256], ckT[:, b * TB:(b + 1) * TB], self.wuv)
            self.CP(self.ev(), vav[:TB, b, :, 0:64], ps[0:TB, 0:256].re("p (h d) -> p h d", d=64))
            self.PS.put(ps)
        for h in range(4):
            ps = self.PS.get()
            self.MM(ps[0:64, :NT], self.wuk[:, h * 64:(h + 1) * 64], ckT[:, :NT])
            kt = self.Bp.get()
            self.CP('act', kt[0:64, :NT], ps[0:64, :NT])
            self.CP('dve', kt[64:96, :NT], pT2[64:96, :NT])
            self.PS.put(ps)
            self.ST(f'sk{h}', sq.KTs[1][h][:, tok0:tok0 + NT], kt[0:96, :NT])
            self.Bp.put(kt)
            self.ST(f'sv{h}', sq.Vs[1][h][:TB, blk0:blk0 + nb, :], vav[:TB, :, h, :])
        self.Bp.put(ckT)

    def phaseA(self, l, sq):
        S = self.S
        S.barrier()
        self.carve('A')
        I, O = self.I, self.O
        pre = sq.name
        NT, TB, nb = sq.NT, sq.TB, sq.nb
        self.wplan_reset(self.plan_A(l) * sq.ntiles)
        for va in self.vaug:
            self.MEMSET('pool', va, 1.0)
        self.MEMSET('pool', self.kpad, 0.0)
        if sq.P == 0:
            self.MEMSET('pool', self.hcar, 0.0)
            self.MEMSET('pool', self.ccar, 0.0)
            self.MEMSET('pool', self.xpad[:, :, 0:3], 0.0)
        else:
            self.LD('par', self.hcar, Tl(I['state_lru_h'].ap[l].rearrange("(c p) -> p c", p=128), 'x'), allow_slow_non_contiguous=True)
            for c in range(2):
                self.LD('par', self.xpad[:, c, 0:3], Tl(I['state_lru_conv'].ap[l][:, c * 128:(c + 1) * 128].rearrange("j p -> p j"), 'x'),
                        allow_slow_non_contiguous=True)
            self.sample_prep(l, sq)
        x_src = sq.x_in if l == 0 else sq.x_mid

        def prologue(qt_):
            t0_ = qt_ * NT
            self.LD('x', self.xres[:TB, :nb, :], Tl(x_src.ap[t0_:t0_ + NT, :].rearrange("(b p) d -> p b d", p=TB), x_src.k))
            hT_ = self.rmsnorm_T(sq, self.xres, D, self.gpre[:, 0, :])
            for kc in range(8):
                self.ST(f'sh{kc % 2}', sq.HT[qt_, :, kc, :], hT_[kc][:, :NT])
            return hT_
        hT_next = None
        for qt in range(sq.ntiles):
            t0 = qt * NT
            g0 = sq.P + t0
            blk0 = g0 // 128
            hT = hT_next if hT_next is not None else prologue(qt)
            self.ck(1)
            self.rope_tables(sq, g0)
            self.ck(2)
            wv = self.wnext('in1')
            for ty, qcol, kcol, outn in ((0, 0, 256, 'fox_k'),):
                self.qk_proj(sq, l, wv, ty, hT, t0, g0, outn)
            self.ck(3)
            wv = self.wnext('in2')
            st = [self.A.get(), self.A.get()]
            va = self.vaug[0]
            vav = va[:, 0:nb * 272].re("p (b h d) -> p b h d", h=4, d=68)
            VV = os.environ.get('MK_VAR', '')
            for b in range(nb):
                if 'a' in VV:
                    break
                ps = self.tm_proj(wv, 0, 256, hT, b, TB)
                if 'b' not in VV:
                    self.CP('act', st[b // 2][:TB, (b % 2) * 256:(b % 2) * 256 + 256], ps[0:TB, 0:256])
                if 'c' not in VV:
                    self.CP('dve', vav[:TB, b, :, 0:64], ps[0:TB, 0:256].re("p (h d) -> p h d", d=64))
                self.PS.put(ps)
            self.ck(30)
            self.store_tm(O[f'{pre}_fox_v'], l, t0, st, 256, TB, nb)
            self.ck(301)
            for h in range(4):
                self.ST(f'sv{h}', sq.Vs[0][h][:TB, blk0:blk0 + nb, :], vav[:TB, :, h, :])
            self.A.put(*st)
            self.ck(31)
            ps = self.fm_proj(wv, 256, 4, hT, NT)
            lf = self.A.get()
            self.ACT(lf[0:4, :NT], ps[0:4, :NT], AF.Exp, bias=self.nbf[0:4, 0:1], scale=-1.0)
            self.PS.put(ps)
            self.ACT(lf[0:4, :NT], lf[0:4, :NT], AF.Ln, bias=1.0)
            self.TSC('dve', lf[0:4, :NT], lf[0:4, :NT], -1.0, None, ALU.mult)
            cT = self.A.get()
            self.SCAN(cT[0:4, :NT], self.ones_f[0:4, :NT], lf[0:4, :NT], self.ccar[0:4, 0:1], ALU.mult, ALU.add)
            self.CP('dve', self.ccar[0:4, 0:1], cT[0:4, NT - 1:NT])
            ref = 256 if NT == 512 else 0
            dqb = self.Bp.get()
            self.TSC('dve', dqb[0:4, :NT], cT[0:4, :NT], cT[0:4, ref:ref + 1], None, ALU.subtract)
            for h in range(4):
                self.ST(f'sq{h % 2}', sq.Qs[0][h][64:65, t0:t0 + NT], dqb[h:h + 1, :NT])
                self.ST(f'sk{h}', sq.KTs[0][h][64:65, g0:g0 + NT], self.ones_row[0:1, :NT])
            self.Bp.put(dqb)
            self.ck(32)
            ps = self.PS.get()
            for b in range(nb):
                self.TR(ps[0:TB, b * 4:b * 4 + 4], lf[0:4, b * TB:(b + 1) * TB], self.ident_f[0:4, 0:4])
                self.TR(ps[0:TB, 64 + b * 4:64 + b * 4 + 4], cT[0:4, b * TB:(b + 1) * TB], self.ident_f[0:4, 0:4])
            lft = self.A.get()
            self.CP('dve', lft[:TB, 0:nb * 4], ps[0:TB, 0:nb * 4])
            self.CP('dve', self.ctok[:TB, :, blk0:blk0 + nb], ps[0:TB, 64:64 + nb * 4].re("p (b h) -> p h b", h=4))
            self.PS.put(ps)
            self.ck(33)
            self.ST('slf', Tl(O[f'{pre}_fox_logf'].ap[l, t0:t0 + NT, :].rearrange("(b p) h -> p b h", p=TB), O[f'{pre}_fox_logf'].k),
                    lft[:TB, 0:nb * 4].re("p (b h) -> p b h", h=4))
            self.A.put(lf, cT, lft)
            self.ck(4)
            wv = self.wnext('in3')
            gs = self.lru_proj(sq, wv, hT)
            self.ck(5)
            wv = self.wnext('in4')
            for b in range(nb):
                ps = self.tm_proj(wv, 0, 416, hT, b, TB)
                self.CP(self.ev(), self.tm4[:TB, b, :], ps[0:TB, 0:416])
                self.PS.put(ps)
            self.mla_new(sq, l, t0, g0)
            self.ck(6)
            wv = self.wnext('in5')
            self.qk_proj(sq, l, wv, 2, hT, t0, g0, 'sb_k')
            wv = self.wnext('in6')
            st = [self.A.get(), self.A.get()]
            va = self.vaug[1]
            vav = va[:, 0:nb * 272].re("p (b h d) -> p b h d", h=4, d=68)
            for b in range(nb):
                ps = self.tm_proj(wv, 0, 256, hT, b, TB)
                self.CP('act', st[b // 2][:TB, (b % 2) * 256:(b % 2) * 256 + 256], ps[0:TB, 0:256])
                self.CP('dve', vav[:TB, b, :, 0:64], ps[0:TB, 0:256].re("p (h d) -> p h d", d=64))
                self.PS.put(ps)
            self.store_tm(O[f'{pre}_sb_v'], l, t0, st, 256, TB, nb)
            for h in range(4):
                self.ST(f'sv{h}', sq.Vs[2][h][:TB, blk0:blk0 + nb, :], vav[:TB, :, h, :])
            self.A.put(*st)
            self.Bp.put(*hT)
            hT_next = prologue(qt + 1) if qt + 1 < sq.ntiles else None
            self.lru(sq, l, gs, qt, t0)

    def store_tm(self, dst, l, t0, st, W, TB, nb):
        for j in range((nb + 1) // 2):
            nbb = min(2, nb - 2 * j)
            self.ST('stm', Tl(dst.ap[l, t0 + 2 * j * TB:t0 + (2 * j + nbb) * TB, :].rearrange("(b p) c -> p b c", p=TB), dst.k),
                    st[j][:TB, 0:nbb * W].re("p (b c) -> p b c", c=W))

    def qk_proj(self, sq, l, wv, ty, hT, t0, g0, outn):
        NT, TB, nb = sq.NT, sq.TB, sq.nb
        O = self.O
        for h in range(4):
            ps = self.fm_proj(wv, h * 64, 64, hT, NT)
            qb = self.Bp.get()
            self.TSC('dve', qb[0:64, :NT], ps[0:64, :NT], 0.125, None, ALU.mult)
            self.PS.put(ps)
            self.ST(f'sq{h % 2}', sq.Qs[ty][h][0:64, t0:t0 + NT], qb[0:64, :NT])
            self.Bp.put(qb)
            ps = self.fm_proj(wv, 256 + h * 64, 64, hT, NT)
            kb = self.Bp.get()
            self.CP('act', kb[0:64, :NT], ps[0:64, :NT])
            self.PS.put(ps)
            self.ST(f'sk{h}', sq.KTs[ty][h][0:64, g0:g0 + NT], kb[0:64, :NT])
            self.Bp.put(kb)
        st = [self.A.get(), self.A.get()]
        for b in range(nb):
            ps = self.tm_proj(wv, 256, 256, hT, b, TB)
            self.CP(self.ev(), st[b // 2][:TB, (b % 2) * 256:(b % 2) * 256 + 256], ps[0:TB, 0:256])
            self.PS.put(ps)
        self.store_tm(O[f'{sq.name}_{outn}'], l, t0, st, 256, TB, nb)
        self.A.put(*st)

    def lru_proj(self, sq, wv, hT):
        NT = sq.NT
        gs = []
        for c in range(2):
            ps = self.fm_proj(wv, c * 128, 128, hT, NT)
            self.CP('act', self.xpad[:, c, 3:3 + NT], ps[:, :NT])
            self.PS.put(ps)
            ps = self.fm_proj(wv, 256 + c * 128, 128, hT, NT)
            g = self.A.get()
            self.ACT(g[:, :NT], ps[:, :NT], AF.Gelu_apprx_tanh)
            self.PS.put(ps)
            gs.append(g)
        return gs

    def lru(self, sq, l, gs, qt, t0):
        NT = sq.NT
        O = self.O
        pre = sq.name
        last = (qt == sq.ntiles - 1)

        def chain(c):
            g = gs[c]
            xc = self.A.get()
            self.TSC('dve', xc[:, :NT], self.xpad[:, c, 0:NT], self.cw[:, c, 0:1], self.cbias[:, c:c + 1], ALU.mult, ALU.add)
            yield
            for tap in range(1, 4):
                self.STT(xc[:, :NT], self.xpad[:, c, tap:tap + NT], self.cw[:, c, tap:tap + 1], xc[:, :NT], ALU.mult, ALU.add)
                yield
            xcb = self.Bp.get()
            self.CP('act', xcb[:, :NT], xc[:, :NT])
            yield
            ps = self.PS.get()
            self.MM(ps[:, :NT], self.wr_bd[:, c, :], xcb[:, :NT])
            r = self.A.get()
            self.ACT(r[:, :NT], ps[:, :NT], AF.Sigmoid, bias=self.br[:, c:c + 1])
            self.PS.put(ps)
            yield
            ps = self.PS.get()
            self.MM(ps[:, :NT], self.wi_bd[:, c, :], xcb[:, :NT])
            ig = self.A.get()
            self.ACT(ig[:, :NT], ps[:, :NT], AF.Sigmoid, bias=self.bi[:, c:c + 1])
            self.PS.put(ps)
            self.Bp.put(xcb)
            yield
            a = self.A.get()
            self.ACT(a[:, :NT], r[:, :NT], AF.Exp, scale=self.sl[:, c:c + 1])
            yield
            self.TT('dve', r[:, :NT], a[:, :NT], a[:, :NT], ALU.mult)
            yield
            self.ACT(r[:, :NT], r[:, :NT], AF.Sqrt, bias=1.0, scale=-1.0)
            yield
            self.TT('dve', ig[:, :NT], ig[:, :NT], xc[:, :NT], ALU.mult)
            yield
            self.TT('dve', r[:, :NT], r[:, :NT], ig[:, :NT], ALU.mult)
            yield
            if sq.P == 0 and qt == 0:
                self.MEMSET('pool', a[:, 0:1], 0.0)
                self.CP('dve', r[:, 0:1], ig[:, 0:1])
            hs = xc
            self.SCAN(hs[:, :NT], a[:, :NT], r[:, :NT], self.hcar[:, c:c + 1], ALU.mult, ALU.add)
            yield
            self.CP('dve', self.hcar[:, c:c + 1], hs[:, NT - 1:NT])
            ob = self.Bp.get()
            self.TT('dve', ob[:, :NT], hs[:, :NT], g[:, :NT], ALU.mult)
            self.ST(f'sob{c}', sq.OB[:, c, t0:t0 + NT], ob[:, :NT])
            self.Bp.put(ob)
            self.A.put(g, xc, r, ig, a)

        gens = [chain(0), chain(1)]
        while gens:
            for gen in list(gens):
                try:
                    next(gen)
                except StopIteration:
                    gens.remove(gen)
        if last:
            self.ST('slh', Tl(O[pre + '_lru_h'].ap[l].rearrange("(c p) -> p c", p=128), O[pre + '_lru_h'].k), self.hcar,
                    allow_slow_non_contiguous=True)
            for c in range(2):
                self.ST('slc', Tl(O[pre + '_lru_conv'].ap[l][:, c * 128:(c + 1) * 128].rearrange("j p -> p j"), O[pre + '_lru_conv'].k),
                        self.xpad[:, c, NT:NT + 3], allow_slow_non_contiguous=True)
        else:
            t = self.A.get()
            self.CP('dve', t[:, 0:6].re("p (c j) -> p c j", j=3), self.xpad[:, :, NT:NT + 3])
            self.CP('dve', self.xpad[:, :, 0:3], t[:, 0:6].re("p (c j) -> p c j", j=3))
            self.A.put(t)

    def mla_new(self, sq, l, t0, g0):
        NT, TB, nb = sq.NT, sq.TB, sq.nb
        O = self.O
        pre = sq.name
        tm4 = self.tm4
        cqT = self.rmsnorm_T(sq, tm4[:, :, 0:256], 256, self.qng)
        for h in range(4):
            psq = self.fm_proj(self.wuq, h * 96, 96, cqT, NT)
            psr = self.PS.get()
            for kc in range(2):
                self.MM(psr[0:96, :NT], self.wuq_rot[:, kc, h, :], cqT[kc][:, :NT], start=(kc == 0), stop=(kc == 1))
            qb = self.Bp.get()
            self.CP('act', qb[0:64, :NT], psq[0:64, :NT])
            t1 = self.A.get()
            t2 = self.A.get()
            self.TT('dve', t1[64:96, :NT], psq[64:96, :NT], self.ropeT[64:96, 0, :NT], ALU.mult)
            self.TT('dve', t2[64:96, :NT], psr[64:96, :NT], self.ropeT[64:96, 1, :NT], ALU.mult)
            self.TT('dve', qb[64:96, :NT], t1[64:96, :NT], t2[64:96, :NT], ALU.add)
            self.PS.put(psq, psr)
            self.A.put(t1, t2)
            self.ST(f'sq{h % 2}', sq.Qs[1][h][:, t0:t0 + NT], qb[0:96, :NT])
            self.Bp.put(qb)
        self.Bp.put(*cqT)
        ss, rstd = self.ss, self.rstd
        for b in range(nb):
            self.ACT(self.junk[:TB, :128], tm4[:TB, b, 256:384], AF.Square, accum=ss[:TB, b:b + 1])
        self.TSC('dve', rstd[:TB, :nb], ss[:TB, :nb], 1.0 / 128, EPS, ALU.mult, ALU.add)
        self.ACT(rstd[:TB, :nb], rstd[:TB, :nb], AF.Sqrt)
        self.RECIP(rstd[:TB, :nb], rstd[:TB, :nb])
        ck = self.A.get()
        ckv = ck[:, 0:nb * 128].re("p (b d) -> p b d", d=128)
        for b in range(nb):
            self.STT(ckv[:TB, b, :], tm4[:TB, b, 256:384], rstd[:TB, b:b + 1], self.kvg[:TB, :], ALU.mult, ALU.mult)
        self.ST('sck', Tl(O[pre + '_mla_ckv'].ap[l, t0:t0 + NT, :].rearrange("(b p) c -> p b c", p=TB), O[pre + '_mla_ckv'].k), ckv[:TB])
        kp = self.A.get()
        kpv = kp[:, 0:nb * 32].re("p (b d) -> p b d", d=32)
        t1 = self.A.get()
        t1v = t1[:, 0:nb * 32].re("p (b d) -> p b d", d=32)
        cosk = self.ropeK[:TB, 0, :nb, :]
        sink = self.ropeK[:TB, 1, :nb, :]
        x1 = tm4[:TB, :nb, 384:400]
        x2 = tm4[:TB, :nb, 400:416]
        self.TT('dve', kpv[:TB, :, 0:16], x1, cosk, ALU.mult)
        self.TT('dve', t1v[:TB, :, 0:16], x2, sink, ALU.mult)
        self.TT('dve', kpv[:TB, :, 0:16], kpv[:TB, :, 0:16], t1v[:TB, :, 0:16], ALU.subtract)
        self.TT('dve', kpv[:TB, :, 16:32], x1, sink, ALU.mult)
        self.TT('dve', t1v[:TB, :, 16:32], x2, cosk, ALU.mult)
        self.TT('dve', kpv[:TB, :, 16:32], kpv[:TB, :, 16:32], t1v[:TB, :, 16:32], ALU.add)
        self.ST('skp', Tl(O[pre + '_mla_kpe'].ap[l, t0:t0 + NT, :].rearrange("(b p) c -> p b c", p=TB), O[pre + '_mla_kpe'].k), kpv[:TB])
        self.mla_kv(sq, l, ckv, kpv, g0, NT, TB, nb)
        self.A.put(ck, kp, t1)

    def sample_prep(self, l, sq):
        I = self.I
        for ty, kn, vn in ((0, 'cache_fox_k', 'cache_fox_v'), (2, 'cache_sb_k', 'cache_sb_v')):
            for j in range(PAST // 512):
                kk = [self.A.get(), self.A.get()]
                vv = [self.A.get(), self.A.get()]
                va = self.vaug[j % 2]
                vav = va[:, 0:4 * 272].re("p (b h d) -> p b h d", h=4, d=68)
                for jj in range(2):
                    r0 = j * 512 + jj * 256
                    self.LD('ck', kk[jj].re("p (b c) -> p b c", c=256), Tl(I[kn].ap[l, r0:r0 + 256, :].rearrange("(b p) c -> p b c", p=128), 'x'))
                    self.LD('cv', vv[jj].re("p (b c) -> p b c", c=256), Tl(I[vn].ap[l, r0:r0 + 256, :].rearrange("(b p) c -> p b c", p=128), 'x'))
                for b in range(4):
                    self.CP(self.ev(), vav[:, b, :, 0:64], vv[b // 2][:, (b % 2) * 256:(b % 2) * 256 + 256].re("p (h d) -> p h d", d=64))
                for h in range(4):
                    self.ST(f'sv{h}', sq.Vs[ty][h][:, 4 * j:4 * j + 4, :], vav[:, :, h, :])
                    ps = self.PS.get()
                    for b in range(4):
                        self.TR(ps[0:64, b * 128:(b + 1) * 128], kk[b // 2][:, (b % 2) * 256 + h * 64:(b % 2) * 256 + h * 64 + 64], self.ident_f)
                    kb = self.Bp.get()
                    self.CP(self.ev(), kb[0:64, :], ps[0:64, :])
                    self.PS.put(ps)
                    self.ST(f'sk{h}', sq.KTs[ty][h][0:64, j * 512:(j + 1) * 512], kb[0:64, :])
                    if ty == 0:
                        self.ST(f'sk{h}', sq.KTs[0][h][64:65, j * 512:(j + 1) * 512], self.ones_row[0:1, :])
                    self.Bp.put(kb)
                self.A.put(*kk, *vv)
        for j in range(PAST // 512):
            ck = self.A.get()
            kp = self.A.get()
            ckv = ck.re("p (b d) -> p b d", d=128)
            kpv = kp[:, 0:128].re("p (b d) -> p b d", d=32)
            self.LD('ck', ckv, Tl(I['cache_mla_ckv'].ap[l, j * 512:(j + 1) * 512, :].rearrange("(b p) c -> p b c", p=128), 'x'))
            self.LD('cv', kpv, Tl(I['cache_mla_kpe'].ap[l, j * 512:(j + 1) * 512, :].rearrange("(b p) c -> p b c", p=128), 'x'))
            self.mla_kv(sq, l, ckv, kpv, j * 512, 512, 128, 4)
            self.A.put(ck, kp)
        lt = self.A.get()
        self.LD('ck', lt[:, 0:64].re("p (b h) -> p b h", h=4), Tl(I['cache_fox_logf'].ap[l].rearrange("(b p) h -> p b h", p=128), 'x'))
        self.MEMSET('pool', self.ccar, 0.0)
        for j in range(PAST // 512):
            ps = self.PS.get()
            for b in range(4):
                self.TR(ps[0:4, b * 128:(b + 1) * 128], lt[:, (4 * j + b) * 4:(4 * j + b) * 4 + 4], self.ident_f)
            lf = self.A.get()
            self.CP('act', lf[0:4, :], ps[0:4, :])
            self.PS.put(ps)
            cT = self.A.get()
            self.SCAN(cT[0:4, :], self.ones_f[0:4, :], lf[0:4, :], self.ccar[0:4, 0:1], ALU.mult, ALU.add)
            self.CP('dve', self.ccar[0:4, 0:1], cT[0:4, 511:512])
            ps = self.PS.get()
            for b in range(4):
                self.TR(ps[:, b * 4:b * 4 + 4], cT[0:4, b * 128:(b + 1) * 128], self.ident_f[0:4, 0:4])
            self.CP('dve', self.ctok[:, :, 4 * j:4 * j + 4], ps[:, 0:16].re("p (b h) -> p h b", h=4))
            self.PS.put(ps)
            self.A.put(lf, cT)
        self.A.put(lt)

    def phaseB(self, l, sq):
        S = self.S
        S.barrier()
        self.carve('B')
        self.make_masks()
        NT, TB = sq.NT, sq.TB
        nblk = sq.nblk
        Ttot = sq.Ttot
        for qt in range(sq.ntiles):
            g = sq.P + qt * NT + (256 if NT == 512 else 0)
            ps = self.PS.get()
            self.MM(ps[:, 0:4], self.E0, self.ctok[:, :, g // 128])
            self.CP('dve', self.rb[:, qt, :], ps[:, 0:4])
            self.PS.put(ps)
        cnt = 0
        for ty in range(3):
            dk = (65, 96, 64)[ty]
            for h in range(4):
                kt = self.ktbuf[cnt % 2]
                vb = self.vbuf[cnt % 2]
                cnt += 1
                self.LD(f'kt{cnt % 2}', kt[0:dk, 0:Ttot], sq.KTs[ty][h])
                vv = vb[:, 0:nblk * 68].re("p (b d) -> p b d", d=68)
                nfull = Ttot // 128
                self.LD(f'vb{cnt % 2}', vv[:, 0:nfull, :], sq.Vs[ty][h][:, 0:nfull, :])
                if Ttot % 128:
                    self.LD(f'vb{cnt % 2}', vv[0:Ttot % 128, nfull, :], sq.Vs[ty][h][0:Ttot % 128, nfull, :])
                for qt in range(sq.ntiles):
                    t0 = qt * NT
                    qb = self.Bp.get()
                    self.LD(f'q{qt % 2}', qb[0:dk, :NT], sq.Qs[ty][h][:, t0:t0 + NT])
                    if sq.P == 0:
                        nkb = 4 * qt + 4
                        diag0 = 4 * qt
                    else:
                        nkb = nblk
                        diag0 = nblk - 1
                    if ty == 0:
                        fb = self.fb[qt % 2]
                        self.TSC('dve', fb[:, 0:nkb], self.ctok[:, h, 0:nkb], -1.0, self.rb[:, qt, h:h + 1], ALU.mult, ALU.add)
                    acc = self.PS.get()
                    if ty == 2:
                        car = self.A.get()
                        self.MEMSET('pool', car[:, :NT], 0.0)
                    order = list(range(nkb - 1, -1, -1))
                    units = []
                    for i, kb in enumerate(order):
                        m = kb - diag0
                        mask = None
                        if m >= 0 and not (ty == 1 and sq.P > 0):
                            mask = self.masks[ty][m]
                        q0 = 128 * m if (m > 0 and sq.P == 0) else 0
                        units.append(dict(kb=kb, nk=min(128, Ttot - kb * 128), first=(i == 0), last=(i == len(order) - 1), mask=mask, q0=q0))
                    nu = len(units)

                    def s_qk(u):
                        nk, kb, mask = u['nk'], u['kb'], u['mask']
                        z = u['z'] = self.PS.get()
                        self.MM(z[0:nk, u['q0']:NT], kt[0:dk, kb * 128:kb * 128 + nk], qb[0:dk, u['q0']:NT], start=True, stop=(mask is None))
                        if mask is not None:
                            self.MM(z[0:nk, u['q0']:NT], self.ident_bf[0:nk, 0:nk], mask[0:nk, u['q0']:NT], start=False, stop=True)

                    def s_pv(u):
                        nk, kb, z = u['nk'], u['kb'], u['z']
                        p = self.Bp.get()
                        if ty == 0:
                            self.ACT(p[0:nk, u['q0']:NT], z[0:nk, u['q0']:NT], AF.Exp, bias=fb[0:nk, kb:kb + 1])
                        else:
                            self.ACT(p[0:nk, u['q0']:NT], z[0:nk, u['q0']:NT], AF.Exp, scale=96 ** -0.5)
                        self.PS.put(z)
                        self.MM(acc[0:65, u['q0']:NT], vv[0:nk, kb, 0:65], p[0:nk, u['q0']:NT], start=u['first'], stop=u['last'], sgc=True)
                        self.Bp.put(p)

                    def sb_a(u):
                        s_qk(u)
                        nk, z = u['nk'], u['z']
                        e = u['e'] = self.A.get()
                        self.ACT(e[0:nk, u['q0']:NT], z[0:nk, u['q0']:NT], AF.Exp)
                        sp = u['sp'] = self.Bp.get()
                        self.ACT(sp[0:nk, u['q0']:NT], e[0:nk, u['q0']:NT], AF.Ln, bias=1.0)

                    def sb_b(u):
                        nk, z, e, sp = u['nk'], u['z'], u['e'], u['sp']
                        cb = self.PS.get()
                        self.MM(cb[0:nk, u['q0']:NT], self.U[0:nk, 0:nk], sp[0:nk, u['q0']:NT])
                        if not u['last']:
                            ob = self.PS.get()
                            self.MM(ob[:, u['q0']:NT], self.ones_bf[0:nk, :], sp[0:nk, u['q0']:NT])
                        t1 = e
                        self.TT('dve', t1[0:nk, u['q0']:NT], z[0:nk, u['q0']:NT], car[0:nk, u['q0']:NT], ALU.subtract)
                        self.PS.put(z)
                        self.TT('dve', t1[0:nk, u['q0']:NT], t1[0:nk, u['q0']:NT], cb[0:nk, u['q0']:NT], ALU.subtract)
                        self.PS.put(cb)
                        if not u['last']:
                            self.TT('dve', car[:, u['q0']:NT], car[:, u['q0']:NT], ob[:, u['q0']:NT], ALU.add)
                            self.PS.put(ob)
                        a = u['a'] = self.Bp.get()
                        self.ACT(a[0:nk, u['q0']:NT], t1[0:nk, u['q0']:NT], AF.Exp)
                        self.A.put(e)

                    def sb_c(u):
                        nk, kb = u['nk'], u['kb']
                        self.MM(acc[0:64, u['q0']:NT], vv[0:nk, kb, 0:64], u['a'][0:nk, u['q0']:NT], start=u['first'], stop=u['last'], sgc=True)
                        self.Bp.put(u['sp'], u['a'])

                    if ty != 2:
                        for st_ in range(nu + 3):
                            if st_ < nu:
                                s_qk(units[st_])
                            if 0 <= st_ - 3 < nu:
                                s_pv(units[st_ - 3])
                    else:
                        for st_ in range(nu + 3):
                            if st_ < nu:
                                sb_a(units[st_])
                            if 0 <= st_ - 2 < nu:
                                sb_b(units[st_ - 2])
                            if 0 <= st_ - 3 < nu:
                                sb_c(units[st_ - 3])
                    self.Bp.put(qb)
                    ob_ = self.Bp.get()
                    if ty == 2:
                        self.CP('act', ob_[0:64, :NT], acc[0:64, :NT])
                        self.A.put(car)
                        self.PS.put(acc)
                    else:
                        accs = self.A.get()
                        self.CP('act', accs[0:65, :NT], acc[0:65, :NT])
                        self.PS.put(acc)
                        self.RECIP(accs[64:65, :NT], accs[64:65, :NT])
                        bc = self.PS.get()
                        self.MM(bc[0:64, :NT], self.ones_f[64:65, 0:64], accs[64:65, :NT])
                        self.TT('dve', ob_[0:64, :NT], accs[0:64, :NT], bc[0:64, :NT], ALU.mult)
                        self.PS.put(bc)
                        self.A.put(accs)
                    self.ST(f'so{qt % 2}', sq.Os[ty][h][:, t0:t0 + NT], ob_[0:64, :NT])
                    self.Bp.put(ob_)

    def tm_linear_norm_res(self, sq, inT, pieces, gain):
        NT, TB, nb = sq.NT, sq.TB, sq.nb
        ss, rstd = self.ss, self.rstd
        h0 = []
        banks = None
        for hf in range(2):
            banks = [self.PS.get() for _ in range(nb)]
            k0 = 0
            nkg = len(pieces[hf])
            for kg, (nm, nk) in enumerate(pieces[hf]):
                wv = self.wnext(nm)
                for b in range(nb):
                    for kk in range(nk):
                        x = inT[k0 + kk]
                        K = x.ap.shape[0]
                        self.MM(banks[b][0:TB, :], x[:, b * TB:(b + 1) * TB], wv[0:K, kk, :],
                                start=(kg == 0 and kk == 0), stop=(kg == nkg - 1 and kk == nk - 1))
                k0 += nk
            for b in range(nb):
                self.ACT(self.junk[:TB, 0:512], banks[b][0:TB, :], AF.Square, accum=ss[:TB, hf * 4 + b:hf * 4 + b + 1])
            for b in range(nb):
                t = self.A.get()
                self.TT('dve', t[:TB, :], banks[b][0:TB, :], gain[:TB, hf * 512:(hf + 1) * 512], ALU.mult)
                h0.append(t)
            self.PS.put(*banks)
        self.TT('dve', rstd[:TB, 0:nb], ss[:TB, 0:nb], ss[:TB, 4:4 + nb], ALU.add)
        self.TSC('dve', rstd[:TB, :nb], rstd[:TB, :nb], 1.0 / D, EPS, ALU.mult, ALU.add)
        self.ACT(rstd[:TB, :nb], rstd[:TB, :nb], AF.Sqrt)
        self.RECIP(rstd[:TB, :nb], rstd[:TB, :nb])
        for hf in range(2):
            for b in range(nb):
                xs = self.xres[:TB, b, hf * 512:(hf + 1) * 512]
                self.STT(xs, h0[hf * nb + b][:TB, :], rstd[:TB, b:b + 1], xs, ALU.mult, ALU.add)
        self.A.put(*h0)

    def phaseC(self, l, sq):
        S = self.S
        S.barrier()
        self.carve('C')
        NT, TB, nb = sq.NT, sq.TB, sq.nb
        self.wplan_reset(self.plan_C(l) * sq.ntiles)
        x_src = sq.x_in if l == 0 else sq.x_mid
        x_dst = sq.x_mid if l == 0 else sq.y
        for qt in range(sq.ntiles):
            t0 = qt * NT
            self.LD('x', self.xres[:TB, :nb, :], Tl(x_src.ap[t0:t0 + NT, :].rearrange("(b p) d -> p b d", p=TB), x_src.k))
            hT = [self.Bp.get() for _ in range(8)]
            for kc in range(8):
                self.LD(f'lh{kc % 2}', hT[kc][:, :NT], sq.HT[qt, :, kc, :])
            oT = []
            for ty in range(3):
                lst = []
                for j in range(2):
                    o = self.Bp.get()
                    self.LD(f'lo{j}', o[0:64, :NT], sq.Os[ty][2 * j][:, t0:t0 + NT])
                    self.LD(f'lo{j}', o[64:128, :NT], sq.Os[ty][2 * j + 1][:, t0:t0 + NT])
                    lst.append(o[:, :NT])
                oT.append(lst)
            lst = []
            for c in range(2):
                o = self.Bp.get()
                self.LD(f'lo{c}', o[:, :NT], sq.OB[:, c, t0:t0 + NT])
                lst.append(o[:, :NT])
            branch_in = [oT[0], lst, oT[1], oT[2]]
            merged = [self.A.get() for _ in range(8)]
            for n in range(4):
                wb = self.wnext(f'wb{n}')
                xin = branch_in[n]
                for hf in range(2):
                    gv = self.wnext(f'gate{n}{hf}')
                    for cc in range(4):
                        c = hf * 4 + cc
                        pg = self.fm_proj(gv, cc * 128, 128, hT, NT)
                        pp = self.PS.get()
                        for j, x in enumerate(xin):
                            K = x.ap.shape[0]
                            self.MM(pp[:, :NT], wb[0:K, j, c * 128:(c + 1) * 128], x, start=(j == 0), stop=(j == len(xin) - 1))
                        gt = self.A.get()
                        self.ACT(gt[:, :NT], pg[:, :NT], AF.Sigmoid)
                        self.PS.put(pg)
                        if n == 0:
                            self.TT('dve', merged[c][:, :NT], gt[:, :NT], pp[:, :NT], ALU.mult)
                        else:
                            self.TT('dve', gt[:, :NT], gt[:, :NT], pp[:, :NT], ALU.mult)
                            self.TT('pool', merged[c][:, :NT], merged[c][:, :NT], gt[:, :NT], ALU.add)
                        self.PS.put(pp)
                        self.A.put(gt)
            self._put_by_key([x.k for lst_ in branch_in for x in lst_])
            mb = []
            for c in range(8):
                b_ = self.Bp.get()
                self.CP('act', b_[:, :NT], merged[c][:, :NT])
                mb.append(b_[:, :NT])
            self.A.put(*merged)
            self._put_by_key([x.k for x in hT])
            self.tm_linear_norm_res(sq, mb, [[('wout0', 8)], [('wout1', 8)]], self.gpost[0])
            self._put_by_key([x.k for x in mb])
            self.ckc(1, x_dst, t0, sq)
            h2 = self.rmsnorm_T(sq, self.xres, D, self.gpre[:, 1, :])
            wq = self.wnext('wq')
            om = []
            for h in range(4):
                ps = self.fm_proj(wq, h * 128, 128, h2, NT)
                qb = self.Bp.get()
                self.CP('act', qb[:, :NT], ps[:, :NT])
                self.PS.put(ps)
                oacc = self.PS.get()
                dacc = self.PS.get()
                for mbk in range(2):
                    s = self.PS.get()
                    self.MM(s[:, :NT], self.MKT[:, h, mbk * 128:(mbk + 1) * 128], qb[:, :NT])
                    p = self.Bp.get()
                    self.ACT(p[:, :NT], s[:, :NT], AF.Exp, scale=128 ** -0.5)
                    self.PS.put(s)
                    self.MM(oacc[:, :NT], self.MV[:, mbk, h * 128:(h + 1) * 128], p[:, :NT], start=(mbk == 0), stop=(mbk == 1))
                    self.MM(dacc[:, :NT], self.ones_bf, p[:, :NT], start=(mbk == 0), stop=(mbk == 1))
                    self.Bp.put(p)
                rd = self.A.get()
                self.RECIP(rd[:, :NT], dacc[:, :NT])
                o = self.Bp.get()
                self.TT('dve', o[:, :NT], oacc[:, :NT], rd[:, :NT], ALU.mult)
                self.PS.put(oacc, dacc)
                self.A.put(rd)
                self.Bp.put(qb)
                om.append(o[:, :NT])
            self.Bp.put(*h2)
            self.tm_linear_norm_res(sq, om, [[('wo0', 4)], [('wo1', 4)]], self.gpost[1])
            self._put_by_key([x.k for x in om])
            self.ckc(2, x_dst, t0, sq)
            h3 = self.rmsnorm_T(sq, self.xres, D, self.gpre[:, 2, :])
            act = []
            for j in range(6):
                ncol = 512 if j < 5 else 256
                wg = self.wnext(f'wg{j}')
                wu = self.wnext(f'wu{j}')
                for cc in range(ncol // 128):
                    pg = self.fm_proj(wg, cc * 128, 128, h3, NT)
                    pu = self.fm_proj(wu, cc * 128, 128, h3, NT)
                    sg = self.A.get()
                    self.ACT(sg[:, :NT], pg[:, :NT], AF.Silu)
                    self.PS.put(pg)
                    a_ = self.Bp.get()
                    self.TT('dve', a_[:, :NT], sg[:, :NT], pu[:, :NT], ALU.mult)
                    self.PS.put(pu)
                    self.A.put(sg)
                    act.append(a_[:, :NT])
            self.Bp.put(*h3)
            self.tm_linear_norm_res(sq, act, [[(f'wd{hf}{kg}', nk) for kg, nk in enumerate((8, 8, 6))] for hf in range(2)], self.gpost[2])
            self._put_by_key([x.k for x in act])
            self.ST('sx', Tl(x_dst.ap[t0:t0 + NT, :].rearrange("(b p) d -> p b d", p=TB), x_dst.k), self.xres[:TB, :nb, :])

    def ckc(self, n, x_dst, t0, sq):
        if int(os.environ.get('MK_SUBC', '999')) == n:
            self.ST('sx', Tl(x_dst.ap[t0:t0 + sq.NT, :].rearrange("(b p) d -> p b d", p=sq.TB), x_dst.k), self.xres[:sq.TB, :sq.nb, :])
            raise StopBuild()

    def _put_by_key(self, keys):
        for k in keys:
            self.Bp.put(self._bt[k])

    def mem_kv(self, l, sq):
        I, O = self.I, self.O
        if sq.P == 0:
            self.wplan_reset([('wk', self.wsrc('mem_wk', l, 0, 8, 0, 512), 128, 8, 512),
                              ('wv', self.wsrc('mem_wv', l, 0, 8, 0, 512), 128, 8, 512)])
            self.LD('x', self.xres[:, 0:2, :], Tl(I['mem_prompt'].ap.rearrange("(b p) d -> p b d", p=128), 'x'))
            mT = self.rmsnorm_T(sq, self.xres, D, self.gpre[:, 3, :], NT=256, TB=128, nb=2)
            wk = self.wnext('wk')
            for h in range(4):
                ps = self.fm_proj(wk, h * 128, 128, mT, 256)
                self.CP(self.ev(), self.MKT[:, h, :], ps[:, 0:256])
                self.PS.put(ps)
            for nm, wv_, on in (('k', wk, 'p_mem_k'), ('v', None, 'p_mem_v')):
                if wv_ is None:
                    wv_ = self.wnext('wv')
                for b in range(2):
                    ps = self.tm_proj(wv_, 0, 512, mT, b, 128)
                    t = self.A.get()
                    self.CP('act', t, ps)
                    if nm == 'v':
                        self.CP('dve', self.MV[:, b, :], ps)
                    self.PS.put(ps)
                    self.ST('smk', Tl(O[on].ap[l, b * 128:(b + 1) * 128, :], O[on].k), t)
                    self.A.put(t)
            self.Bp.put(*mT)
        else:
            for b in range(2):
                tk = self.A.get()
                tv = self.A.get()
                self.LD('ck', tk, Tl(I['cache_mem_k'].ap[l, b * 128:(b + 1) * 128, :], 'x'))
                self.LD('cv', tv, Tl(I['cache_mem_v'].ap[l, b * 128:(b + 1) * 128, :], 'x'))
                self.CP('dve', self.MV[:, b, :], tv)
                ps = self.PS.get()
                for h in range(4):
                    self.TR(ps[:, h * 128:(h + 1) * 128], tk[:, h * 128:(h + 1) * 128], self.ident_f)
                self.CP('act', self.MKT[:, :, b * 128:(b + 1) * 128], ps.re("p (h m) -> p h m", m=128))
                self.PS.put(ps)
                self.A.put(tk, tv)

    def build(self):
        S = self.S
        self.carve('A')
        stage = int(os.environ.get('MK_STAGE', '999'))
        st = [0]

        def go():
            st[0] += 1
            return st[0] <= stage
        self.setup_consts()
        try:
            self.build_body(go)
        except StopBuild:
            pass
        S.finish()
        return self.nc

    def build_body(self, go):
        S = self.S
        if go():
            self.cast_weights()
        for l in range(L):
            S.barrier()
            self.carve('A')
            if go():
                self.load_layer_params(l)
            for sq in self.seqs:
                if go():
                    self.phaseA(l, sq)
                if go():
                    self.phaseB(l, sq)
                S.barrier()
                self.carve('C')
                if go():
                    self.mem_kv(l, sq)
                if go():
                    self.phaseC(l, sq)


_CACHE = {}


def get_program(T):
    if T not in _CACHE:
        _CACHE[T] = Builder(T).build()
    return _CACHE[T]


def kernel(**inputs):
    inputs = {k: np.asarray(v) for k, v in inputs.items()}
    B, T, _ = inputs['x_prompt'].shape
    nc = get_program(T)
    f32 = lambda a: np.ascontiguousarray(a, dtype=np.float32)
    in_maps = []
    wnames = ['ln_mix_pre', 'ln_mix_post', 'w_in', 'fox_bf', 'lru_conv_w', 'lru_conv_b', 'lru_wr', 'lru_br', 'lru_wi', 'lru_bi',
              'lru_lam', 'mla_q_norm', 'mla_w_uq', 'mla_kv_norm', 'mla_w_uk', 'mla_w_uv', 'w_branch', 'w_out', 'ln_mem_pre',
              'ln_mem_post', 'mem_norm', 'mem_wq', 'mem_wk', 'mem_wv', 'mem_wo', 'ln_ffn_pre', 'ln_ffn_post', 'ffn_wg', 'ffn_wu', 'ffn_wd']
    shared = {}
    for n in wnames:
        a = inputs[n]
        if n in ('lru_wr', 'lru_wi'):
            a = a.reshape(L, 256, 64)
        elif n in ('lru_br', 'lru_bi'):
            a = a.reshape(L, 256)
        elif n == 'w_branch':
            a = a.reshape(L, 1024, 1024)
        shared[n] = f32(a)
    for c in range(B):
        m = dict(shared)
        m['x_prompt'] = f32(inputs['x_prompt'][c])
        m['x_sample'] = f32(inputs['x_sample'][c])
        for n in ('cache_fox_k', 'cache_fox_v', 'cache_sb_k', 'cache_sb_v'):
            m[n] = f32(inputs[n][:, c].reshape(L, PAST, 256))
        m['cache_fox_logf'] = f32(inputs['cache_fox_logf'][:, c])
        m['state_lru_h'] = f32(inputs['state_lru_h'][:, c])
        m['state_lru_conv'] = f32(inputs['state_lru_conv'][:, c])
        m['cache_mla_ckv'] = f32(inputs['cache_mla_ckv'][:, c])
        m['cache_mla_kpe'] = f32(inputs['cache_mla_kpe'][:, c])
        m['cache_mem_k'] = f32(inputs['cache_mem_k'][:, c].reshape(L, MEM, 512))
        m['cache_mem_v'] = f32(inputs['cache_mem_v'][:, c].reshape(L, MEM, 512))
        m['mem_prompt'] = f32(inputs['mem_prompt'][c])
        in_maps.append(m)
    res = run_bass_kernel_spmd(nc, in_maps, core_ids=list(range(B)))
    R = res.results
    global _LAST
    _LAST = R

    def gather(name, shape_tail, batch_axis):
        return np.stack([np.asarray(R[c][name], dtype=np.float32) for c in range(B)], axis=batch_axis).reshape(shape_tail)

    outs = []
    outs.append(gather('p_y', (B, T, D), 0))
    outs.append(gather('s_y', (B, TS_, D), 0))
    for pre, t in (('p', T), ('s', TS_)):
        lst = [gather(f'{pre}_fox_k', (L, B, t, 4, 64), 1), gather(f'{pre}_fox_v', (L, B, t, 4, 64), 1),
               gather(f'{pre}_fox_logf', (L, B, t, 4), 1), gather(f'{pre}_lru_h', (L, B, 256), 1),
               gather(f'{pre}_lru_conv', (L, B, 3, 256), 1), gather(f'{pre}_mla_ckv', (L, B, t, 128), 1),
               gather(f'{pre}_mla_kpe', (L, B, t, 32), 1), gather(f'{pre}_sb_k', (L, B, t, 4, 64), 1),
               gather(f'{pre}_sb_v', (L, B, t, 4, 64), 1)]
        if pre == 'p':
            lst += [gather('p_mem_k', (L, B, MEM, 4, 128), 1), gather('p_mem_v', (L, B, MEM, 4, 128), 1)]
        outs += lst
    return tuple(outs)
```
